# Optimizing a Trainium2 kernel written in Bass

```python
import math
import jax, jax.numpy as jnp
from jax import lax
import numpy as np

D_MODEL = 2048
BATCH = 4
SEQ = 8192
DEPTH = 2

GRID_W = 64
CTX_LEN = 256
NA_HEADS = 8
NA_HEAD_DIM = D_MODEL // 16
NA_W = NA_HEADS * NA_HEAD_DIM
NA_WIN_ROWS = 8
NA_WIN_COLS = 16
CONV_CH = D_MODEL // 2
CONV_WIDTH = 31
DIFF_HEADS = 8
DIFF_QK_DIM = D_MODEL // 32
DIFF_V_DIM = 2 * DIFF_QK_DIM
DIFF_QK_W = DIFF_HEADS * 2 * DIFF_QK_DIM
DIFF_V_W = DIFF_HEADS * DIFF_V_DIM
N_BRANCH = 3
IN_SIZES = (NA_W, NA_W, NA_W, 2 * CONV_CH, DIFF_QK_W, DIFF_QK_W, DIFF_V_W, N_BRANCH * D_MODEL)
IN_COLS = 3 * NA_W + 2 * CONV_CH + 2 * DIFF_QK_W + DIFF_V_W + N_BRANCH * D_MODEL
FFN_DIM = 5632
FFN_CONV_WIDTH = 3
ROPE_THETA = 10000.0
EPS = 1e-6
BLOCK_Q = 128

kernel_name = 'hybrid_na_conformer_diffattn_convffn_dit'

F32 = jnp.float32


def _rmsnorm(x, g):
    x32 = x.astype(F32)
    y = x32 * lax.rsqrt(jnp.mean(x32 * x32, axis=-1, keepdims=True) + EPS)
    return (y * g.astype(F32)).astype(x.dtype)


def _layernorm(x, g, b):
    x32 = x.astype(F32)
    mu = jnp.mean(x32, axis=-1, keepdims=True)
    xc = x32 - mu
    y = xc * lax.rsqrt(jnp.mean(xc * xc, axis=-1, keepdims=True) + EPS)
    return (y * g.astype(F32) + b.astype(F32)).astype(x.dtype)


def _modulate(h, shift, scale):
    return h * (1.0 + scale) + shift


def _heads(t, n):
    b, s, _ = t.shape
    return t.reshape(b, s, n, -1).transpose(0, 2, 1, 3)


def _merge_heads(t):
    b, n, s, d = t.shape
    return t.transpose(0, 2, 1, 3).reshape(b, s, n * d)


def _split_in(p):
    idx, acc = [], 0
    for n in IN_SIZES[:-1]:
        acc += n
        idx.append(acc)
    return jnp.split(p, idx, axis=-1)


def _rope_axis(x, pos):
    d = x.shape[-1]
    inv = ROPE_THETA ** (-jnp.arange(0, d, 2, dtype=F32) / d)
    ang = pos.astype(F32)[:, None] * inv[None, :]
    cos, sin = jnp.cos(ang).astype(x.dtype), jnp.sin(ang).astype(x.dtype)
    x1, x2 = x[..., : d // 2], x[..., d // 2:]
    return jnp.concatenate([x1 * cos - x2 * sin, x1 * sin + x2 * cos], axis=-1)


def _rope_2d(x, rows, cols):
    h = x.shape[-1] // 2
    return jnp.concatenate([_rope_axis(x[..., :h], rows), _rope_axis(x[..., h:], cols)], axis=-1)


def _dwconv(x, w, b):
    k, ch = w.shape
    y = lax.conv_general_dilated(x, w.reshape(k, 1, ch).astype(x.dtype), (1,), [(k // 2, k // 2)],
                                 dimension_numbers=('NWC', 'WIO', 'NWC'), feature_group_count=ch)
    return y + b.astype(x.dtype)


def _attend(q, k, v):
    s = jnp.einsum('bhqd,bhkd->bhqk', q, k, preferred_element_type=F32) * (q.shape[-1] ** -0.5)
    return jnp.einsum('bhqk,bhkd->bhqd', jax.nn.softmax(s, axis=-1).astype(v.dtype), v)


def _na_latent(q, k, v, kc, vc, rpb):
    b, h, s, dh = q.shape
    n_rows = s // GRID_W
    wr = min(NA_WIN_ROWS, n_rows)
    wc = NA_WIN_COLS
    col = np.arange(GRID_W)
    cidx = np.clip(col - wc // 2, 0, GRID_W - wc)[:, None] + np.arange(wc)[None, :]
    dc = cidx - col[:, None] + (NA_WIN_COLS - 1)
    qg = q.reshape(b, h, n_rows, GRID_W, dh)
    kg = k.reshape(b, h, n_rows, GRID_W, dh)
    vg = v.reshape(b, h, n_rows, GRID_W, dh)
    scale = dh ** -0.5

    def row_block(r):
        r0 = jnp.clip(r - wr // 2, 0, n_rows - wr)
        qr = lax.dynamic_index_in_dim(qg, r, axis=2, keepdims=False)
        kw = lax.dynamic_slice_in_dim(kg, r0, wr, axis=2)[:, :, :, cidx]
        vw = lax.dynamic_slice_in_dim(vg, r0, wr, axis=2)[:, :, :, cidx]
        dr = r0 + jnp.arange(wr) - r + (NA_WIN_ROWS - 1)
        bias = rpb[:, dr][:, :, dc].transpose(0, 2, 1, 3).astype(F32)
        s_win = jnp.einsum('bhqd,bhrqkd->bhqrk', qr, kw, preferred_element_type=F32) * scale + bias[None]
        s_ctx = jnp.einsum('bhqd,bhld->bhql', qr, kc, preferred_element_type=F32) * scale
        p = jax.nn.softmax(jnp.concatenate([s_win.reshape(b, h, GRID_W, wr * wc), s_ctx], axis=-1), axis=-1)
        p = p.astype(v.dtype)
        p_win = p[..., : wr * wc].reshape(b, h, GRID_W, wr, wc)
        return (jnp.einsum('bhqrk,bhrqkd->bhqd', p_win, vw)
                + jnp.einsum('bhql,bhld->bhqd', p[..., wr * wc:], vc))

    o = lax.map(row_block, jnp.arange(n_rows))
    return o.transpose(1, 2, 0, 3, 4).reshape(b, h, s, dh)


def _diff_maps(q1, q2, k1, k2, v, lam):
    scale = DIFF_QK_DIM ** -0.5
    a1 = jax.nn.softmax(jnp.einsum('bhqd,bhkd->bhqk', q1, k1, preferred_element_type=F32) * scale, axis=-1)
    a2 = jax.nn.softmax(jnp.einsum('bhqd,bhkd->bhqk', q2, k2, preferred_element_type=F32) * scale, axis=-1)
    return jnp.einsum('bhqk,bhkd->bhqd', (a1 - lam * a2).astype(v.dtype), v)


def _diff_latent(q1, q2, k1, k2, v, lam):
    b, h, s, _ = q1.shape

    def blk(i):
        sl = lambda t: lax.dynamic_slice_in_dim(t, i * BLOCK_Q, BLOCK_Q, axis=2)
        return _diff_maps(sl(q1), sl(q2), k1, k2, v, lam)

    o = lax.map(blk, jnp.arange(s // BLOCK_Q))
    return o.transpose(1, 2, 0, 3, 4).reshape(b, h, s, -1)


def _diff_out(o, g, lam_init):
    return _merge_heads(_rmsnorm(o, g) * (1.0 - lam_init))


def _conformer(u, conv_w, conv_b, ln_g, ln_b):
    a, gt = jnp.split(u, 2, axis=-1)
    y = a * jax.nn.sigmoid(gt)
    y = _dwconv(y, conv_w, conv_b)
    return jax.nn.silu(_layernorm(y, ln_g, ln_b))


def _merge(gate_pre, oa, ob, oc, p_a, p_b, p_c, w_out):
    b, s, _ = gate_pre.shape
    g = jax.nn.sigmoid(gate_pre.astype(F32)).astype(oa.dtype).reshape(b, s, N_BRANCH, D_MODEL)
    y = g[:, :, 0] * (oa @ p_a) + g[:, :, 1] * (ob @ p_b) + g[:, :, 2] * (oc @ p_c)
    return y @ w_out


def _conv_ffn(h, w_up, conv_w, conv_b, w_down):
    u = _dwconv(h @ w_up, conv_w, conv_b)
    a, b = jnp.split(u, 2, axis=-1)
    return (jax.nn.silu(a) * b) @ w_down


def setup_inputs(seed: int = 0) -> dict:
    key = jax.random.key(seed)
    ks = iter(jax.random.split(key, 32))
    L = DEPTH

    def nrm(shape, scale):
        return jax.random.normal(next(ks), shape, F32) * scale

    def gain(shape):
        return 1.0 + nrm(shape, 0.02)

    return {
        'x': nrm((BATCH, SEQ, D_MODEL), 1.0),
        'c': nrm((BATCH, D_MODEL), 1.0),
        'ctx': nrm((BATCH, CTX_LEN, D_MODEL), 1.0),
        'c_ctx': nrm((D_MODEL,), 1.0),
        'w_mod': nrm((L, D_MODEL, 6 * D_MODEL), 0.5 * D_MODEL ** -0.5),
        'b_mod': nrm((L, 6 * D_MODEL), 0.02),
        'g_pre_mix': gain((L, D_MODEL)),
        'w_in': nrm((L, D_MODEL, IN_COLS), D_MODEL ** -0.5),
        'na_rpb': nrm((L, NA_HEADS, 2 * NA_WIN_ROWS - 1, 2 * NA_WIN_COLS - 1), 0.1),
        'conv_w': nrm((L, CONV_WIDTH, CONV_CH), CONV_WIDTH ** -0.5),
        'conv_b': nrm((L, CONV_CH), 0.02),
        'conv_ln_g': gain((L, CONV_CH)),
        'conv_ln_b': nrm((L, CONV_CH), 0.02),
        'lam_q1': nrm((L, DIFF_QK_DIM), 0.1),
        'lam_k1': nrm((L, DIFF_QK_DIM), 0.1),
        'lam_q2': nrm((L, DIFF_QK_DIM), 0.1),
        'lam_k2': nrm((L, DIFF_QK_DIM), 0.1),
        'diff_ln_g': gain((L, DIFF_V_DIM)),
        'p_a': nrm((L, NA_W, D_MODEL), NA_W ** -0.5),
        'p_b': nrm((L, CONV_CH, D_MODEL), CONV_CH ** -0.5),
        'p_c': nrm((L, DIFF_V_W, D_MODEL), DIFF_V_W ** -0.5),
        'w_out': nrm((L, D_MODEL, D_MODEL), D_MODEL ** -0.5),
        'g_post_mix': gain((L, D_MODEL)),
        'g_pre_ffn': gain((L, D_MODEL)),
        'w_up': nrm((L, D_MODEL, 2 * FFN_DIM), D_MODEL ** -0.5),
        'ffn_conv_w': nrm((L, FFN_CONV_WIDTH, 2 * FFN_DIM), FFN_CONV_WIDTH ** -0.5),
        'ffn_conv_b': nrm((L, 2 * FFN_DIM), 0.02),
        'w_down': nrm((L, FFN_DIM, D_MODEL), FFN_DIM ** -0.5),
        'g_post_ffn': gain((L, D_MODEL)),
    }


def reference(x, c, ctx, c_ctx, w_mod, b_mod, g_pre_mix, w_in, na_rpb, conv_w, conv_b, conv_ln_g,
              conv_ln_b, lam_q1, lam_k1, lam_q2, lam_k2, diff_ln_g, p_a, p_b, p_c, w_out, g_post_mix,
              g_pre_ffn, w_up, ffn_conv_w, ffn_conv_b, w_down, g_post_ffn):
    b, s, _ = x.shape
    t = jnp.arange(s)
    rows, cols = t // GRID_W, t % GRID_W
    dqk = DIFF_QK_DIM
    for l in range(DEPTH):
        last = l == DEPTH - 1
        lam_init = 0.8 - 0.6 * math.exp(-0.3 * l)
        lam = (jnp.exp(jnp.sum((lam_q1[l] * lam_k1[l]).astype(F32)))
               - jnp.exp(jnp.sum((lam_q2[l] * lam_k2[l]).astype(F32))) + lam_init)
        mx = jnp.split((jax.nn.silu(c) @ w_mod[l] + b_mod[l])[:, None, :], 6, axis=-1)
        mc = jnp.split((jax.nn.silu(c_ctx) @ w_mod[l] + b_mod[l])[None, None, :], 6, axis=-1)

        hx = _modulate(_rmsnorm(x, g_pre_mix[l]), mx[0], mx[1])
        hc = _modulate(_rmsnorm(ctx, g_pre_mix[l]), mc[0], mc[1])
        nq, nk, nv, u, dq, dk, dv, gt = _split_in(hx @ w_in[l])
        cnq, cnk, cnv, cu, cdq, cdk, cdv, cgt = _split_in(hc @ w_in[l])
        kc_na, vc_na = _heads(cnk, NA_HEADS), _heads(cnv, NA_HEADS)
        dkc, dvc = _heads(cdk, DIFF_HEADS), _heads(cdv, DIFF_HEADS)

        oa = _merge_heads(_na_latent(_heads(nq, NA_HEADS), _heads(nk, NA_HEADS), _heads(nv, NA_HEADS),
                                     kc_na, vc_na, na_rpb[l]))
        ob = _conformer(u, conv_w[l], conv_b[l], conv_ln_g[l], conv_ln_b[l])
        dqh, dkh = _heads(dq, DIFF_HEADS), _heads(dk, DIFF_HEADS)
        q1 = _rope_2d(dqh[..., :dqk], rows, cols)
        q2 = _rope_2d(dqh[..., dqk:], rows, cols)
        k1 = jnp.concatenate([_rope_2d(dkh[..., :dqk], rows, cols), dkc[..., :dqk]], axis=2)
        k2 = jnp.concatenate([_rope_2d(dkh[..., dqk:], rows, cols), dkc[..., dqk:]], axis=2)
        v_all = jnp.concatenate([_heads(dv, DIFF_HEADS), dvc], axis=2)
        oc = _diff_out(_diff_latent(q1, q2, k1, k2, v_all, lam), diff_ln_g[l], lam_init)
        mix_x = _merge(gt, oa, ob, oc, p_a[l], p_b[l], p_c[l], w_out[l])
        if not last:
            dqc = _heads(cdq, DIFF_HEADS)
            oa_c = _merge_heads(_attend(_heads(cnq, NA_HEADS), kc_na, vc_na))
            ob_c = _conformer(cu, conv_w[l], conv_b[l], conv_ln_g[l], conv_ln_b[l])
            oc_c = _diff_out(_diff_maps(dqc[..., :dqk], dqc[..., dqk:], dkc[..., :dqk], dkc[..., dqk:], dvc, lam),
                             diff_ln_g[l], lam_init)
            mix_c = _merge(cgt, oa_c, ob_c, oc_c, p_a[l], p_b[l], p_c[l], w_out[l])
            ctx = ctx + mc[2] * _rmsnorm(mix_c, g_post_mix[l])
        x = x + mx[2] * _rmsnorm(mix_x, g_post_mix[l])

        hx = _modulate(_rmsnorm(x, g_pre_ffn[l]), mx[3], mx[4])
        x = x + mx[5] * _rmsnorm(_conv_ffn(hx, w_up[l], ffn_conv_w[l], ffn_conv_b[l], w_down[l]), g_post_ffn[l])
        if not last:
            hc = _modulate(_rmsnorm(ctx, g_pre_ffn[l]), mc[3], mc[4])
            ctx = ctx + mc[5] * _rmsnorm(_conv_ffn(hc, w_up[l], ffn_conv_w[l], ffn_conv_b[l], w_down[l]),
                                         g_post_ffn[l])
    return x
```

```python
import math
from contextlib import ExitStack
import numpy as np
import concourse.bass as bass
import concourse.mybir as mybir
from concourse.bass_utils import run_bass_kernel_spmd

F32 = mybir.dt.float32
BF16 = mybir.dt.bfloat16
AF = mybir.ActivationFunctionType
ALU = mybir.AluOpType

D = 2048
KC = 16
L_CTX = 256
GRID_W = 64
NH = 8
IN_COLS = 14336
FFN = 5632
EPS = 1e-6
NDMA_SEM = 16
NEG = -30000.0
STRICT = True


class Eng:
    def __init__(self, name, h, sem, dsems):
        self.name, self.h, self.sem, self.n = name, h, sem, 0
        self.seen = {}
        self.dsems = dsems
        self.ndma = 0


class Buf:
    __slots__ = ("name", "ap", "w", "r", "t", "prev")

    def __init__(self, name, t=None):
        self.name = name
        self.t = t
        self.ap = t
        self.w = {}
        self.r = {}
        self.prev = {}

    def __getitem__(self, idx):
        return self.t[idx]


class Pool:
    def __init__(self, bufs):
        self.bufs = bufs
        self.i = 0

    def next(self):
        b = self.bufs[self.i % len(self.bufs)]
        self.i += 1
        return b


class Prog:
    def __init__(self, nc, es):
        self.nc = nc
        self.es = es
        self.E = {}
        for name, h, nd in (("pe", nc.tensor, 0), ("act", nc.scalar, 0), ("dve", nc.vector, 0),
                            ("pool", nc.gpsimd, NDMA_SEM), ("sp", nc.sync, NDMA_SEM)):
            sem = es.enter_context(nc.semaphore("sem_" + name))
            ds = [es.enter_context(nc.semaphore("dsem_%s_%d" % (name, i))) for i in range(nd)]
            self.E[name] = Eng(name, h, sem, ds)
        self.uid = 0

    def sb(self, stack, shape, dt, name=None):
        self.uid += 1
        name = (name or "t") + "_%d" % self.uid
        t = stack.enter_context(self.nc.sbuf_tensor(name, list(shape), dt))
        return Buf(name, t)

    def sbpool(self, stack, n, shape, dt, name=None):
        return Pool([self.sb(stack, shape, dt, name) for _ in range(n)])

    def ps(self, stack, shape=(128, 512), dt=F32, name=None):
        self.uid += 1
        name = (name or "ps") + "_%d" % self.uid
        t = stack.enter_context(self.nc.psum_tensor(name, list(shape), dt))
        return Buf(name, t)

    def _deps(self, eng, reads, writes, waw, strict=True):
        deps = {}

        def add(d):
            for k, (sem, val, small) in d.items():
                if sem is eng.sem and (eng.name == "pe" or not (small or (STRICT and strict))):
                    continue
                if k not in deps or deps[k][1] < val:
                    deps[k] = (sem, val)
        for b in reads:
            add(b.w)
        for b in writes:
            if b.r:
                add(b.r)
            else:
                add(b.prev)
                if waw:
                    add(b.w)
        return deps

    def _wait(self, eng, deps):
        for k, (sem, val) in deps.items():
            if eng.seen.get(k, 0) >= val:
                continue
            eng.h.wait_ge(sem, val)
            eng.seen[k] = val

    def _record(self, tok, reads, writes):
        k = id(tok[0])
        for b in reads:
            b.r[k] = tok
        for b in writes:
            if b.r:
                b.w = {k: tok}
                b.prev = b.r
                b.r = {}
            else:
                b.w[k] = tok

    def op(self, engname, fn, reads=(), writes=(), small=False, waw=True, strict=True, signal=True):
        eng = self.E[engname]
        self._wait(eng, self._deps(eng, reads, writes, waw, strict))
        ins = fn(eng.h)
        if signal:
            ins.then_inc(eng.sem, 1)
            eng.n += 1
            self._record((eng.sem, eng.n, small), reads, writes)
        else:
            self._record((eng.sem, eng.n + 1, small), reads, writes)

    def dma(self, q, out, in_, reads=(), writes=(), waw=False):
        eng = self.E[q]
        deps = self._deps(eng, reads, writes, waw)
        i = eng.ndma % NDMA_SEM
        rnd = eng.ndma // NDMA_SEM
        sem = eng.dsems[i]
        if rnd > 0:
            k = id(sem)
            if k not in deps or deps[k][1] < 16 * rnd:
                deps[k] = (sem, 16 * rnd)
        self._wait(eng, deps)
        eng.h.dma_start(out=out, in_=in_).then_inc(sem, 16)
        eng.ndma += 1
        self._record((sem, 16 * (rnd + 1), False), reads, writes)

    def barrier(self):
        toks = {}
        for e in self.E.values():
            if e.n:
                toks[id(e.sem)] = (e.sem, e.n, e)
            for i, s in enumerate(e.dsems):
                cnt = (e.ndma - i + NDMA_SEM - 1) // NDMA_SEM
                if cnt > 0:
                    toks[id(s)] = (s, 16 * cnt, None)
        for e in self.E.values():
            d = {k: (s, v) for k, (s, v, own) in toks.items() if own is not e}
            self._wait(e, d)

    def mm(self, out, lhsT, rhs, start, stop, reads, writes, signal=None):
        if signal is None:
            signal = stop
        self.op("pe", lambda h: h.matmul(out, lhsT, rhs, start=start, stop=stop), reads, writes, signal=signal)

    def act(self, out, in_, func, reads, writes, bias=None, scale=None, small=False):
        kw = {}
        if bias is not None:
            kw["bias"] = bias
        if scale is not None:
            kw["scale"] = scale
        self.op("act", lambda h: h.activation(out, in_, func, **kw), reads, writes, small=small)

    def tt(self, eng, out, in0, in1, op, reads, writes, small=False, strict=True):
        self.op(eng, lambda h: h.tensor_tensor(out, in0, in1, op), reads, writes, small=small, strict=strict)

    def ts(self, eng, out, in0, s1, s2, op0, op1, reads, writes, small=False):
        if s2 is None:
            self.op(eng, lambda h: h.tensor_scalar(out, in0, s1, None, op0), reads, writes, small=small)
        else:
            self.op(eng, lambda h: h.tensor_scalar(out, in0, s1, s2, op0, op1), reads, writes, small=small)

    def stt(self, eng, out, in0, scalar, in1, op0, op1, reads, writes, small=False, strict=True):
        self.op(eng, lambda h: h.scalar_tensor_tensor(out, in0, scalar, in1, op0, op1), reads, writes, small=small,
                strict=strict)

    def copy(self, eng, out, in_, reads, writes, small=False, strict=True):
        if eng == "act":
            self.op("act", lambda h: h.copy(out, in_), reads, writes, small=small, strict=strict)
        else:
            self.op(eng, lambda h: h.tensor_copy(out, in_), reads, writes, small=small, strict=strict)


def kb0_of(j, nblk):
    return min(max(j - 2, 0), nblk - 5)


def variant_of(j, nblk):
    if j == 0:
        return 1
    if j == 1:
        return 2
    if j == nblk - 2:
        return 3
    if j == nblk - 1:
        return 4
    return 0


class Kern:
    def __init__(self, S, NL=2, debug=()):
        self.S, self.NL = S, NL
        self.T = S + L_CTX
        self.debug = set(debug)
        nc = bass.Bass("TRN2", target_bir_lowering=False)
        self.nc = nc
        T = self.T

        def din(name, shape, dt=F32):
            return nc.dram_tensor(name, list(shape), dt, kind="ExternalInput").ap()

        def scr(name, shape, dt=BF16):
            kind = "ExternalOutput" if name in self.debug else "Internal"
            return nc.dram_tensor(name, list(shape), dt, kind=kind).ap()

        self.x = din("x", [S, D])
        self.ctx = din("ctx", [L_CTX, D])
        self.cvec = din("cvec", [128, KC, 2])
        self.w_mod = din("w_mod", [NL, D, 6 * D])
        self.b_mod = din("b_mod_c", [NL, 128, 96])
        self.gvec = din("gvec", [NL, 128, 4, KC])
        self.w_in = din("w_in", [NL, D, IN_COLS])
        self.na_tab = din("na_tab", [NL, NH, 5, 128, 640])
        self.conv_w = din("conv_w_c", [NL, 128, 8, 31])
        self.conv_v = din("conv_v_c", [NL, 128, 3, 8])
        self.lamv = din("lamv", [NL, 128, 4, 64])
        self.dlg = din("dlg_c", [NL, 128, 1])
        self.p_a = din("p_a", [NL, 1024, D])
        self.p_b = din("p_b", [NL, 1024, D])
        self.p_c = din("p_c", [NL, 1024, D])
        self.w_out = din("w_out", [NL, D, D])
        self.w_up = din("w_up", [NL, D, 2 * FFN])
        self.fcw = din("fcw_c", [NL, 128, 88, 3])
        self.fcb = din("fcb_c", [NL, 128, 88])
        self.w_down = din("w_down", [NL, FFN, D])
        self.ropeC = din("ropeC", [128, S])
        self.ropeS = din("ropeS", [128, S])
        self.perm_in = din("perm", [128, 128])
        self.ident_in = din("ident", [128, 128])
        self.out = nc.dram_tensor("out", [S, D], F32, kind="ExternalOutput").ap()
        self.Wi = scr("Wi", [NL, 28, 128, KC, 512])
        self.Pa = scr("Pa", [NL, 4, 128, 8, 512])
        self.Pb = scr("Pb", [NL, 4, 128, 8, 512])
        self.Pc = scr("Pc", [NL, 4, 128, 8, 512])
        self.Wo = scr("Wo", [NL, 4, 128, KC, 512])
        self.Wu = scr("Wu", [NL, 22, 128, KC, 512])
        self.Wd = scr("Wd", [NL, 16, 128, 44, 128])
        self.xT = scr("xT", [D, T], F32)
        self.NQ = scr("NQ", [1024, T])
        self.NK = scr("NK", [1024, T])
        self.NV = scr("NV", [NH, 128, T // 128, 128])
        self.GLU = scr("GLU", [1024, T])
        self.DQ = scr("DQ", [1024, T])
        self.DK = scr("DK", [1024, T])
        self.DV = scr("DV", [NH, 128, T // 128, 128])
        self.G = scr("G", [3 * D, T])
        self.OA = scr("OA", [1024, T])
        self.OB = scr("OB", [1024, T])
        self.OC = scr("OC", [1024, T])
        self.U = scr("U", [2 * FFN, T])

    def build(self, phases=None):
        nc = self.nc
        with ExitStack() as es:
            p = Prog(nc, es)
            self.p = p
            self.ones_bf = p.sb(es, [128, 128], BF16, "ones_bf")
            self.ones_f = p.sb(es, [128, 128], F32, "ones_f")
            self.ident = p.sb(es, [128, 128], F32, "ident")
            self.perm = p.sb(es, [128, 128], BF16, "perm")
            self.cs = p.sb(es, [128, KC, 2], F32, "cs")
            self.modc = p.sb(es, [128, 6, 2, KC], F32, "modc")
            self.lam = p.sb(es, [128, 2], F32, "lam")
            self.dlgs = p.sb(es, [128, 1], F32, "dlgs")
            self.psT = [p.ps(es, shape=(128, 1024)) for _ in range(4)]
            halves = []
            for T in self.psT:
                halves.append(Buf(T.name + "a", T.t[:, 0:512]))
                halves.append(Buf(T.name + "b", T.t[:, 512:1024]))
            self.psum = Pool(halves)
            self.cb_vals = [D * EPS, EPS, 128 * EPS]
            self.cbias = p.sb(es, [128, len(self.cb_vals)], F32, "cbias")
            for i, v in enumerate(self.cb_vals):
                p.op("dve", lambda h, i=i, v=v: h.memset(self.cbias[:, i:i + 1], v), (), [self.cbias], small=True)
            p.op("dve", lambda h: h.memset(self.ones_f[:], 1.0), (), [self.ones_f])
            p.op("dve", lambda h: h.memset(self.ones_bf[:], 1.0), (), [self.ones_bf])
            p.dma("sp", self.ident[:], self.ident_in, writes=[self.ident])
            p.dma("pool", self.perm[:], self.perm_in, writes=[self.perm])
            p.dma("sp", self.cs[:], self.cvec, writes=[self.cs])
            with ExitStack() as st:
                sg = p.sb(st, [128, KC, 2], F32, "sg")
                p.act(sg[:], self.cs[:], AF.Sigmoid, [self.cs], [sg], small=True)
                p.tt("dve", self.cs[:], self.cs[:], sg[:], ALU.mult, [self.cs, sg], [self.cs], small=True)
                p.barrier()
            S, T = self.S, self.T
            lat_tiles = [(t0, 512, 0) for t0 in range(0, S, 512)]
            ctx_tile = (S, L_CTX, 1)
            ph = phases
            if ph is None or "wconv" in ph:
                self.phase_wconv()
            if ph is None or "tin" in ph:
                self.phase_tin()
            for l in range(self.NL):
                last = l == self.NL - 1
                if ph is None or "mod" in ph:
                    self.phase_mod(l)
                if ph is None or "A" in ph:
                    self.phase_A(l, lat_tiles + [ctx_tile], last)
                tl = lat_tiles + ([] if last else [ctx_tile])
                if ph is None or "NA" in ph:
                    self.phase_NA(l, tl)
                if ph is None or "CF" in ph:
                    self.phase_conformer(l, tl)
                if ph is None or "DF" in ph:
                    self.phase_diff(l, tl)
                if ph is None or "MG" in ph:
                    self.phase_merge(l, tl)
                if ph is None or "C1" in ph:
                    self.phase_ffn_up(l, tl)
                if ph is None or "C2" in ph:
                    self.phase_ffn_down(l, tl)
            if ph is None or "tout" in ph:
                self.phase_tout()
            p.barrier()
        return nc

    def phase_wconv(self):
        p = self.p
        for l in range(self.NL):
            def cv(dst, src, kcn, ng, cols=512):
                v = src.rearrange("(kc p) (g c) -> g p kc c", p=128, c=cols)
                for g in range(ng):
                    p.dma("pool", dst[g], v[g])
            cv(self.Wi[l], self.w_in[l], KC, 28)
            cv(self.Pa[l], self.p_a[l], 8, 4)
            cv(self.Pb[l], self.p_b[l], 8, 4)
            cv(self.Pc[l], self.p_c[l], 8, 4)
            cv(self.Wo[l], self.w_out[l], KC, 4)
            cv(self.Wu[l], self.w_up[l], KC, 22)
            cv(self.Wd[l], self.w_down[l], 44, 16, cols=128)
        p.barrier()

    def phase_tin(self):
        p = self.p
        S = self.S
        with ExitStack() as st:
            xin = p.sbpool(st, 8, [128, D], F32, "xin")
            xo = p.sbpool(st, 4, [128, 512], F32, "xo")
            tiles = [(self.x, t0, 512, t0) for t0 in range(0, S, 512)] + [(self.ctx, 0, L_CTX, S)]
            cnt = 0
            for (src, r0, n, c0) in tiles:
                nb = n // 128
                xs = []
                for i in range(nb):
                    b = xin.next()
                    p.dma("sp", b[:], src[r0 + i * 128: r0 + (i + 1) * 128, :], writes=[b])
                    xs.append(b)
                for fc in range(KC):
                    ps = self.psum.next()
                    for i in range(nb):
                        p.op("pe", lambda h, i=i, fc=fc, ps=ps: h.transpose(
                            ps[:, i * 128:(i + 1) * 128], xs[i][:, fc * 128:(fc + 1) * 128], self.ident[:]),
                            [xs[i], self.ident], [ps])
                    o = xo.next()
                    eng = "act" if cnt % 2 == 0 else "dve"
                    cnt += 1
                    p.copy(eng, o[:, :n], ps[:, :n], [ps], [o])
                    p.dma("pool", self.xT[fc * 128:(fc + 1) * 128, c0:c0 + n], o[:, :n], reads=[o])
            p.barrier()

    def phase_tout(self):
        p = self.p
        S = self.S
        xTv = self.xT.rearrange("(kc p) t -> p kc t", p=128)
        with ExitStack() as st:
            xi = p.sbpool(st, 2, [128, KC, 512], F32, "xi")
            xo = p.sbpool(st, 3, [128, D], F32, "xo2")
            cnt = 0
            for t0 in range(0, S, 512):
                b = xi.next()
                p.dma("sp", b[:], xTv[:, :, t0:t0 + 512], writes=[b])
                for i in range(4):
                    o = xo.next()
                    for f4 in range(4):
                        ps = self.psum.next()
                        for k in range(4):
                            fc = f4 * 4 + k
                            p.op("pe", lambda h, ps=ps, k=k, fc=fc, i=i, b=b: h.transpose(
                                ps[:, k * 128:(k + 1) * 128], b[:, fc, i * 128:(i + 1) * 128], self.ident[:]),
                                [b, self.ident], [ps])
                        eng = "act" if cnt % 2 == 0 else "dve"
                        cnt += 1
                        p.copy(eng, o[:, f4 * 512:(f4 + 1) * 512], ps[:], [ps], [o])
                    p.dma("pool", self.out[t0 + i * 128: t0 + (i + 1) * 128, :], o[:], reads=[o])
            p.barrier()

    def phase_mod(self, l):
        p = self.p
        with ExitStack() as st:
            wm = p.sbpool(st, 2, [128, KC, 512], F32, "wm")
            modv = p.sb(st, [128, 96, 2], F32, "modv")
            bm = p.sb(st, [128, 96], F32, "bm")
            gv = p.sb(st, [128, 4, KC], F32, "gv")
            lv = p.sb(st, [128, 4, 64], F32, "lv")
            lt = p.sb(st, [128, 2, 64], F32, "lt")
            ls = p.sb(st, [128, 2], F32, "ls")
            p.dma("sp", bm[:], self.b_mod[l], writes=[bm])
            p.dma("sp", gv[:], self.gvec[l], writes=[gv])
            p.dma("sp", self.dlgs[:], self.dlg[l], writes=[self.dlgs])
            p.dma("sp", lv[:], self.lamv[l], writes=[lv])
            wv = self.w_mod[l].rearrange("(kc p) (g c) -> g p kc c", p=128, c=512)
            ps = self.psum.next()
            for g in range(24):
                w = wm.next()
                p.dma("sp", w[:], wv[g], writes=[w])
                for j in range(4):
                    oc = g * 4 + j
                    for kc in range(KC):
                        p.mm(ps[:, oc * 2:oc * 2 + 2], w[:, kc, j * 128:(j + 1) * 128], self.cs[:, kc, :],
                             kc == 0, kc == KC - 1, [w, self.cs], [ps])
            psv = ps[:, 0:192].rearrange("p (o n) -> p o n", n=2)
            for n in range(2):
                p.tt("dve", modv[:, :, n], psv[:, :, n], bm[:], ALU.add, [ps, bm], [modv], small=True)
            sD = math.sqrt(D)
            mc = self.modc
            for n in range(2):
                m = lambda j: modv[:, j * 16:(j + 1) * 16, n]
                p.stt("dve", mc[:, 0, n, :], m(1), 1.0, gv[:, 0, :], ALU.add, ALU.mult, [modv, gv], [mc], small=True)
                p.ts("dve", mc[:, 0, n, :], mc[:, 0, n, :], sD, None, ALU.mult, None, [mc], [mc], small=True)
                p.copy("dve", mc[:, 1, n, :], m(0), [modv], [mc], small=True)
                p.stt("dve", mc[:, 2, n, :], m(2), sD, gv[:, 1, :], ALU.mult, ALU.mult, [modv, gv], [mc], small=True)
                p.stt("dve", mc[:, 3, n, :], m(4), 1.0, gv[:, 2, :], ALU.add, ALU.mult, [modv, gv], [mc], small=True)
                p.ts("dve", mc[:, 3, n, :], mc[:, 3, n, :], sD, None, ALU.mult, None, [mc], [mc], small=True)
                p.copy("dve", mc[:, 4, n, :], m(3), [modv], [mc], small=True)
                p.stt("dve", mc[:, 5, n, :], m(5), sD, gv[:, 3, :], ALU.mult, ALU.mult, [modv, gv], [mc], small=True)
            lam_init = 0.8 - 0.6 * math.exp(-0.3 * l)
            p.tt("dve", lt[:, 0, :], lv[:, 0, :], lv[:, 1, :], ALU.mult, [lv], [lt], small=True)
            p.tt("dve", lt[:, 1, :], lv[:, 2, :], lv[:, 3, :], ALU.mult, [lv], [lt], small=True)
            p.op("dve", lambda h: h.tensor_reduce(ls[:], lt[:], mybir.AxisListType.X, ALU.add), [lt], [ls], small=True)
            p.act(ls[:], ls[:], AF.Exp, [ls], [ls], small=True)
            p.stt("dve", self.lam[:, 0:1], ls[:, 1:2], -lam_init, ls[:, 0:1], ALU.add, ALU.subtract,
                  [ls], [self.lam], small=True)
            p.ts("dve", self.dlgs[:], self.dlgs[:], (1.0 - lam_init) * math.sqrt(128.0), None, ALU.mult, None,
                 [self.dlgs], [self.dlgs], small=True)
            p.barrier()

    def rsqrt(self, out, in_, c, reads, obuf, small=False):
        p = self.p
        p.act(out, in_, AF.Sqrt, reads + [self.cbias], [obuf], bias=self.cbias[:, self.cbias_idx(c):self.cbias_idx(c) + 1], small=small)
        p.op("dve", lambda h: h.reciprocal(out, out), [obuf], [obuf], small=small)

    def cbias_idx(self, c):
        return self.cb_vals.index(c)

    def norm_mod(self, xt, hT, n, ia, ib, m, sq, rs, tmp):
        p = self.p
        ps = self.psum.next()
        for kc in range(KC):
            q = sq.next()
            p.act(q[:, :n], xt[:, kc, :n], AF.Square, [xt], [q])
            p.mm(ps[:, :n], self.ones_bf[:], q[:, :n], kc == 0, kc == KC - 1, [self.ones_bf, q], [ps], signal=True)
        self.rsqrt(rs[:, :n], ps[:, :n], D * EPS, [ps], rs)
        mc = self.modc
        for kc in range(KC):
            t = tmp.next()
            p.stt("dve", t[:, :n], xt[:, kc, :n], mc[:, ia, m, kc:kc + 1], rs[:, :n], ALU.mult, ALU.mult,
                  [xt, mc, rs], [t])
            p.act(hT[:, kc, :n], t[:, :n], AF.Identity, [t, mc], [hT], bias=mc[:, ib, m, kc:kc + 1])

    def phase_A(self, l, tiles, last):
        p = self.p
        S = self.S
        xTv = self.xT.rearrange("(kc p) t -> p kc t", p=128)
        with ExitStack() as st:
            xpool = p.sbpool(st, 2, [128, KC, 512], F32, "xa")
            sq = p.sbpool(st, 2, [128, 512], BF16, "sq")
            rs = p.sb(st, [128, 512], F32, "rs")
            tmp = p.sbpool(st, 2, [128, 512], F32, "tmpa")
            hpool = p.sbpool(st, 2, [128, KC, 512], BF16, "hT")
            wpool = p.sbpool(st, 3, [128, KC, 512], BF16, "wa")
            abuf = p.sb(st, [128, 8, 512], F32, "abuf")
            stg = p.sbpool(st, 4, [128, 512], BF16, "stg")
            f32t = p.sbpool(st, 3, [128, 512], F32, "f32t")
            rc = p.sbpool(st, 2, [128, 512], F32, "rc")
            rsn = p.sbpool(st, 2, [128, 512], F32, "rsn")
            cnt = 0
            for (t0, n, m) in tiles:
                xt = xpool.next()
                p.dma("sp", xt[:, :, :n], xTv[:, :, t0:t0 + n], writes=[xt])
                hT = hpool.next()
                self.norm_mod(xt, hT, n, 0, 1, m, sq, rs, tmp)
                if not m:
                    rcb, rsb = rc.next(), rsn.next()
                    p.dma("sp", rcb[:, :n], self.ropeC[:, t0:t0 + n], writes=[rcb])
                    p.dma("sp", rsb[:, :n], self.ropeS[:, t0:t0 + n], writes=[rsb])
                groups = range(28)
                import os
                if os.environ.get("KGROUPS"):
                    groups = [int(v) for v in os.environ["KGROUPS"].split(",")]
                if m and last:
                    groups = [2, 3, 4, 5, 12, 13, 14, 15]
                for g in groups:
                    w = wpool.next()
                    p.dma("sp", w[:], self.Wi[l, g], writes=[w])
                    if g in (4, 5, 14, 15):
                        dst = self.NV if g < 6 else self.DV
                        hh = (g - 4) * 4 if g < 6 else (g - 14) * 4
                        for tb in range(n // 128):
                            ps = self.psum.next()
                            for kc in range(KC):
                                p.mm(ps[:, :], hT[:, kc, tb * 128:(tb + 1) * 128], w[:, kc, :], kc == 0, kc == KC - 1,
                                     [hT, w], [ps])
                            o = stg.next()
                            eng = "act" if cnt % 2 == 0 else "dve"
                            cnt += 1
                            p.copy(eng, o[:], ps[:], [ps], [o])
                            blk = (t0 + tb * 128) // 128
                            p.dma("pool", dst[hh:hh + 4, :, blk, :].rearrange("h p c -> p h c"),
                                  o[:].rearrange("p (h c) -> p h c", c=128), reads=[o])
                        continue
                    for j in range(4):
                        c = g * 4 + j
                        ps = self.psum.next()
                        for kc in range(KC):
                            p.mm(ps[:, :n], w[:, kc, j * 128:(j + 1) * 128], hT[:, kc, :n], kc == 0, kc == KC - 1,
                                 [w, hT], [ps])
                        if c < 16:
                            dst = self.NQ if c < 8 else self.NK
                            o = stg.next()
                            eng = "act" if cnt % 2 == 0 else "dve"
                            cnt += 1
                            p.copy(eng, o[:, :n], ps[:, :n], [ps], [o])
                            r = (c % 8) * 128
                            p.dma("pool", dst[r:r + 128, t0:t0 + n], o[:, :n], reads=[o])
                        elif c < 32:
                            p.copy("act", abuf[:, c - 24, :n], ps[:, :n], [ps], [abuf])
                        elif c < 40:
                            f = f32t.next()
                            p.act(f[:, :n], ps[:, :n], AF.Sigmoid, [ps], [f])
                            o = stg.next()
                            p.tt("dve", o[:, :n], abuf[:, c - 32, :n], f[:, :n], ALU.mult, [abuf, f], [o])
                            r = (c - 32) * 128
                            p.dma("pool", self.GLU[r:r + 128, t0:t0 + n], o[:, :n], reads=[o])
                        elif c < 56:
                            dst = self.DQ if c < 48 else self.DK
                            r = (c % 8) * 128
                            xb = stg.next()
                            p.copy("act", xb[:, :n], ps[:, :n], [ps], [xb])
                            if m:
                                p.dma("pool", dst[r:r + 128, t0:t0 + n], xb[:, :n], reads=[xb])
                            else:
                                ps2 = self.psum.next()
                                p.mm(ps2[:, :n], self.perm[:], xb[:, :n], True, True, [self.perm, xb], [ps2])
                                f1 = f32t.next()
                                p.tt("dve", f1[:, :n], ps2[:, :n], rsb[:, :n], ALU.mult, [ps2, rsb], [f1])
                                f2 = f32t.next()
                                p.tt("dve", f2[:, :n], xb[:, :n], rcb[:, :n], ALU.mult, [xb, rcb], [f2])
                                o = stg.next()
                                p.tt("dve", o[:, :n], f1[:, :n], f2[:, :n], ALU.add, [f1, f2], [o])
                                p.dma("pool", dst[r:r + 128, t0:t0 + n], o[:, :n], reads=[o])
                        else:
                            o = stg.next()
                            p.act(o[:, :n], ps[:, :n], AF.Sigmoid, [ps], [o])
                            r = (c - 64) * 128
                            p.dma("pool", self.G[r:r + 128, t0:t0 + n], o[:, :n], reads=[o])
            p.barrier()

    def phase_NA(self, l, tiles):
        p = self.p
        S, T = self.S, self.T
        nblk = S // 128
        scale = 128 ** -0.5
        NKv = self.NK.rearrange("(h p) t -> p h t", p=128)
        NQv = self.NQ.rearrange("(h p) t -> p h t", p=128)
        OAv = self.OA.rearrange("(h p) t -> p h t", p=128)
        with ExitStack() as st:
            kc_sb = p.sb(st, [128, NH, 256], BF16, "kcs")
            vc_sb = p.sb(st, [128, NH, 2, 128], BF16, "vcs")
            p.dma("sp", kc_sb[:], NKv[:, :, S:T], writes=[kc_sb])
            p.dma("sp", vc_sb[:], self.NV[:, :, nblk:nblk + 2, :].rearrange("h p b c -> p h b c"), writes=[vc_sb])
            kpool = p.sbpool(st, 2, [128, NH, 1024], BF16, "nak")
            vpool = p.sbpool(st, 2, [128, NH, 8, 128], BF16, "nav")
            qpool = p.sbpool(st, 2, [128, NH, 512], BF16, "naq")
            tabp = p.sbpool(st, 3, [128, 640], F32, "tab")
            sbp = p.sbpool(st, 2, [128, 896], F32, "nsb")
            ep = p.sbpool(st, 2, [128, 896], BF16, "nae")
            rzp = p.sbpool(st, 2, [128, 128], F32, "rz")
            oap = p.sbpool(st, 2, [128, NH, 512], BF16, "oat")
            for (t0, n, m) in tiles:
                qt = qpool.next()
                p.dma("sp", qt[:, :, :n], NQv[:, :, t0:t0 + n], writes=[qt])
                oa = oap.next()
                if not m:
                    j0 = t0 // 128
                    lo = kb0_of(j0, nblk)
                    hi = kb0_of(j0 + 3, nblk) + 5
                    nb = hi - lo
                    kt = kpool.next()
                    p.dma("sp", kt[:, :, :nb * 128], NKv[:, :, lo * 128:hi * 128], writes=[kt])
                    vt = vpool.next()
                    p.dma("sp", vt[:, :, :nb, :], self.NV[:, :, lo:hi, :].rearrange("h p b c -> p h b c"), writes=[vt])
                for h in range(NH):
                    for jj in range(n // 128):
                        q_ap = qt[:, h, jj * 128:(jj + 1) * 128]
                        e = ep.next()
                        if not m:
                            j = j0 + jj
                            kb = kb0_of(j, nblk) - lo
                            tb = tabp.next()
                            p.dma("sp", tb[:], self.na_tab[l, h, variant_of(j, nblk)], writes=[tb])
                            psA = self.psum.next()
                            psB = self.psum.next()
                            for i in range(5):
                                dst = psA[:, i * 128:(i + 1) * 128] if i < 4 else psB[:, 0:128]
                                p.mm(dst, kt[:, h, (kb + i) * 128:(kb + i + 1) * 128], q_ap, True, True,
                                     [kt, qt], [psA if i < 4 else psB])
                            for i in range(2):
                                p.mm(psB[:, (1 + i) * 128:(2 + i) * 128], kc_sb[:, h, i * 128:(i + 1) * 128], q_ap,
                                     True, True, [kc_sb, qt], [psB])
                            sb = sbp.next()
                            p.stt("dve", sb[:, 0:512], psA[:, :], scale, tb[:, 0:512], ALU.mult, ALU.add, [psA, tb], [sb])
                            p.stt("dve", sb[:, 512:640], psB[:, 0:128], scale, tb[:, 512:640], ALU.mult, ALU.add,
                                  [psB, tb], [sb])
                            p.ts("dve", sb[:, 640:896], psB[:, 128:384], scale, None, ALU.mult, None, [psB], [sb])
                            p.act(e[:, 0:896], sb[:], AF.Exp, [sb], [e])
                            nkb = 7
                            lhs_v = lambda i: vt[:, h, kb + i, :] if i < 5 else vc_sb[:, h, i - 5, :]
                            vreads = [vt, vc_sb]
                        else:
                            psB = self.psum.next()
                            for i in range(2):
                                p.mm(psB[:, i * 128:(i + 1) * 128], kc_sb[:, h, i * 128:(i + 1) * 128], q_ap,
                                     True, True, [kc_sb, qt], [psB])
                            p.act(e[:, 0:256], psB[:, 0:256], AF.Exp, [psB], [e], scale=scale)
                            nkb = 2
                            lhs_v = lambda i: vc_sb[:, h, i, :]
                            vreads = [vc_sb]
                        psO = self.psum.next()
                        psZ = self.psum.next()
                        for i in range(nkb):
                            p.mm(psO[:, 0:128], lhs_v(i), e[:, i * 128:(i + 1) * 128], i == 0, i == nkb - 1,
                                 vreads + [e], [psO])
                            p.mm(psZ[:, 0:128], self.ones_bf[:], e[:, i * 128:(i + 1) * 128], i == 0, i == nkb - 1,
                                 [self.ones_bf, e], [psZ])
                        rz = rzp.next()
                        p.op("dve", lambda hh: hh.reciprocal(rz[:], psZ[:, 0:128]), [psZ], [rz])
                        p.tt("dve", oa[:, h, jj * 128:(jj + 1) * 128], psO[:, 0:128], rz[:], ALU.mult, [psO, rz], [oa])
                p.dma("pool", OAv[:, :, t0:t0 + n], oa[:, :, :n], reads=[oa])
            p.barrier()

    def phase_conformer(self, l, tiles):
        p = self.p
        S, T = self.S, self.T
        OBv = self.OB.rearrange("(c p) t -> p c t", p=128)
        with ExitStack() as st:
            cw = p.sb(st, [128, 8, 31], F32, "cw")
            cvv = p.sb(st, [128, 3, 8], F32, "cvv")
            p.dma("sp", cw[:], self.conv_w[l], writes=[cw])
            p.dma("sp", cvv[:], self.conv_v[l], writes=[cvv])
            glp = p.sbpool(st, 3, [128, 544], BF16, "gl")
            acc = [p.sb(st, [128, 512], F32, "cacc") for _ in range(8)]
            sqp = p.sbpool(st, 2, [128, 512], F32, "csq")
            mu = p.sb(st, [128, 512], F32, "mu")
            var = p.sb(st, [128, 512], F32, "var")
            rstd = p.sb(st, [128, 512], F32, "rstd")
            tp = p.sbpool(st, 3, [128, 512], F32, "ct")
            obp = p.sbpool(st, 2, [128, 8, 512], BF16, "obt")
            for (t0, n, m) in tiles:
                seq_lo, seq_hi = (S, T) if m else (0, S)
                lo = max(t0 - 15, seq_lo)
                hi = min(t0 + n + 15, seq_hi)
                off = lo - (t0 - 15)
                psM = self.psum.next()
                psQ = self.psum.next()
                for i in range(8):
                    gl = glp.next()
                    if off > 0:
                        p.op("dve", lambda hh: hh.memset(gl[:, 0:off], 0.0), (), [gl], small=True)
                    if off + hi - lo < n + 30:
                        p.op("dve", lambda hh: hh.memset(gl[:, off + hi - lo:n + 30], 0.0), (), [gl], small=True)
                    p.dma("sp", gl[:, off:off + hi - lo], self.GLU[i * 128:(i + 1) * 128, lo:hi], writes=[gl])
                    a = acc[i]
                    p.act(a[:, :n], gl[:, 0:n], AF.Identity, [gl, cw, cvv], [a], scale=cw[:, i, 0:1], bias=cvv[:, 0, i:i + 1])
                    for j in range(1, 31):
                        p.stt("dve", a[:, :n], gl[:, j:j + n], cw[:, i, j:j + 1], a[:, :n], ALU.mult, ALU.add,
                              [gl, cw, a], [a], strict=(j == 1))
                    sq = sqp.next()
                    p.act(sq[:, :n], a[:, :n], AF.Square, [a], [sq])
                    p.mm(psM[:, :n], self.ones_f[:], a[:, :n], i == 0, i == 7, [self.ones_f, a], [psM], signal=True)
                    p.mm(psQ[:, :n], self.ones_f[:], sq[:, :n], i == 0, i == 7, [self.ones_f, sq], [psQ], signal=True)
                p.ts("dve", mu[:, :n], psM[:, :n], 1.0 / 1024, None, ALU.mult, None, [psM], [mu])
                p.tt("dve", var[:, :n], mu[:, :n], mu[:, :n], ALU.mult, [mu], [var])
                p.stt("dve", var[:, :n], psQ[:, :n], 1.0 / 1024, var[:, :n], ALU.mult, ALU.subtract, [psQ, var], [var])
                self.rsqrt(rstd[:, :n], var[:, :n], EPS, [var], rstd)
                ob = obp.next()
                for i in range(8):
                    t = tp.next()
                    p.tt("dve", t[:, :n], acc[i][:, :n], mu[:, :n], ALU.subtract, [acc[i], mu], [t])
                    p.tt("dve", t[:, :n], t[:, :n], rstd[:, :n], ALU.mult, [t, rstd], [t])
                    p.act(ob[:, i, :n], t[:, :n], AF.Silu, [t, cvv], [ob], scale=cvv[:, 1, i:i + 1], bias=cvv[:, 2, i:i + 1])
                p.dma("pool", OBv[:, :, t0:t0 + n], ob[:, :, :n], reads=[ob])
            p.barrier()

    def phase_diff(self, l, tiles):
        p = self.p
        S, T = self.S, self.T
        scale = 64 ** -0.5
        DQv = self.DQ.rearrange("(h p) t -> p h t", p=128)
        OCv = self.OC.rearrange("(h p) t -> p h t", p=128)
        with ExitStack() as st:
            qpool = p.sbpool(st, 2, [128, NH, 512], BF16, "dq")
            kpool = p.sbpool(st, 2, [128, T], BF16, "dk")
            vpool = p.sbpool(st, 2, [128, T // 128, 128], BF16, "dv")
            ocp = p.sbpool(st, 2, [128, NH, 512], BF16, "oct")
            fp = p.sbpool(st, 6, [128, 512], F32, "df")
            acc = self.psum.bufs[0:2]
            pairs = [(self.psT[i], self.psum.bufs[2 * i], self.psum.bufs[2 * i + 1]) for i in (1, 2, 3)]
            pi = 0
            z = p.sb(st, [128, 2, 512], F32, "z")
            ep = p.sbpool(st, 4, [128, 2, 512], BF16, "e12")
            for (t0, n, m) in tiles:
                qt = qpool.next()
                p.dma("sp", qt[:, :, :n], DQv[:, :, t0:t0 + n], writes=[qt])
                kts = list(range(S // 128, T // 128)) if m else list(range(T // 128))
                oc = ocp.next()
                for h in range(NH):
                    kb = kpool.next()
                    p.dma("sp", kb[:], self.DK[h * 128:(h + 1) * 128, :], writes=[kb])
                    vb = vpool.next()
                    p.dma("sp", vb[:], self.DV[h], writes=[vb])
                    O1, O2 = acc

                    def pv(e, kt, first, lastk):
                        p.mm(O1[:, :n], vb[:, kt, :], e[:, 0, :n], first, lastk, [vb, e], [O1], signal=False)
                        p.mm(O2[:, :n], vb[:, kt, :], e[:, 1, :n], first, lastk, [vb, e], [O2], signal=True)
                        if first:
                            p.copy("dve", z[:, :, :n], e[:, :, :n], [e], [z])
                        else:
                            p.tt("dve", z[:, :, :n], z[:, :, :n], e[:, :, :n], ALU.add, [z, e], [z], strict=False)
                    prev = None
                    for idx, kt in enumerate(kts):
                        TT, s1, s2 = pairs[pi % 3]
                        pi += 1
                        p.mm(s1[:, :n], kb[0:64, kt * 128:(kt + 1) * 128], qt[0:64, h, :n], True, True, [kb, qt], [s1],
                             signal=False)
                        p.mm(s2[:, :n], kb[64:128, kt * 128:(kt + 1) * 128], qt[64:128, h, :n], True, True, [kb, qt], [s2])
                        e = ep.next()
                        p.act(e[:, :, :n], TT.t[:].rearrange("p (b c) -> p b c", b=2)[:, :, :n], AF.Exp, [s1, s2], [e],
                              scale=scale)
                        if prev is not None:
                            pv(*prev)
                        prev = (e, kt, idx == 0, idx == len(kts) - 1)
                    pv(*prev)
                    z1 = z[:, 0, :]
                    z2 = z[:, 1, :]
                    z1b = z2b = z
                    _, Z1, Z2 = pairs[pi % 3]
                    pi += 1
                    p.mm(Z1[:, :n], self.ones_f[:], z1[:, :n], True, True, [self.ones_f, z], [Z1])
                    p.mm(Z2[:, :n], self.ones_f[:], z2[:, :n], True, True, [self.ones_f, z], [Z2])
                    r1 = fp.next()
                    p.op("dve", lambda hh: hh.reciprocal(r1[:, :n], Z1[:, :n]), [Z1], [r1])
                    t1 = fp.next()
                    p.tt("dve", t1[:, :n], O1[:, :n], r1[:, :n], ALU.mult, [O1, r1], [t1])
                    r2 = fp.next()
                    p.op("dve", lambda hh: hh.reciprocal(r2[:, :n], Z2[:, :n]), [Z2], [r2])
                    t2 = fp.next()
                    p.tt("dve", t2[:, :n], O2[:, :n], r2[:, :n], ALU.mult, [O2, r2], [t2])
                    p.stt("dve", t1[:, :n], t2[:, :n], self.lam[:, 0:1], t1[:, :n], ALU.mult, ALU.add,
                          [t2, self.lam, t1], [t1])
                    p.act(r1[:, :n], t1[:, :n], AF.Square, [t1], [r1])
                    _, psS, _unused = pairs[pi % 3]
                    pi += 1
                    p.mm(psS[:, :n], self.ones_f[:], r1[:, :n], True, True, [self.ones_f, r1], [psS])
                    self.rsqrt(r2[:, :n], psS[:, :n], 128 * EPS, [psS], r2)
                    p.stt("dve", oc[:, h, :n], t1[:, :n], self.dlgs[:, 0:1], r2[:, :n], ALU.mult, ALU.mult,
                          [t1, self.dlgs, r2], [oc])
                p.dma("pool", OCv[:, :, t0:t0 + n], oc[:, :, :n], reads=[oc])
            p.barrier()

    def post_residual(self, mix, pss, rs, n, t0, m, ig, xp, tp):
        p = self.p
        mc = self.modc
        self.rsqrt(rs[:, :n], pss[:, :n], D * EPS, [pss], rs)
        for kc in range(KC):
            xt = xp.next()
            p.dma("sp", xt[:, :n], self.xT[kc * 128:(kc + 1) * 128, t0:t0 + n], writes=[xt])
            t = tp.next()
            p.stt("dve", t[:, :n], mix[:, kc, :n], mc[:, ig, m, kc:kc + 1], rs[:, :n], ALU.mult, ALU.mult,
                  [mix, mc, rs], [t])
            p.tt("dve", xt[:, :n], xt[:, :n], t[:, :n], ALU.add, [xt, t], [xt])
            p.dma("pool", self.xT[kc * 128:(kc + 1) * 128, t0:t0 + n], xt[:, :n], reads=[xt])

    def phase_merge(self, l, tiles):
        p = self.p
        Gv = self.G.rearrange("(br c p) t -> p br c t", p=128, c=16)
        with ExitStack() as st:
            bp = [p.sbpool(st, 1, [128, 8, 512], BF16, "mb%d" % i) for i in range(3)]
            wp = [p.sbpool(st, 2, [128, 8, 512], BF16, "mw%d" % i) for i in range(3)]
            gtp = p.sbpool(st, 3, [128, 3, 512], BF16, "gt")
            tp = p.sbpool(st, 6, [128, 512], F32, "mt")
            y = p.sb(st, [128, KC, 512], BF16, "y")
            wop = p.sbpool(st, 2, [128, KC, 512], BF16, "wo")
            mix = p.sb(st, [128, KC, 512], F32, "mix")
            sq = p.sbpool(st, 2, [128, 512], BF16, "msq")
            rs = p.sb(st, [128, 512], F32, "mrs")
            xp = p.sbpool(st, 3, [128, 512], F32, "mx")
            srcs = [self.OA, self.OB, self.OC]
            Ws = [self.Pa, self.Pb, self.Pc]
            full_psum = self.psum
            pss = full_psum.bufs[7]
            self.psum = Pool(full_psum.bufs[0:7])
            for (t0, n, m) in tiles:
                br = []
                for i in range(3):
                    b = bp[i].next()
                    p.dma("sp", b[:, :, :n], srcs[i].rearrange("(c p) t -> p c t", p=128)[:, :, t0:t0 + n], writes=[b])
                    br.append(b)
                for og in range(4):
                    ws = []
                    for i in range(3):
                        w = wp[i].next()
                        p.dma("sp", w[:], Ws[i][l, og], writes=[w])
                        ws.append(w)
                    for j in range(4):
                        c = og * 4 + j
                        gt = gtp.next()
                        p.dma("sp", gt[:, :, :n], Gv[:, :, c, t0:t0 + n], writes=[gt])
                        ts_ = []
                        for i in range(3):
                            ps = self.psum.next()
                            for kc in range(8):
                                p.mm(ps[:, :n], ws[i][:, kc, j * 128:(j + 1) * 128], br[i][:, kc, :n], kc == 0, kc == 7,
                                     [ws[i], br[i]], [ps])
                            t = tp.next()
                            p.tt("dve", t[:, :n], ps[:, :n], gt[:, i, :n], ALU.mult, [ps, gt], [t])
                            ts_.append(t)
                        p.tt("dve", ts_[0][:, :n], ts_[0][:, :n], ts_[1][:, :n], ALU.add, [ts_[0], ts_[1]], [ts_[0]])
                        p.tt("dve", y[:, c, :n], ts_[0][:, :n], ts_[2][:, :n], ALU.add, [ts_[0], ts_[2]], [y])
                for og in range(4):
                    wo = wop.next()
                    p.dma("sp", wo[:], self.Wo[l, og], writes=[wo])
                    for j in range(4):
                        c = og * 4 + j
                        ps = self.psum.next()
                        for kc in range(KC):
                            p.mm(ps[:, :n], wo[:, kc, j * 128:(j + 1) * 128], y[:, kc, :n], kc == 0, kc == KC - 1,
                                 [wo, y], [ps])
                        p.copy("dve", mix[:, c, :n], ps[:, :n], [ps], [mix])
                        q = sq.next()
                        p.act(q[:, :n], mix[:, c, :n], AF.Square, [mix], [q])
                        p.mm(pss[:, :n], self.ones_bf[:], q[:, :n], c == 0, c == KC - 1, [self.ones_bf, q], [pss], signal=True)
                self.post_residual(mix, pss, rs, n, t0, m, 2, xp, tp)
            self.psum = full_psum
            p.barrier()

    def phase_ffn_up(self, l, tiles):
        p = self.p
        xTv = self.xT.rearrange("(kc p) t -> p kc t", p=128)
        with ExitStack() as st:
            xpool = p.sbpool(st, 2, [128, KC, 512], F32, "xu")
            sq = p.sbpool(st, 2, [128, 512], BF16, "usq")
            rs = p.sb(st, [128, 512], F32, "urs")
            tmp = p.sbpool(st, 2, [128, 512], F32, "utmp")
            hpool = p.sbpool(st, 2, [128, KC, 512], BF16, "uh")
            wpool = p.sbpool(st, 3, [128, KC, 512], BF16, "uw")
            stg = p.sbpool(st, 4, [128, 512], BF16, "ustg")
            cnt = 0
            for (t0, n, m) in tiles:
                xt = xpool.next()
                p.dma("sp", xt[:, :, :n], xTv[:, :, t0:t0 + n], writes=[xt])
                hT = hpool.next()
                self.norm_mod(xt, hT, n, 3, 4, m, sq, rs, tmp)
                for g in range(22):
                    w = wpool.next()
                    p.dma("sp", w[:], self.Wu[l, g], writes=[w])
                    for j in range(4):
                        c = g * 4 + j
                        ps = self.psum.next()
                        for kc in range(KC):
                            p.mm(ps[:, :n], w[:, kc, j * 128:(j + 1) * 128], hT[:, kc, :n], kc == 0, kc == KC - 1,
                                 [w, hT], [ps])
                        o = stg.next()
                        eng = "act" if cnt % 2 == 0 else "dve"
                        cnt += 1
                        p.copy(eng, o[:, :n], ps[:, :n], [ps], [o])
                        p.dma("pool", self.U[c * 128:(c + 1) * 128, t0:t0 + n], o[:, :n], reads=[o])
            p.barrier()

    def phase_ffn_down(self, l, tiles):
        p = self.p
        S, T = self.S, self.T
        with ExitStack() as st:
            fw = p.sb(st, [128, 88, 3], F32, "fw")
            fb = p.sb(st, [128, 88], F32, "fb")
            p.dma("sp", fw[:], self.fcw[l], writes=[fw])
            p.dma("sp", fb[:], self.fcb[l], writes=[fb])
            up = p.sbpool(st, 4, [128, 516], BF16, "fu")
            ca = p.sbpool(st, 6, [128, 512], F32, "fca")
            gT = p.sb(st, [128, 44, 512], BF16, "gT")
            wdp = p.sbpool(st, 3, [128, 44, 128], BF16, "wd")
            mix = p.sb(st, [128, KC, 512], F32, "fmix")
            sq = p.sbpool(st, 2, [128, 512], BF16, "fsq")
            rs = p.sb(st, [128, 512], F32, "frs")
            xp = p.sbpool(st, 3, [128, 512], F32, "fx")
            tp = p.sbpool(st, 3, [128, 512], F32, "ft")
            full_psum = self.psum
            pss = full_psum.bufs[7]
            self.psum = Pool(full_psum.bufs[0:7])
            for (t0, n, m) in tiles:
                seq_lo, seq_hi = (S, T) if m else (0, S)
                lo = max(t0 - 1, seq_lo)
                hi = min(t0 + n + 1, seq_hi)
                off = lo - (t0 - 1)
                for i in range(44):
                    res = []
                    for half in range(2):
                        ch = half * 44 + i
                        u = up.next()
                        if off > 0:
                            p.op("dve", lambda hh: hh.memset(u[:, 0:off], 0.0), (), [u], small=True)
                        if off + hi - lo < n + 2:
                            p.op("dve", lambda hh: hh.memset(u[:, off + hi - lo:n + 2], 0.0), (), [u], small=True)
                        p.dma("sp", u[:, off:off + hi - lo], self.U[ch * 128:(ch + 1) * 128, lo:hi], writes=[u])
                        a = ca.next()
                        p.act(a[:, :n], u[:, 1:n + 1], AF.Identity, [u, fw, fb], [a], scale=fw[:, ch, 1:2], bias=fb[:, ch:ch + 1])
                        p.stt("dve", a[:, :n], u[:, 0:n], fw[:, ch, 0:1], a[:, :n], ALU.mult, ALU.add, [u, fw, a], [a])
                        p.stt("dve", a[:, :n], u[:, 2:n + 2], fw[:, ch, 2:3], a[:, :n], ALU.mult, ALU.add, [u, fw, a], [a],
                              strict=False)
                        res.append(a)
                    s = ca.next()
                    p.act(s[:, :n], res[0][:, :n], AF.Silu, [res[0]], [s])
                    p.tt("dve", gT[:, i, :n], s[:, :n], res[1][:, :n], ALU.mult, [s, res[1]], [gT])
                for c in range(16):
                    wd = wdp.next()
                    p.dma("sp", wd[:], self.Wd[l, c], writes=[wd])
                    ps = self.psum.next()
                    for kc in range(44):
                        p.mm(ps[:, :n], wd[:, kc, :], gT[:, kc, :n], kc == 0, kc == 43, [wd, gT], [ps])
                    p.copy("dve", mix[:, c, :n], ps[:, :n], [ps], [mix])
                    q = sq.next()
                    p.act(q[:, :n], mix[:, c, :n], AF.Square, [mix], [q])
                    p.mm(pss[:, :n], self.ones_bf[:], q[:, :n], c == 0, c == KC - 1, [self.ones_bf, q], [pss], signal=True)
                self.post_residual(mix, pss, rs, n, t0, m, 5, xp, tp)
            self.psum = full_psum
            p.barrier()


def _col(v, nch):
    v = np.asarray(v, np.float32)
    return np.ascontiguousarray(np.swapaxes(v.reshape(v.shape[:-1] + (nch, 128)), -1, -2))


def _na_tables(na_rpb, nrows):
    NL = na_rpb.shape[0]
    nblk = nrows // 2
    wr, wc = 8, 16
    col = np.arange(GRID_W)
    c0 = np.clip(col - wc // 2, 0, GRID_W - wc)
    rep = {0: min(2, nblk - 3), 1: 0, 2: 1, 3: nblk - 2, 4: nblk - 1}
    tab = np.full((NL, NH, 5, 128, 640), NEG, np.float32)
    for v, j in rep.items():
        kb0 = kb0_of(j, nblk)
        for qi in range(128):
            r = 2 * j + qi // 64
            c = qi % 64
            r0 = min(max(r - wr // 2, 0), nrows - wr)
            for i in range(5):
                for kr in range(2):
                    rr = (kb0 + i) * 2 + kr
                    if rr < r0 or rr >= r0 + wr:
                        continue
                    cc = np.arange(c0[c], c0[c] + wc)
                    tab[:, :, v, kr * 64 + cc, i * 128 + qi] = na_rpb[:, :, rr - r + 7, :][:, :, cc - c + 15]
    return tab


def _rope_tables(S):
    t = np.arange(S)
    rows, cols = t // GRID_W, t % GRID_W
    inv = (10000.0 ** (-np.arange(0, 32, 2, dtype=np.float32) / 32)).astype(np.float32)
    C = np.zeros((128, S), np.float32)
    Sn = np.zeros((128, S), np.float32)
    perm = np.zeros((128, 128), np.float32)
    for pp in range(128):
        sub = pp % 64
        part = sub // 32
        i = sub % 16
        pos = (rows if part == 0 else cols).astype(np.float32)
        ang = pos * inv[i]
        C[pp] = np.cos(ang)
        Sn[pp] = np.sin(ang)
        first = (sub % 32) < 16
        partner = pp + 16 if first else pp - 16
        perm[partner, pp] = -1.0 if first else 1.0
    return C, Sn, perm


def prep_shared(inp, S):
    NL = inp["w_mod"].shape[0]
    f = lambda k: np.ascontiguousarray(np.asarray(inp[k], np.float32))
    C, Sn, perm = _rope_tables(S)
    sh = {
        "w_mod": f("w_mod"), "w_in": f("w_in"), "p_a": f("p_a"), "p_b": f("p_b"), "p_c": f("p_c"),
        "w_out": f("w_out"), "w_up": f("w_up"), "w_down": f("w_down"),
        "b_mod_c": _col(inp["b_mod"], 96),
        "gvec": np.ascontiguousarray(np.stack([_col(inp[k], KC) for k in
                                               ("g_pre_mix", "g_post_mix", "g_pre_ffn", "g_post_ffn")], axis=2)),
        "na_tab": _na_tables(np.asarray(inp["na_rpb"], np.float32), S // GRID_W),
        "conv_w_c": np.ascontiguousarray(np.asarray(inp["conv_w"], np.float32).reshape(NL, 31, 8, 128).transpose(0, 3, 2, 1)),
        "conv_v_c": np.ascontiguousarray(np.stack([_col(inp[k], 8) for k in ("conv_b", "conv_ln_g", "conv_ln_b")], axis=2)),
        "lamv": np.ascontiguousarray(np.broadcast_to(
            np.stack([np.asarray(inp[k], np.float32) for k in ("lam_q1", "lam_k1", "lam_q2", "lam_k2")], axis=1)[:, None],
            (NL, 128, 4, 64))),
        "dlg_c": np.ascontiguousarray(np.asarray(inp["diff_ln_g"], np.float32).reshape(NL, 128, 1)),
        "fcw_c": np.ascontiguousarray(np.asarray(inp["ffn_conv_w"], np.float32).reshape(NL, 3, 88, 128).transpose(0, 3, 2, 1)),
        "fcb_c": _col(inp["ffn_conv_b"], 88),
        "ropeC": C, "ropeS": Sn, "perm": perm, "ident": np.eye(128, dtype=np.float32),
    }
    return sh


def prep_core(inp, b):
    cv = np.stack([_col(np.asarray(inp["c"], np.float32)[b], KC), _col(np.asarray(inp["c_ctx"], np.float32), KC)], axis=2)
    return {"x": np.ascontiguousarray(np.asarray(inp["x"], np.float32)[b]),
            "ctx": np.ascontiguousarray(np.asarray(inp["ctx"], np.float32)[b]),
            "cvec": np.ascontiguousarray(cv)}


def kernel(**inputs):
    B, S, _ = inputs["x"].shape
    kern = Kern(S)
    nc = kern.build()
    sh = prep_shared(inputs, S)
    in_maps = [dict(sh, **prep_core(inputs, b % B)) for b in range(8)]
    res = run_bass_kernel_spmd(nc, in_maps, core_ids=list(range(8)))
    return np.stack([np.asarray(res.results[b]["out"], np.float32) for b in range(B)], axis=0)
```

```python
import math
from contextlib import ExitStack
import numpy as np
import concourse.bass as bass
import concourse.mybir as mybir
from concourse.bass_utils import run_bass_kernel_spmd

F32 = mybir.dt.float32
BF16 = mybir.dt.bfloat16
AF = mybir.ActivationFunctionType
ALU = mybir.AluOpType

D = 2048
KC = 16
L_CTX = 256
GRID_W = 64
NH = 8
IN_COLS = 14336
FFN = 5632
EPS = 1e-6
NDMA_SEM = 16
NEG = -30000.0
STRICT = True


class Eng:
    def __init__(self, name, h, sem, dsems):
        self.name, self.h, self.sem, self.n = name, h, sem, 0
        self.seen = {}
        self.dsems = dsems
        self.ndma = 0


class Buf:
    __slots__ = ("name", "ap", "w", "r", "t", "prev")

    def __init__(self, name, t=None):
        self.name = name
        self.t = t
        self.ap = t
        self.w = {}
        self.r = {}
        self.prev = {}

    def __getitem__(self, idx):
        return self.t[idx]


class Pool:
    def __init__(self, bufs):
        self.bufs = bufs
        self.i = 0

    def next(self):
        b = self.bufs[self.i % len(self.bufs)]
        self.i += 1
        return b


class Prog:
    def __init__(self, nc, es):
        self.nc = nc
        self.es = es
        self.E = {}
        for name, h, nd in (("pe", nc.tensor, 0), ("act", nc.scalar, 0), ("dve", nc.vector, 0),
                            ("pool", nc.gpsimd, NDMA_SEM), ("sp", nc.sync, NDMA_SEM)):
            sem = es.enter_context(nc.semaphore("sem_" + name))
            ds = [es.enter_context(nc.semaphore("dsem_%s_%d" % (name, i))) for i in range(nd)]
            self.E[name] = Eng(name, h, sem, ds)
        self.uid = 0

    def sb(self, stack, shape, dt, name=None):
        self.uid += 1
        name = (name or "t") + "_%d" % self.uid
        t = stack.enter_context(self.nc.sbuf_tensor(name, list(shape), dt))
        return Buf(name, t)

    def sbpool(self, stack, n, shape, dt, name=None):
        return Pool([self.sb(stack, shape, dt, name) for _ in range(n)])

    def ps(self, stack, shape=(128, 512), dt=F32, name=None):
        self.uid += 1
        name = (name or "ps") + "_%d" % self.uid
        t = stack.enter_context(self.nc.psum_tensor(name, list(shape), dt))
        return Buf(name, t)

    def _deps(self, eng, reads, writes, waw, strict=True):
        deps = {}

        def add(d):
            for k, (sem, val, small) in d.items():
                if sem is eng.sem and (eng.name == "pe" or not (small or (STRICT and strict))):
                    continue
                if k not in deps or deps[k][1] < val:
                    deps[k] = (sem, val)
        for b in reads:
            add(b.w)
        for b in writes:
            if b.r:
                add(b.r)
            else:
                add(b.prev)
                if waw:
                    add(b.w)
        return deps

    def _wait(self, eng, deps):
        for k, (sem, val) in deps.items():
            if eng.seen.get(k, 0) >= val:
                continue
            eng.h.wait_ge(sem, val)
            eng.seen[k] = val

    def _record(self, tok, reads, writes):
        k = id(tok[0])
        for b in reads:
            b.r[k] = tok
        for b in writes:
            if b.r:
                b.w = {k: tok}
                b.prev = b.r
                b.r = {}
            else:
                b.w[k] = tok

    def op(self, engname, fn, reads=(), writes=(), small=False, waw=True, strict=True, signal=True):
        eng = self.E[engname]
        self._wait(eng, self._deps(eng, reads, writes, waw, strict))
        ins = fn(eng.h)
        if signal:
            ins.then_inc(eng.sem, 1)
            eng.n += 1
            self._record((eng.sem, eng.n, small), reads, writes)
        else:
            self._record((eng.sem, eng.n + 1, small), reads, writes)

    def dma(self, q, out, in_, reads=(), writes=(), waw=False):
        eng = self.E[q]
        deps = self._deps(eng, reads, writes, waw)
        i = eng.ndma % NDMA_SEM
        rnd = eng.ndma // NDMA_SEM
        sem = eng.dsems[i]
        if rnd > 0:
            k = id(sem)
            if k not in deps or deps[k][1] < 16 * rnd:
                deps[k] = (sem, 16 * rnd)
        self._wait(eng, deps)
        eng.h.dma_start(out=out, in_=in_).then_inc(sem, 16)
        eng.ndma += 1
        self._record((sem, 16 * (rnd + 1), False), reads, writes)

    def barrier(self):
        toks = {}
        for e in self.E.values():
            if e.n:
                toks[id(e.sem)] = (e.sem, e.n, e)
            for i, s in enumerate(e.dsems):
                cnt = (e.ndma - i + NDMA_SEM - 1) // NDMA_SEM
                if cnt > 0:
                    toks[id(s)] = (s, 16 * cnt, None)
        for e in self.E.values():
            d = {k: (s, v) for k, (s, v, own) in toks.items() if own is not e}
            self._wait(e, d)

    def mm(self, out, lhsT, rhs, start, stop, reads, writes, signal=None):
        if signal is None:
            signal = stop
        self.op("pe", lambda h: h.matmul(out, lhsT, rhs, start=start, stop=stop), reads, writes, signal=signal)

    def act(self, out, in_, func, reads, writes, bias=None, scale=None, small=False):
        kw = {}
        if bias is not None:
            kw["bias"] = bias
        if scale is not None:
            kw["scale"] = scale
        self.op("act", lambda h: h.activation(out, in_, func, **kw), reads, writes, small=small)

    def tt(self, eng, out, in0, in1, op, reads, writes, small=False, strict=True):
        self.op(eng, lambda h: h.tensor_tensor(out, in0, in1, op), reads, writes, small=small, strict=strict)

    def ts(self, eng, out, in0, s1, s2, op0, op1, reads, writes, small=False):
        if s2 is None:
            self.op(eng, lambda h: h.tensor_scalar(out, in0, s1, None, op0), reads, writes, small=small)
        else:
            self.op(eng, lambda h: h.tensor_scalar(out, in0, s1, s2, op0, op1), reads, writes, small=small)

    def stt(self, eng, out, in0, scalar, in1, op0, op1, reads, writes, small=False, strict=True):
        self.op(eng, lambda h: h.scalar_tensor_tensor(out, in0, scalar, in1, op0, op1), reads, writes, small=small,
                strict=strict)

    def copy(self, eng, out, in_, reads, writes, small=False, strict=True):
        if eng == "act":
            self.op("act", lambda h: h.copy(out, in_), reads, writes, small=small, strict=strict)
        else:
            self.op(eng, lambda h: h.tensor_copy(out, in_), reads, writes, small=small, strict=strict)


def kb0_of(j, nblk):
    return min(max(j - 2, 0), nblk - 5)


def variant_of(j, nblk):
    if j == 0:
        return 1
    if j == 1:
        return 2
    if j == nblk - 2:
        return 3
    if j == nblk - 1:
        return 4
    return 0


class Kern:
    def __init__(self, S, NL=2, debug=()):
        self.S, self.NL = S, NL
        self.T = S + L_CTX
        self.debug = set(debug)
        nc = bass.Bass("TRN2", target_bir_lowering=False)
        self.nc = nc
        T = self.T

        def din(name, shape, dt=F32):
            return nc.dram_tensor(name, list(shape), dt, kind="ExternalInput").ap()

        def scr(name, shape, dt=BF16):
            kind = "ExternalOutput" if name in self.debug else "Internal"
            return nc.dram_tensor(name, list(shape), dt, kind=kind).ap()

        self.x = din("x", [S, D])
        self.ctx = din("ctx", [L_CTX, D])
        self.cvec = din("cvec", [128, KC, 2])
        self.w_mod = din("w_mod", [NL, D, 6 * D])
        self.b_mod = din("b_mod_c", [NL, 128, 96])
        self.gvec = din("gvec", [NL, 128, 4, KC])
        self.w_in = din("w_in", [NL, D, IN_COLS])
        self.na_tab = din("na_tab", [NL, NH, 5, 128, 640])
        self.conv_w = din("conv_w_c", [NL, 128, 8, 31])
        self.conv_v = din("conv_v_c", [NL, 128, 3, 8])
        self.lamv = din("lamv", [NL, 128, 4, 64])
        self.dlg = din("dlg_c", [NL, 128, 1])
        self.p_a = din("p_a", [NL, 1024, D])
        self.p_b = din("p_b", [NL, 1024, D])
        self.p_c = din("p_c", [NL, 1024, D])
        self.w_out = din("w_out", [NL, D, D])
        self.w_up = din("w_up", [NL, D, 2 * FFN])
        self.fcw = din("fcw_c", [NL, 128, 88, 3])
        self.fcb = din("fcb_c", [NL, 128, 88])
        self.w_down = din("w_down", [NL, FFN, D])
        self.ropeC = din("ropeC", [128, S])
        self.ropeS = din("ropeS", [128, S])
        self.perm_in = din("perm", [128, 128])
        self.ident_in = din("ident", [128, 128])
        self.out = nc.dram_tensor("out", [S, D], F32, kind="ExternalOutput").ap()
        self.Wi = scr("Wi", [NL, 28, 128, KC, 512])
        self.Pa = scr("Pa", [NL, 4, 128, 8, 512])
        self.Pb = scr("Pb", [NL, 4, 128, 8, 512])
        self.Pc = scr("Pc", [NL, 4, 128, 8, 512])
        self.Wo = scr("Wo", [NL, 4, 128, KC, 512])
        self.Wu = scr("Wu", [NL, 22, 128, KC, 512])
        self.Wd = scr("Wd", [NL, 16, 128, 44, 128])
        self.xT = scr("xT", [D, T], F32)
        self.NQ = scr("NQ", [1024, T])
        self.NK = scr("NK", [1024, T])
        self.NV = scr("NV", [NH, 128, T // 128, 128])
        self.GLU = scr("GLU", [1024, T])
        self.DQ = scr("DQ", [1024, T])
        self.DK = scr("DK", [1024, T])
        self.DV = scr("DV", [NH, 128, T // 128, 128])
        self.G = scr("G", [3 * D, T])
        self.OA = scr("OA", [1024, T])
        self.OB = scr("OB", [1024, T])
        self.OC = scr("OC", [1024, T])
        self.U = scr("U", [2 * FFN, T])

    def build(self, phases=None):
        nc = self.nc
        with ExitStack() as es:
            p = Prog(nc, es)
            self.p = p
            self.ones_bf = p.sb(es, [128, 128], BF16, "ones_bf")
            self.ones_f = p.sb(es, [128, 128], F32, "ones_f")
            self.ident = p.sb(es, [128, 128], F32, "ident")
            self.perm = p.sb(es, [128, 128], BF16, "perm")
            self.cs = p.sb(es, [128, KC, 2], F32, "cs")
            self.modc = p.sb(es, [128, 6, 2, KC], F32, "modc")
            self.lam = p.sb(es, [128, 2], F32, "lam")
            self.dlgs = p.sb(es, [128, 1], F32, "dlgs")
            self.psT = [p.ps(es, shape=(128, 1024)) for _ in range(4)]
            halves = []
            for T in self.psT:
                halves.append(Buf(T.name + "a", T.t[:, 0:512]))
                halves.append(Buf(T.name + "b", T.t[:, 512:1024]))
            self.psum = Pool(halves)
            self.cb_vals = [D * EPS, EPS, 128 * EPS]
            self.cbias = p.sb(es, [128, len(self.cb_vals)], F32, "cbias")
            for i, v in enumerate(self.cb_vals):
                p.op("dve", lambda h, i=i, v=v: h.memset(self.cbias[:, i:i + 1], v), (), [self.cbias], small=True)
            p.op("dve", lambda h: h.memset(self.ones_f[:], 1.0), (), [self.ones_f])
            p.op("dve", lambda h: h.memset(self.ones_bf[:], 1.0), (), [self.ones_bf])
            p.dma("sp", self.ident[:], self.ident_in, writes=[self.ident])
            p.dma("pool", self.perm[:], self.perm_in, writes=[self.perm])
            p.dma("sp", self.cs[:], self.cvec, writes=[self.cs])
            with ExitStack() as st:
                sg = p.sb(st, [128, KC, 2], F32, "sg")
                p.act(sg[:], self.cs[:], AF.Sigmoid, [self.cs], [sg], small=True)
                p.tt("dve", self.cs[:], self.cs[:], sg[:], ALU.mult, [self.cs, sg], [self.cs], small=True)
                p.barrier()
            S, T = self.S, self.T
            lat_tiles = [(t0, 512, 0) for t0 in range(0, S, 512)]
            ctx_tile = (S, L_CTX, 1)
            ph = phases
            if ph is None or "wconv" in ph:
                self.phase_wconv()
            if ph is None or "tin" in ph:
                self.phase_tin()
            for l in range(self.NL):
                last = l == self.NL - 1
                if ph is None or "mod" in ph:
                    self.phase_mod(l)
                if ph is None or "A" in ph:
                    self.phase_A(l, lat_tiles + [ctx_tile], last)
                tl = lat_tiles + ([] if last else [ctx_tile])
                if ph is None or "NA" in ph:
                    self.phase_NA(l, tl)
                if ph is None or "CF" in ph:
                    self.phase_conformer(l, tl)
                if ph is None or "DF" in ph:
                    self.phase_diff(l, tl)
                if ph is None or "MG" in ph:
                    self.phase_merge(l, tl)
                if ph is None or "C1" in ph:
                    self.phase_ffn_up(l, tl)
                if ph is None or "C2" in ph:
                    self.phase_ffn_down(l, tl)
            if ph is None or "tout" in ph:
                self.phase_tout()
            p.barrier()
        return nc

    def phase_wconv(self):
        p = self.p
        for l in range(self.NL):
            def cv(dst, src, kcn, ng, cols=512):
                v = src.rearrange("(kc p) (g c) -> g p kc c", p=128, c=cols)
                for g in range(ng):
                    p.dma("pool", dst[g], v[g])
            cv(self.Wi[l], self.w_in[l], KC, 28)
            cv(self.Pa[l], self.p_a[l], 8, 4)
            cv(self.Pb[l], self.p_b[l], 8, 4)
            cv(self.Pc[l], self.p_c[l], 8, 4)
            cv(self.Wo[l], self.w_out[l], KC, 4)
            cv(self.Wu[l], self.w_up[l], KC, 22)
            cv(self.Wd[l], self.w_down[l], 44, 16, cols=128)
        p.barrier()

    def phase_tin(self):
        p = self.p
        S = self.S
        with ExitStack() as st:
            xin = p.sbpool(st, 8, [128, D], F32, "xin")
            xo = p.sbpool(st, 4, [128, 512], F32, "xo")
            tiles = [(self.x, t0, 512, t0) for t0 in range(0, S, 512)] + [(self.ctx, 0, L_CTX, S)]
            cnt = 0
            for (src, r0, n, c0) in tiles:
                nb = n // 128
                xs = []
                for i in range(nb):
                    b = xin.next()
                    p.dma("sp", b[:], src[r0 + i * 128: r0 + (i + 1) * 128, :], writes=[b])
                    xs.append(b)
                for fc in range(KC):
                    ps = self.psum.next()
                    for i in range(nb):
                        p.op("pe", lambda h, i=i, fc=fc, ps=ps: h.transpose(
                            ps[:, i * 128:(i + 1) * 128], xs[i][:, fc * 128:(fc + 1) * 128], self.ident[:]),
                            [xs[i], self.ident], [ps])
                    o = xo.next()
                    eng = "act" if cnt % 2 == 0 else "dve"
                    cnt += 1
                    p.copy(eng, o[:, :n], ps[:, :n], [ps], [o])
                    p.dma("pool", self.xT[fc * 128:(fc + 1) * 128, c0:c0 + n], o[:, :n], reads=[o])
            p.barrier()

    def phase_tout(self):
        p = self.p
        S = self.S
        xTv = self.xT.rearrange("(kc p) t -> p kc t", p=128)
        with ExitStack() as st:
            xi = p.sbpool(st, 2, [128, KC, 512], F32, "xi")
            xo = p.sbpool(st, 3, [128, D], F32, "xo2")
            cnt = 0
            for t0 in range(0, S, 512):
                b = xi.next()
                p.dma("sp", b[:], xTv[:, :, t0:t0 + 512], writes=[b])
                for i in range(4):
                    o = xo.next()
                    for f4 in range(4):
                        ps = self.psum.next()
                        for k in range(4):
                            fc = f4 * 4 + k
                            p.op("pe", lambda h, ps=ps, k=k, fc=fc, i=i, b=b: h.transpose(
                                ps[:, k * 128:(k + 1) * 128], b[:, fc, i * 128:(i + 1) * 128], self.ident[:]),
                                [b, self.ident], [ps])
                        eng = "act" if cnt % 2 == 0 else "dve"
                        cnt += 1
                        p.copy(eng, o[:, f4 * 512:(f4 + 1) * 512], ps[:], [ps], [o])
                    p.dma("pool", self.out[t0 + i * 128: t0 + (i + 1) * 128, :], o[:], reads=[o])
            p.barrier()

    def phase_mod(self, l):
        p = self.p
        with ExitStack() as st:
            wm = p.sbpool(st, 2, [128, KC, 512], F32, "wm")
            modv = p.sb(st, [128, 96, 2], F32, "modv")
            bm = p.sb(st, [128, 96], F32, "bm")
            gv = p.sb(st, [128, 4, KC], F32, "gv")
            lv = p.sb(st, [128, 4, 64], F32, "lv")
            lt = p.sb(st, [128, 2, 64], F32, "lt")
            ls = p.sb(st, [128, 2], F32, "ls")
            p.dma("sp", bm[:], self.b_mod[l], writes=[bm])
            p.dma("sp", gv[:], self.gvec[l], writes=[gv])
            p.dma("sp", self.dlgs[:], self.dlg[l], writes=[self.dlgs])
            p.dma("sp", lv[:], self.lamv[l], writes=[lv])
            wv = self.w_mod[l].rearrange("(kc p) (g c) -> g p kc c", p=128, c=512)
            ps = self.psum.next()
            for g in range(24):
                w = wm.next()
                p.dma("sp", w[:], wv[g], writes=[w])
                for j in range(4):
                    oc = g * 4 + j
                    for kc in range(KC):
                        p.mm(ps[:, oc * 2:oc * 2 + 2], w[:, kc, j * 128:(j + 1) * 128], self.cs[:, kc, :],
                             kc == 0, kc == KC - 1, [w, self.cs], [ps])
            psv = ps[:, 0:192].rearrange("p (o n) -> p o n", n=2)
            for n in range(2):
                p.tt("dve", modv[:, :, n], psv[:, :, n], bm[:], ALU.add, [ps, bm], [modv], small=True)
            sD = math.sqrt(D)
            mc = self.modc
            for n in range(2):
                m = lambda j: modv[:, j * 16:(j + 1) * 16, n]
                p.stt("dve", mc[:, 0, n, :], m(1), 1.0, gv[:, 0, :], ALU.add, ALU.mult, [modv, gv], [mc], small=True)
                p.ts("dve", mc[:, 0, n, :], mc[:, 0, n, :], sD, None, ALU.mult, None, [mc], [mc], small=True)
                p.copy("dve", mc[:, 1, n, :], m(0), [modv], [mc], small=True)
                p.stt("dve", mc[:, 2, n, :], m(2), sD, gv[:, 1, :], ALU.mult, ALU.mult, [modv, gv], [mc], small=True)
                p.stt("dve", mc[:, 3, n, :], m(4), 1.0, gv[:, 2, :], ALU.add, ALU.mult, [modv, gv], [mc], small=True)
                p.ts("dve", mc[:, 3, n, :], mc[:, 3, n, :], sD, None, ALU.mult, None, [mc], [mc], small=True)
                p.copy("dve", mc[:, 4, n, :], m(3), [modv], [mc], small=True)
                p.stt("dve", mc[:, 5, n, :], m(5), sD, gv[:, 3, :], ALU.mult, ALU.mult, [modv, gv], [mc], small=True)
            lam_init = 0.8 - 0.6 * math.exp(-0.3 * l)
            p.tt("dve", lt[:, 0, :], lv[:, 0, :], lv[:, 1, :], ALU.mult, [lv], [lt], small=True)
            p.tt("dve", lt[:, 1, :], lv[:, 2, :], lv[:, 3, :], ALU.mult, [lv], [lt], small=True)
            p.op("dve", lambda h: h.tensor_reduce(ls[:], lt[:], mybir.AxisListType.X, ALU.add), [lt], [ls], small=True)
            p.act(ls[:], ls[:], AF.Exp, [ls], [ls], small=True)
            p.stt("dve", self.lam[:, 0:1], ls[:, 1:2], -lam_init, ls[:, 0:1], ALU.add, ALU.subtract,
                  [ls], [self.lam], small=True)
            p.ts("dve", self.dlgs[:], self.dlgs[:], (1.0 - lam_init) * math.sqrt(128.0), None, ALU.mult, None,
                 [self.dlgs], [self.dlgs], small=True)
            p.barrier()

    def rsqrt(self, out, in_, c, reads, obuf, small=False):
        p = self.p
        p.act(out, in_, AF.Sqrt, reads + [self.cbias], [obuf], bias=self.cbias[:, self.cbias_idx(c):self.cbias_idx(c) + 1], small=small)
        p.op("dve", lambda h: h.reciprocal(out, out), [obuf], [obuf], small=small)

    def cbias_idx(self, c):
        return self.cb_vals.index(c)

    def norm_mod(self, xt, hT, n, ia, ib, m, sq, rs, tmp):
        p = self.p
        ps = self.psum.next()
        for kc in range(KC):
            q = sq.next()
            p.act(q[:, :n], xt[:, kc, :n], AF.Square, [xt], [q])
            p.mm(ps[:, :n], self.ones_bf[:], q[:, :n], kc == 0, kc == KC - 1, [self.ones_bf, q], [ps], signal=True)
        self.rsqrt(rs[:, :n], ps[:, :n], D * EPS, [ps], rs)
        mc = self.modc
        for kc in range(KC):
            t = tmp.next()
            p.stt("dve", t[:, :n], xt[:, kc, :n], mc[:, ia, m, kc:kc + 1], rs[:, :n], ALU.mult, ALU.mult,
                  [xt, mc, rs], [t])
            p.act(hT[:, kc, :n], t[:, :n], AF.Identity, [t, mc], [hT], bias=mc[:, ib, m, kc:kc + 1])

    def phase_A(self, l, tiles, last):
        p = self.p
        S = self.S
        xTv = self.xT.rearrange("(kc p) t -> p kc t", p=128)
        with ExitStack() as st:
            xpool = p.sbpool(st, 2, [128, KC, 512], F32, "xa")
            sq = p.sbpool(st, 2, [128, 512], BF16, "sq")
            rs = p.sb(st, [128, 512], F32, "rs")
            tmp = p.sbpool(st, 2, [128, 512], F32, "tmpa")
            hpool = p.sbpool(st, 2, [128, KC, 512], BF16, "hT")
            wpool = p.sbpool(st, 3, [128, KC, 512], BF16, "wa")
            abuf = p.sb(st, [128, 8, 512], F32, "abuf")
            stg = p.sbpool(st, 4, [128, 512], BF16, "stg")
            f32t = p.sbpool(st, 3, [128, 512], F32, "f32t")
            rc = p.sbpool(st, 2, [128, 512], F32, "rc")
            rsn = p.sbpool(st, 2, [128, 512], F32, "rsn")
            cnt = 0
            for (t0, n, m) in tiles:
                xt = xpool.next()
                p.dma("sp", xt[:, :, :n], xTv[:, :, t0:t0 + n], writes=[xt])
                hT = hpool.next()
                self.norm_mod(xt, hT, n, 0, 1, m, sq, rs, tmp)
                if not m:
                    rcb, rsb = rc.next(), rsn.next()
                    p.dma("sp", rcb[:, :n], self.ropeC[:, t0:t0 + n], writes=[rcb])
                    p.dma("sp", rsb[:, :n], self.ropeS[:, t0:t0 + n], writes=[rsb])
                groups = range(28)
                import os
                if os.environ.get("KGROUPS"):
                    groups = [int(v) for v in os.environ["KGROUPS"].split(",")]
                if m and last:
                    groups = [2, 3, 4, 5, 12, 13, 14, 15]
                for g in groups:
                    w = wpool.next()
                    p.dma("sp", w[:], self.Wi[l, g], writes=[w])
                    if g in (4, 5, 14, 15):
                        dst = self.NV if g < 6 else self.DV
                        hh = (g - 4) * 4 if g < 6 else (g - 14) * 4
                        for tb in range(n // 128):
                            ps = self.psum.next()
                            for kc in range(KC):
                                p.mm(ps[:, :], hT[:, kc, tb * 128:(tb + 1) * 128], w[:, kc, :], kc == 0, kc == KC - 1,
                                     [hT, w], [ps])
                            o = stg.next()
                            eng = "act" if cnt % 2 == 0 else "dve"
                            cnt += 1
                            p.copy(eng, o[:], ps[:], [ps], [o])
                            blk = (t0 + tb * 128) // 128
                            p.dma("pool", dst[hh:hh + 4, :, blk, :].rearrange("h p c -> p h c"),
                                  o[:].rearrange("p (h c) -> p h c", c=128), reads=[o])
                        continue
                    for j in range(4):
                        c = g * 4 + j
                        ps = self.psum.next()
                        for kc in range(KC):
                            p.mm(ps[:, :n], w[:, kc, j * 128:(j + 1) * 128], hT[:, kc, :n], kc == 0, kc == KC - 1,
                                 [w, hT], [ps])
                        if c < 16:
                            dst = self.NQ if c < 8 else self.NK
                            o = stg.next()
                            eng = "act" if cnt % 2 == 0 else "dve"
                            cnt += 1
                            p.copy(eng, o[:, :n], ps[:, :n], [ps], [o])
                            r = (c % 8) * 128
                            p.dma("pool", dst[r:r + 128, t0:t0 + n], o[:, :n], reads=[o])
                        elif c < 32:
                            p.copy("act", abuf[:, c - 24, :n], ps[:, :n], [ps], [abuf])
                        elif c < 40:
                            f = f32t.next()
                            p.act(f[:, :n], ps[:, :n], AF.Sigmoid, [ps], [f])
                            o = stg.next()
                            p.tt("dve", o[:, :n], abuf[:, c - 32, :n], f[:, :n], ALU.mult, [abuf, f], [o])
                            r = (c - 32) * 128
                            p.dma("pool", self.GLU[r:r + 128, t0:t0 + n], o[:, :n], reads=[o])
                        elif c < 56:
                            dst = self.DQ if c < 48 else self.DK
                            r = (c % 8) * 128
                            xb = stg.next()
                            p.copy("act", xb[:, :n], ps[:, :n], [ps], [xb])
                            if m:
                                p.dma("pool", dst[r:r + 128, t0:t0 + n], xb[:, :n], reads=[xb])
                            else:
                                ps2 = self.psum.next()
                                p.mm(ps2[:, :n], self.perm[:], xb[:, :n], True, True, [self.perm, xb], [ps2])
                                f1 = f32t.next()
                                p.tt("dve", f1[:, :n], ps2[:, :n], rsb[:, :n], ALU.mult, [ps2, rsb], [f1])
                                f2 = f32t.next()
                                p.tt("dve", f2[:, :n], xb[:, :n], rcb[:, :n], ALU.mult, [xb, rcb], [f2])
                                o = stg.next()
                                p.tt("dve", o[:, :n], f1[:, :n], f2[:, :n], ALU.add, [f1, f2], [o])
                                p.dma("pool", dst[r:r + 128, t0:t0 + n], o[:, :n], reads=[o])
                        else:
                            o = stg.next()
                            p.act(o[:, :n], ps[:, :n], AF.Sigmoid, [ps], [o])
                            r = (c - 64) * 128
                            p.dma("pool", self.G[r:r + 128, t0:t0 + n], o[:, :n], reads=[o])
            p.barrier()

    def phase_NA(self, l, tiles):
        p = self.p
        S, T = self.S, self.T
        nblk = S // 128
        scale = 128 ** -0.5
        NKv = self.NK.rearrange("(h p) t -> p h t", p=128)
        NQv = self.NQ.rearrange("(h p) t -> p h t", p=128)
        OAv = self.OA.rearrange("(h p) t -> p h t", p=128)
        with ExitStack() as st:
            kc_sb = p.sb(st, [128, NH, 256], BF16, "kcs")
            vc_sb = p.sb(st, [128, NH, 2, 128], BF16, "vcs")
            p.dma("sp", kc_sb[:], NKv[:, :, S:T], writes=[kc_sb])
            p.dma("sp", vc_sb[:], self.NV[:, :, nblk:nblk + 2, :].rearrange("h p b c -> p h b c"), writes=[vc_sb])
            kpool = p.sbpool(st, 2, [128, NH, 1024], BF16, "nak")
            vpool = p.sbpool(st, 2, [128, NH, 8, 128], BF16, "nav")
            qpool = p.sbpool(st, 2, [128, NH, 512], BF16, "naq")
            tabp = p.sbpool(st, 3, [128, 640], F32, "tab")
            sbp = p.sbpool(st, 2, [128, 896], F32, "nsb")
            ep = p.sbpool(st, 2, [128, 896], BF16, "nae")
            rzp = p.sbpool(st, 2, [128, 128], F32, "rz")
            oap = p.sbpool(st, 2, [128, NH, 512], BF16, "oat")
            for (t0, n, m) in tiles:
                qt = qpool.next()
                p.dma("sp", qt[:, :, :n], NQv[:, :, t0:t0 + n], writes=[qt])
                oa = oap.next()
                if not m:
                    j0 = t0 // 128
                    lo = kb0_of(j0, nblk)
                    hi = kb0_of(j0 + 3, nblk) + 5
                    nb = hi - lo
                    kt = kpool.next()
                    p.dma("sp", kt[:, :, :nb * 128], NKv[:, :, lo * 128:hi * 128], writes=[kt])
                    vt = vpool.next()
                    p.dma("sp", vt[:, :, :nb, :], self.NV[:, :, lo:hi, :].rearrange("h p b c -> p h b c"), writes=[vt])
                for h in range(NH):
                    for jj in range(n // 128):
                        q_ap = qt[:, h, jj * 128:(jj + 1) * 128]
                        e = ep.next()
                        if not m:
                            j = j0 + jj
                            kb = kb0_of(j, nblk) - lo
                            tb = tabp.next()
                            p.dma("sp", tb[:], self.na_tab[l, h, variant_of(j, nblk)], writes=[tb])
                            psA = self.psum.next()
                            psB = self.psum.next()
                            for i in range(5):
                                dst = psA[:, i * 128:(i + 1) * 128] if i < 4 else psB[:, 0:128]
                                p.mm(dst, kt[:, h, (kb + i) * 128:(kb + i + 1) * 128], q_ap, True, True,
                                     [kt, qt], [psA if i < 4 else psB])
                            for i in range(2):
                                p.mm(psB[:, (1 + i) * 128:(2 + i) * 128], kc_sb[:, h, i * 128:(i + 1) * 128], q_ap,
                                     True, True, [kc_sb, qt], [psB])
                            sb = sbp.next()
                            p.stt("dve", sb[:, 0:512], psA[:, :], scale, tb[:, 0:512], ALU.mult, ALU.add, [psA, tb], [sb])
                            p.stt("dve", sb[:, 512:640], psB[:, 0:128], scale, tb[:, 512:640], ALU.mult, ALU.add,
                                  [psB, tb], [sb])
                            p.ts("dve", sb[:, 640:896], psB[:, 128:384], scale, None, ALU.mult, None, [psB], [sb])
                            p.act(e[:, 0:896], sb[:], AF.Exp, [sb], [e])
                            nkb = 7
                            lhs_v = lambda i: vt[:, h, kb + i, :] if i < 5 else vc_sb[:, h, i - 5, :]
                            vreads = [vt, vc_sb]
                        else:
                            psB = self.psum.next()
                            for i in range(2):
                                p.mm(psB[:, i * 128:(i + 1) * 128], kc_sb[:, h, i * 128:(i + 1) * 128], q_ap,
                                     True, True, [kc_sb, qt], [psB])
                            p.act(e[:, 0:256], psB[:, 0:256], AF.Exp, [psB], [e], scale=scale)
                            nkb = 2
                            lhs_v = lambda i: vc_sb[:, h, i, :]
                            vreads = [vc_sb]
                        psO = self.psum.next()
                        psZ = self.psum.next()
                        for i in range(nkb):
                            p.mm(psO[:, 0:128], lhs_v(i), e[:, i * 128:(i + 1) * 128], i == 0, i == nkb - 1,
                                 vreads + [e], [psO])
                            p.mm(psZ[:, 0:128], self.ones_bf[:], e[:, i * 128:(i + 1) * 128], i == 0, i == nkb - 1,
                                 [self.ones_bf, e], [psZ])
                        rz = rzp.next()
                        p.op("dve", lambda hh: hh.reciprocal(rz[:], psZ[:, 0:128]), [psZ], [rz])
                        p.tt("dve", oa[:, h, jj * 128:(jj + 1) * 128], psO[:, 0:128], rz[:], ALU.mult, [psO, rz], [oa])
                p.dma("pool", OAv[:, :, t0:t0 + n], oa[:, :, :n], reads=[oa])
            p.barrier()

    def phase_conformer(self, l, tiles):
        p = self.p
        S, T = self.S, self.T
        OBv = self.OB.rearrange("(c p) t -> p c t", p=128)
        with ExitStack() as st:
            cw = p.sb(st, [128, 8, 31], F32, "cw")
            cvv = p.sb(st, [128, 3, 8], F32, "cvv")
            p.dma("sp", cw[:], self.conv_w[l], writes=[cw])
            p.dma("sp", cvv[:], self.conv_v[l], writes=[cvv])
            glp = p.sbpool(st, 5, [128, 544], BF16, "gl")
            cps = Pool(self.psum.bufs[0:6])
            identb = p.sb(st, [128, 128], BF16, "identb")
            p.copy("dve", identb[:], self.ident[:], [self.ident], [identb])
            dg = p.sb(st, [128, 8, 31, 128], BF16, "dg")
            for i in range(8):
                for j in range(31):
                    p.ts("dve", dg[:, i, j, :], identb[:], cw[:, i, j:j + 1], None, ALU.mult, None, [identb, cw], [dg])
            acc = [p.sb(st, [128, 512], F32, "cacc") for _ in range(8)]
            sqp = p.sbpool(st, 2, [128, 512], F32, "csq")
            mu = p.sb(st, [128, 512], F32, "mu")
            var = p.sb(st, [128, 512], F32, "var")
            rstd = p.sb(st, [128, 512], F32, "rstd")
            tp = p.sbpool(st, 3, [128, 512], F32, "ct")
            obp = p.sbpool(st, 2, [128, 8, 512], BF16, "obt")
            for (t0, n, m) in tiles:
                seq_lo, seq_hi = (S, T) if m else (0, S)
                lo = max(t0 - 15, seq_lo)
                hi = min(t0 + n + 15, seq_hi)
                off = lo - (t0 - 15)
                psM = self.psum.bufs[6]
                psQ = self.psum.bufs[7]
                for i in range(8):
                    gl = glp.next()
                    if off > 0:
                        p.op("dve", lambda hh: hh.memset(gl[:, 0:off], 0.0), (), [gl], small=True)
                    if off + hi - lo < n + 30:
                        p.op("dve", lambda hh: hh.memset(gl[:, off + hi - lo:n + 30], 0.0), (), [gl], small=True)
                    p.dma("sp", gl[:, off:off + hi - lo], self.GLU[i * 128:(i + 1) * 128, lo:hi], writes=[gl])
                    a = acc[i]
                    psc = cps.next()
                    for j in range(31):
                        p.mm(psc[:, :n], dg[:, i, j, :], gl[:, j:j + n], j == 0, j == 30, [dg, gl], [psc])
                    p.act(a[:, :n], psc[:, :n], AF.Identity, [psc, cvv], [a], bias=cvv[:, 0, i:i + 1])
                    sq = sqp.next()
                    p.act(sq[:, :n], a[:, :n], AF.Square, [a], [sq])
                    p.mm(psM[:, :n], self.ones_f[:], a[:, :n], i == 0, i == 7, [self.ones_f, a], [psM], signal=True)
                    p.mm(psQ[:, :n], self.ones_f[:], sq[:, :n], i == 0, i == 7, [self.ones_f, sq], [psQ], signal=True)
                p.ts("dve", mu[:, :n], psM[:, :n], 1.0 / 1024, None, ALU.mult, None, [psM], [mu])
                p.tt("dve", var[:, :n], mu[:, :n], mu[:, :n], ALU.mult, [mu], [var])
                p.stt("dve", var[:, :n], psQ[:, :n], 1.0 / 1024, var[:, :n], ALU.mult, ALU.subtract, [psQ, var], [var])
                self.rsqrt(rstd[:, :n], var[:, :n], EPS, [var], rstd)
                ob = obp.next()
                for i in range(8):
                    t = tp.next()
                    p.tt("dve", t[:, :n], acc[i][:, :n], mu[:, :n], ALU.subtract, [acc[i], mu], [t])
                    p.tt("dve", t[:, :n], t[:, :n], rstd[:, :n], ALU.mult, [t, rstd], [t])
                    p.act(ob[:, i, :n], t[:, :n], AF.Silu, [t, cvv], [ob], scale=cvv[:, 1, i:i + 1], bias=cvv[:, 2, i:i + 1])
                p.dma("pool", OBv[:, :, t0:t0 + n], ob[:, :, :n], reads=[ob])
            p.barrier()

    def phase_diff(self, l, tiles):
        p = self.p
        S, T = self.S, self.T
        scale = 64 ** -0.5
        DQv = self.DQ.rearrange("(h p) t -> p h t", p=128)
        OCv = self.OC.rearrange("(h p) t -> p h t", p=128)
        with ExitStack() as st:
            qpool = p.sbpool(st, 2, [128, NH, 512], BF16, "dq")
            kpool = p.sbpool(st, 2, [128, T], BF16, "dk")
            vpool = p.sbpool(st, 2, [128, T // 128, 128], BF16, "dv")
            ocp = p.sbpool(st, 2, [128, NH, 512], BF16, "oct")
            fp = p.sbpool(st, 6, [128, 512], F32, "df")
            acc = self.psum.bufs[0:2]
            pairs = [(self.psT[i], self.psum.bufs[2 * i], self.psum.bufs[2 * i + 1]) for i in (1, 2, 3)]
            pi = 0
            z = p.sb(st, [128, 2, 512], F32, "z")
            ep = p.sbpool(st, 6, [128, 2, 512], BF16, "e12")
            esp = p.sbpool(st, 2, [128, 2, 512], BF16, "esum")
            for (t0, n, m) in tiles:
                qt = qpool.next()
                p.dma("sp", qt[:, :, :n], DQv[:, :, t0:t0 + n], writes=[qt])
                kts = list(range(S // 128, T // 128)) if m else list(range(T // 128))
                oc = ocp.next()
                for h in range(NH):
                    kb = kpool.next()
                    p.dma("sp", kb[:], self.DK[h * 128:(h + 1) * 128, :], writes=[kb])
                    vb = vpool.next()
                    p.dma("sp", vb[:], self.DV[h], writes=[vb])
                    O1, O2 = acc

                    pend = []

                    def pv(e, kt, first, lastk):
                        p.mm(O1[:, :n], vb[:, kt, :], e[:, 0, :n], first, lastk, [vb, e], [O1], signal=False)
                        p.mm(O2[:, :n], vb[:, kt, :], e[:, 1, :n], first, lastk, [vb, e], [O2], signal=True)
                        pend.append((e, first))
                        if len(pend) == 2:
                            (ea, fa), (eb, _) = pend
                            del pend[:]
                            es = esp.next()
                            p.tt("dve", es[:, :, :n], ea[:, :, :n], eb[:, :, :n], ALU.add, [ea, eb], [es])
                            if fa:
                                p.copy("dve", z[:, :, :n], es[:, :, :n], [es], [z])
                            else:
                                p.tt("dve", z[:, :, :n], z[:, :, :n], es[:, :, :n], ALU.add, [z, es], [z], strict=False)
                    prev = None
                    for idx, kt in enumerate(kts):
                        TT, s1, s2 = pairs[pi % 3]
                        pi += 1
                        p.mm(s1[:, :n], kb[0:64, kt * 128:(kt + 1) * 128], qt[0:64, h, :n], True, True, [kb, qt], [s1],
                             signal=False)
                        p.mm(s2[:, :n], kb[64:128, kt * 128:(kt + 1) * 128], qt[64:128, h, :n], True, True, [kb, qt], [s2])
                        e = ep.next()
                        p.act(e[:, :, :n], TT.t[:].rearrange("p (b c) -> p b c", b=2)[:, :, :n], AF.Exp, [s1, s2], [e],
                              scale=scale)
                        if prev is not None:
                            pv(*prev)
                        prev = (e, kt, idx == 0, idx == len(kts) - 1)
                    pv(*prev)
                    z1 = z[:, 0, :]
                    z2 = z[:, 1, :]
                    z1b = z2b = z
                    _, Z1, Z2 = pairs[pi % 3]
                    pi += 1
                    p.mm(Z1[:, :n], self.ones_f[:], z1[:, :n], True, True, [self.ones_f, z], [Z1])
                    p.mm(Z2[:, :n], self.ones_f[:], z2[:, :n], True, True, [self.ones_f, z], [Z2])
                    r1 = fp.next()
                    p.op("dve", lambda hh: hh.reciprocal(r1[:, :n], Z1[:, :n]), [Z1], [r1])
                    t1 = fp.next()
                    p.tt("dve", t1[:, :n], O1[:, :n], r1[:, :n], ALU.mult, [O1, r1], [t1])
                    r2 = fp.next()
                    p.op("dve", lambda hh: hh.reciprocal(r2[:, :n], Z2[:, :n]), [Z2], [r2])
                    t2 = fp.next()
                    p.tt("dve", t2[:, :n], O2[:, :n], r2[:, :n], ALU.mult, [O2, r2], [t2])
                    p.stt("dve", t1[:, :n], t2[:, :n], self.lam[:, 0:1], t1[:, :n], ALU.mult, ALU.add,
                          [t2, self.lam, t1], [t1])
                    p.act(r1[:, :n], t1[:, :n], AF.Square, [t1], [r1])
                    _, psS, _unused = pairs[pi % 3]
                    pi += 1
                    p.mm(psS[:, :n], self.ones_f[:], r1[:, :n], True, True, [self.ones_f, r1], [psS])
                    self.rsqrt(r2[:, :n], psS[:, :n], 128 * EPS, [psS], r2)
                    p.stt("dve", oc[:, h, :n], t1[:, :n], self.dlgs[:, 0:1], r2[:, :n], ALU.mult, ALU.mult,
                          [t1, self.dlgs, r2], [oc])
                p.dma("pool", OCv[:, :, t0:t0 + n], oc[:, :, :n], reads=[oc])
            p.barrier()

    def post_residual(self, mix, pss, rs, n, t0, m, ig, xp, tp):
        p = self.p
        mc = self.modc
        self.rsqrt(rs[:, :n], pss[:, :n], D * EPS, [pss], rs)
        for kc in range(KC):
            xt = xp.next()
            p.dma("sp", xt[:, :n], self.xT[kc * 128:(kc + 1) * 128, t0:t0 + n], writes=[xt])
            t = tp.next()
            p.stt("dve", t[:, :n], mix[:, kc, :n], mc[:, ig, m, kc:kc + 1], rs[:, :n], ALU.mult, ALU.mult,
                  [mix, mc, rs], [t])
            p.tt("dve", xt[:, :n], xt[:, :n], t[:, :n], ALU.add, [xt, t], [xt])
            p.dma("pool", self.xT[kc * 128:(kc + 1) * 128, t0:t0 + n], xt[:, :n], reads=[xt])

    def phase_merge(self, l, tiles):
        p = self.p
        Gv = self.G.rearrange("(br c p) t -> p br c t", p=128, c=16)
        with ExitStack() as st:
            bp = [p.sbpool(st, 1, [128, 8, 512], BF16, "mb%d" % i) for i in range(3)]
            wp = [p.sbpool(st, 2, [128, 8, 512], BF16, "mw%d" % i) for i in range(3)]
            gtp = p.sbpool(st, 3, [128, 3, 512], BF16, "gt")
            tp = p.sbpool(st, 6, [128, 512], F32, "mt")
            y = p.sb(st, [128, KC, 512], BF16, "y")
            wop = p.sbpool(st, 2, [128, KC, 512], BF16, "wo")
            mix = p.sb(st, [128, KC, 512], F32, "mix")
            sq = p.sbpool(st, 2, [128, 512], BF16, "msq")
            rs = p.sb(st, [128, 512], F32, "mrs")
            xp = p.sbpool(st, 3, [128, 512], F32, "mx")
            srcs = [self.OA, self.OB, self.OC]
            Ws = [self.Pa, self.Pb, self.Pc]
            full_psum = self.psum
            pss = full_psum.bufs[7]
            self.psum = Pool(full_psum.bufs[0:7])
            for (t0, n, m) in tiles:
                br = []
                for i in range(3):
                    b = bp[i].next()
                    p.dma("sp", b[:, :, :n], srcs[i].rearrange("(c p) t -> p c t", p=128)[:, :, t0:t0 + n], writes=[b])
                    br.append(b)
                for og in range(4):
                    ws = []
                    for i in range(3):
                        w = wp[i].next()
                        p.dma("sp", w[:], Ws[i][l, og], writes=[w])
                        ws.append(w)
                    for j in range(4):
                        c = og * 4 + j
                        gt = gtp.next()
                        p.dma("sp", gt[:, :, :n], Gv[:, :, c, t0:t0 + n], writes=[gt])
                        ts_ = []
                        for i in range(3):
                            ps = self.psum.next()
                            for kc in range(8):
                                p.mm(ps[:, :n], ws[i][:, kc, j * 128:(j + 1) * 128], br[i][:, kc, :n], kc == 0, kc == 7,
                                     [ws[i], br[i]], [ps])
                            t = tp.next()
                            p.tt("dve", t[:, :n], ps[:, :n], gt[:, i, :n], ALU.mult, [ps, gt], [t])
                            ts_.append(t)
                        p.tt("dve", ts_[0][:, :n], ts_[0][:, :n], ts_[1][:, :n], ALU.add, [ts_[0], ts_[1]], [ts_[0]])
                        p.tt("dve", y[:, c, :n], ts_[0][:, :n], ts_[2][:, :n], ALU.add, [ts_[0], ts_[2]], [y])
                for og in range(4):
                    wo = wop.next()
                    p.dma("sp", wo[:], self.Wo[l, og], writes=[wo])
                    for j in range(4):
                        c = og * 4 + j
                        ps = self.psum.next()
                        for kc in range(KC):
                            p.mm(ps[:, :n], wo[:, kc, j * 128:(j + 1) * 128], y[:, kc, :n], kc == 0, kc == KC - 1,
                                 [wo, y], [ps])
                        p.copy("dve", mix[:, c, :n], ps[:, :n], [ps], [mix])
                        q = sq.next()
                        p.act(q[:, :n], mix[:, c, :n], AF.Square, [mix], [q])
                        p.mm(pss[:, :n], self.ones_bf[:], q[:, :n], c == 0, c == KC - 1, [self.ones_bf, q], [pss], signal=True)
                self.post_residual(mix, pss, rs, n, t0, m, 2, xp, tp)
            self.psum = full_psum
            p.barrier()

    def phase_ffn_up(self, l, tiles):
        p = self.p
        xTv = self.xT.rearrange("(kc p) t -> p kc t", p=128)
        with ExitStack() as st:
            xpool = p.sbpool(st, 2, [128, KC, 512], F32, "xu")
            sq = p.sbpool(st, 2, [128, 512], BF16, "usq")
            rs = p.sb(st, [128, 512], F32, "urs")
            tmp = p.sbpool(st, 2, [128, 512], F32, "utmp")
            hpool = p.sbpool(st, 2, [128, KC, 512], BF16, "uh")
            wpool = p.sbpool(st, 3, [128, KC, 512], BF16, "uw")
            stg = p.sbpool(st, 4, [128, 512], BF16, "ustg")
            cnt = 0
            for (t0, n, m) in tiles:
                xt = xpool.next()
                p.dma("sp", xt[:, :, :n], xTv[:, :, t0:t0 + n], writes=[xt])
                hT = hpool.next()
                self.norm_mod(xt, hT, n, 3, 4, m, sq, rs, tmp)
                for g in range(22):
                    w = wpool.next()
                    p.dma("sp", w[:], self.Wu[l, g], writes=[w])
                    for j in range(4):
                        c = g * 4 + j
                        ps = self.psum.next()
                        for kc in range(KC):
                            p.mm(ps[:, :n], w[:, kc, j * 128:(j + 1) * 128], hT[:, kc, :n], kc == 0, kc == KC - 1,
                                 [w, hT], [ps])
                        o = stg.next()
                        eng = "act" if cnt % 2 == 0 else "dve"
                        cnt += 1
                        p.copy(eng, o[:, :n], ps[:, :n], [ps], [o])
                        p.dma("pool", self.U[c * 128:(c + 1) * 128, t0:t0 + n], o[:, :n], reads=[o])
            p.barrier()

    def phase_ffn_down(self, l, tiles):
        p = self.p
        S, T = self.S, self.T
        with ExitStack() as st:
            fw = p.sb(st, [128, 88, 3], F32, "fw")
            fb = p.sb(st, [128, 88], F32, "fb")
            p.dma("sp", fw[:], self.fcw[l], writes=[fw])
            p.dma("sp", fb[:], self.fcb[l], writes=[fb])
            up = p.sbpool(st, 12, [128, 516], BF16, "fu")
            ca = p.sbpool(st, 9, [128, 512], F32, "fca")
            gT = p.sb(st, [128, 44, 512], BF16, "gT")
            wdp = p.sbpool(st, 3, [128, 44, 128], BF16, "wd")
            mix = p.sb(st, [128, KC, 512], F32, "fmix")
            sq = p.sbpool(st, 2, [128, 512], BF16, "fsq")
            rs = p.sb(st, [128, 512], F32, "frs")
            xp = p.sbpool(st, 3, [128, 512], F32, "fx")
            tp = p.sbpool(st, 3, [128, 512], F32, "ft")
            full_psum = self.psum
            pss = full_psum.bufs[7]
            self.psum = Pool(full_psum.bufs[0:7])
            for (t0, n, m) in tiles:
                seq_lo, seq_hi = (S, T) if m else (0, S)
                lo = max(t0 - 1, seq_lo)
                hi = min(t0 + n + 1, seq_hi)
                off = lo - (t0 - 1)
                for i in range(44):
                    res = []
                    for half in range(2):
                        ch = half * 44 + i
                        u = up.next()
                        if off > 0:
                            p.op("dve", lambda hh: hh.memset(u[:, 0:off], 0.0), (), [u], small=True)
                        if off + hi - lo < n + 2:
                            p.op("dve", lambda hh: hh.memset(u[:, off + hi - lo:n + 2], 0.0), (), [u], small=True)
                        p.dma("sp", u[:, off:off + hi - lo], self.U[ch * 128:(ch + 1) * 128, lo:hi], writes=[u])
                        a = ca.next()
                        p.act(a[:, :n], u[:, 1:n + 1], AF.Identity, [u, fw, fb], [a], scale=fw[:, ch, 1:2], bias=fb[:, ch:ch + 1])
                        p.stt("dve", a[:, :n], u[:, 0:n], fw[:, ch, 0:1], a[:, :n], ALU.mult, ALU.add, [u, fw, a], [a])
                        p.stt("dve", a[:, :n], u[:, 2:n + 2], fw[:, ch, 2:3], a[:, :n], ALU.mult, ALU.add, [u, fw, a], [a],
                              strict=False)
                        res.append(a)
                    s = ca.next()
                    p.act(s[:, :n], res[0][:, :n], AF.Silu, [res[0]], [s])
                    p.tt("dve", gT[:, i, :n], s[:, :n], res[1][:, :n], ALU.mult, [s, res[1]], [gT])
                for c in range(16):
                    wd = wdp.next()
                    p.dma("sp", wd[:], self.Wd[l, c], writes=[wd])
                    ps = self.psum.next()
                    for kc in range(44):
                        p.mm(ps[:, :n], wd[:, kc, :], gT[:, kc, :n], kc == 0, kc == 43, [wd, gT], [ps])
                    p.copy("dve", mix[:, c, :n], ps[:, :n], [ps], [mix])
                    q = sq.next()
                    p.act(q[:, :n], mix[:, c, :n], AF.Square, [mix], [q])
                    p.mm(pss[:, :n], self.ones_bf[:], q[:, :n], c == 0, c == KC - 1, [self.ones_bf, q], [pss], signal=True)
                self.post_residual(mix, pss, rs, n, t0, m, 5, xp, tp)
            self.psum = full_psum
            p.barrier()


def _col(v, nch):
    v = np.asarray(v, np.float32)
    return np.ascontiguousarray(np.swapaxes(v.reshape(v.shape[:-1] + (nch, 128)), -1, -2))


def _na_tables(na_rpb, nrows):
    NL = na_rpb.shape[0]
    nblk = nrows // 2
    wr, wc = 8, 16
    col = np.arange(GRID_W)
    c0 = np.clip(col - wc // 2, 0, GRID_W - wc)
    rep = {0: min(2, nblk - 3), 1: 0, 2: 1, 3: nblk - 2, 4: nblk - 1}
    tab = np.full((NL, NH, 5, 128, 640), NEG, np.float32)
    for v, j in rep.items():
        kb0 = kb0_of(j, nblk)
        for qi in range(128):
            r = 2 * j + qi // 64
            c = qi % 64
            r0 = min(max(r - wr // 2, 0), nrows - wr)
            for i in range(5):
                for kr in range(2):
                    rr = (kb0 + i) * 2 + kr
                    if rr < r0 or rr >= r0 + wr:
                        continue
                    cc = np.arange(c0[c], c0[c] + wc)
                    tab[:, :, v, kr * 64 + cc, i * 128 + qi] = na_rpb[:, :, rr - r + 7, :][:, :, cc - c + 15]
    return tab


def _rope_tables(S):
    t = np.arange(S)
    rows, cols = t // GRID_W, t % GRID_W
    inv = (10000.0 ** (-np.arange(0, 32, 2, dtype=np.float32) / 32)).astype(np.float32)
    C = np.zeros((128, S), np.float32)
    Sn = np.zeros((128, S), np.float32)
    perm = np.zeros((128, 128), np.float32)
    for pp in range(128):
        sub = pp % 64
        part = sub // 32
        i = sub % 16
        pos = (rows if part == 0 else cols).astype(np.float32)
        ang = pos * inv[i]
        C[pp] = np.cos(ang)
        Sn[pp] = np.sin(ang)
        first = (sub % 32) < 16
        partner = pp + 16 if first else pp - 16
        perm[partner, pp] = -1.0 if first else 1.0
    return C, Sn, perm


def prep_shared(inp, S):
    NL = inp["w_mod"].shape[0]
    f = lambda k: np.ascontiguousarray(np.asarray(inp[k], np.float32))
    C, Sn, perm = _rope_tables(S)
    sh = {
        "w_mod": f("w_mod"), "w_in": f("w_in"), "p_a": f("p_a"), "p_b": f("p_b"), "p_c": f("p_c"),
        "w_out": f("w_out"), "w_up": f("w_up"), "w_down": f("w_down"),
        "b_mod_c": _col(inp["b_mod"], 96),
        "gvec": np.ascontiguousarray(np.stack([_col(inp[k], KC) for k in
                                               ("g_pre_mix", "g_post_mix", "g_pre_ffn", "g_post_ffn")], axis=2)),
        "na_tab": _na_tables(np.asarray(inp["na_rpb"], np.float32), S // GRID_W),
        "conv_w_c": np.ascontiguousarray(np.asarray(inp["conv_w"], np.float32).reshape(NL, 31, 8, 128).transpose(0, 3, 2, 1)),
        "conv_v_c": np.ascontiguousarray(np.stack([_col(inp[k], 8) for k in ("conv_b", "conv_ln_g", "conv_ln_b")], axis=2)),
        "lamv": np.ascontiguousarray(np.broadcast_to(
            np.stack([np.asarray(inp[k], np.float32) for k in ("lam_q1", "lam_k1", "lam_q2", "lam_k2")], axis=1)[:, None],
            (NL, 128, 4, 64))),
        "dlg_c": np.ascontiguousarray(np.asarray(inp["diff_ln_g"], np.float32).reshape(NL, 128, 1)),
        "fcw_c": np.ascontiguousarray(np.asarray(inp["ffn_conv_w"], np.float32).reshape(NL, 3, 88, 128).transpose(0, 3, 2, 1)),
        "fcb_c": _col(inp["ffn_conv_b"], 88),
        "ropeC": C, "ropeS": Sn, "perm": perm, "ident": np.eye(128, dtype=np.float32),
    }
    return sh


def prep_core(inp, b):
    cv = np.stack([_col(np.asarray(inp["c"], np.float32)[b], KC), _col(np.asarray(inp["c_ctx"], np.float32), KC)], axis=2)
    return {"x": np.ascontiguousarray(np.asarray(inp["x"], np.float32)[b]),
            "ctx": np.ascontiguousarray(np.asarray(inp["ctx"], np.float32)[b]),
            "cvec": np.ascontiguousarray(cv)}


def kernel(**inputs):
    B, S, _ = inputs["x"].shape
    kern = Kern(S)
    nc = kern.build()
    sh = prep_shared(inputs, S)
    in_maps = [dict(sh, **prep_core(inputs, b % B)) for b in range(8)]
    res = run_bass_kernel_spmd(nc, in_maps, core_ids=list(range(8)))
    return np.stack([np.asarray(res.results[b]["out"], np.float32) for b in range(B)], axis=0)
```

```python
import math
from contextlib import ExitStack
import numpy as np
import concourse.bass as bass
import concourse.mybir as mybir
from concourse.bass_utils import run_bass_kernel_spmd

F32 = mybir.dt.float32
BF16 = mybir.dt.bfloat16
AF = mybir.ActivationFunctionType
ALU = mybir.AluOpType

D = 2048
KC = 16
L_CTX = 256
GRID_W = 64
NH = 8
IN_COLS = 14336
FFN = 5632
EPS = 1e-6
NDMA_SEM = 16
NEG = -30000.0
STRICT = True


class Eng:
    def __init__(self, name, h, sem, dsems):
        self.name, self.h, self.sem, self.n = name, h, sem, 0
        self.seen = {}
        self.dsems = dsems
        self.ndma = 0


class Buf:
    __slots__ = ("name", "ap", "w", "r", "t", "prev")

    def __init__(self, name, t=None):
        self.name = name
        self.t = t
        self.ap = t
        self.w = {}
        self.r = {}
        self.prev = {}

    def __getitem__(self, idx):
        return self.t[idx]


class Pool:
    def __init__(self, bufs):
        self.bufs = bufs
        self.i = 0

    def next(self):
        b = self.bufs[self.i % len(self.bufs)]
        self.i += 1
        return b


class Prog:
    def __init__(self, nc, es):
        self.nc = nc
        self.es = es
        self.E = {}
        for name, h, nd in (("pe", nc.tensor, 0), ("act", nc.scalar, 0), ("dve", nc.vector, 0),
                            ("pool", nc.gpsimd, NDMA_SEM), ("sp", nc.sync, NDMA_SEM)):
            sem = es.enter_context(nc.semaphore("sem_" + name))
            ds = [es.enter_context(nc.semaphore("dsem_%s_%d" % (name, i))) for i in range(nd)]
            self.E[name] = Eng(name, h, sem, ds)
        self.uid = 0

    def sb(self, stack, shape, dt, name=None):
        self.uid += 1
        name = (name or "t") + "_%d" % self.uid
        t = stack.enter_context(self.nc.sbuf_tensor(name, list(shape), dt))
        return Buf(name, t)

    def sbpool(self, stack, n, shape, dt, name=None):
        return Pool([self.sb(stack, shape, dt, name) for _ in range(n)])

    def ps(self, stack, shape=(128, 512), dt=F32, name=None):
        self.uid += 1
        name = (name or "ps") + "_%d" % self.uid
        t = stack.enter_context(self.nc.psum_tensor(name, list(shape), dt))
        return Buf(name, t)

    def _deps(self, eng, reads, writes, waw, strict=True):
        deps = {}

        def add(d):
            for k, (sem, val, small) in d.items():
                if sem is eng.sem and (eng.name == "pe" or not (small or (STRICT and strict))):
                    continue
                if k not in deps or deps[k][1] < val:
                    deps[k] = (sem, val)
        for b in reads:
            add(b.w)
        for b in writes:
            if b.r:
                add(b.r)
            else:
                add(b.prev)
                if waw:
                    add(b.w)
        return deps

    def _wait(self, eng, deps):
        for k, (sem, val) in deps.items():
            if eng.seen.get(k, 0) >= val:
                continue
            eng.h.wait_ge(sem, val)
            eng.seen[k] = val

    def _record(self, tok, reads, writes):
        k = id(tok[0])
        for b in reads:
            b.r[k] = tok
        for b in writes:
            if b.r:
                b.w = {k: tok}
                b.prev = b.r
                b.r = {}
            else:
                b.w[k] = tok

    def op(self, engname, fn, reads=(), writes=(), small=False, waw=True, strict=True, signal=True):
        eng = self.E[engname]
        self._wait(eng, self._deps(eng, reads, writes, waw, strict))
        ins = fn(eng.h)
        if signal:
            ins.then_inc(eng.sem, 1)
            eng.n += 1
            self._record((eng.sem, eng.n, small), reads, writes)
        else:
            self._record((eng.sem, eng.n + 1, small), reads, writes)

    def dma(self, q, out, in_, reads=(), writes=(), waw=False):
        eng = self.E[q]
        deps = self._deps(eng, reads, writes, waw)
        i = eng.ndma % NDMA_SEM
        rnd = eng.ndma // NDMA_SEM
        sem = eng.dsems[i]
        if rnd > 0:
            k = id(sem)
            if k not in deps or deps[k][1] < 16 * rnd:
                deps[k] = (sem, 16 * rnd)
        self._wait(eng, deps)
        eng.h.dma_start(out=out, in_=in_).then_inc(sem, 16)
        eng.ndma += 1
        self._record((sem, 16 * (rnd + 1), False), reads, writes)

    def barrier(self):
        toks = {}
        for e in self.E.values():
            if e.n:
                toks[id(e.sem)] = (e.sem, e.n, e)
            for i, s in enumerate(e.dsems):
                cnt = (e.ndma - i + NDMA_SEM - 1) // NDMA_SEM
                if cnt > 0:
                    toks[id(s)] = (s, 16 * cnt, None)
        for e in self.E.values():
            d = {k: (s, v) for k, (s, v, own) in toks.items() if own is not e}
            self._wait(e, d)

    def mm(self, out, lhsT, rhs, start, stop, reads, writes, signal=None):
        if signal is None:
            signal = stop
        self.op("pe", lambda h: h.matmul(out, lhsT, rhs, start=start, stop=stop), reads, writes, signal=signal)

    def act(self, out, in_, func, reads, writes, bias=None, scale=None, small=False):
        kw = {}
        if bias is not None:
            kw["bias"] = bias
        if scale is not None:
            kw["scale"] = scale
        self.op("act", lambda h: h.activation(out, in_, func, **kw), reads, writes, small=small)

    def tt(self, eng, out, in0, in1, op, reads, writes, small=False, strict=True):
        self.op(eng, lambda h: h.tensor_tensor(out, in0, in1, op), reads, writes, small=small, strict=strict)

    def ts(self, eng, out, in0, s1, s2, op0, op1, reads, writes, small=False):
        if s2 is None:
            self.op(eng, lambda h: h.tensor_scalar(out, in0, s1, None, op0), reads, writes, small=small)
        else:
            self.op(eng, lambda h: h.tensor_scalar(out, in0, s1, s2, op0, op1), reads, writes, small=small)

    def stt(self, eng, out, in0, scalar, in1, op0, op1, reads, writes, small=False, strict=True):
        self.op(eng, lambda h: h.scalar_tensor_tensor(out, in0, scalar, in1, op0, op1), reads, writes, small=small,
                strict=strict)

    def copy(self, eng, out, in_, reads, writes, small=False, strict=True):
        if eng == "act":
            self.op("act", lambda h: h.copy(out, in_), reads, writes, small=small, strict=strict)
        else:
            self.op(eng, lambda h: h.tensor_copy(out, in_), reads, writes, small=small, strict=strict)


def kb0_of(j, nblk):
    return min(max(j - 2, 0), nblk - 5)


def variant_of(j, nblk):
    if j == 0:
        return 1
    if j == 1:
        return 2
    if j == nblk - 2:
        return 3
    if j == nblk - 1:
        return 4
    return 0


class Kern:
    def __init__(self, S, NL=2, debug=()):
        self.S, self.NL = S, NL
        self.T = S + L_CTX
        self.debug = set(debug)
        nc = bass.Bass("TRN2", target_bir_lowering=False)
        self.nc = nc
        T = self.T

        def din(name, shape, dt=F32):
            return nc.dram_tensor(name, list(shape), dt, kind="ExternalInput").ap()

        def scr(name, shape, dt=BF16):
            kind = "ExternalOutput" if name in self.debug else "Internal"
            return nc.dram_tensor(name, list(shape), dt, kind=kind).ap()

        self.x = din("x", [S, D])
        self.ctx = din("ctx", [L_CTX, D])
        self.cvec = din("cvec", [128, KC, 2])
        self.w_mod = din("w_mod", [NL, D, 6 * D])
        self.b_mod = din("b_mod_c", [NL, 128, 96])
        self.gvec = din("gvec", [NL, 128, 4, KC])
        self.w_in = din("w_in", [NL, D, IN_COLS])
        self.na_tab = din("na_tab", [NL, NH, 5, 128, 640])
        self.conv_w = din("conv_w_c", [NL, 128, 8, 31])
        self.conv_v = din("conv_v_c", [NL, 128, 3, 8])
        self.lamv = din("lamv", [NL, 128, 4, 64])
        self.dlg = din("dlg_c", [NL, 128, 1])
        self.p_a = din("p_a", [NL, 1024, D])
        self.p_b = din("p_b", [NL, 1024, D])
        self.p_c = din("p_c", [NL, 1024, D])
        self.w_out = din("w_out", [NL, D, D])
        self.w_up = din("w_up", [NL, D, 2 * FFN])
        self.fcw = din("fcw_c", [NL, 128, 88, 3])
        self.fcb = din("fcb_c", [NL, 128, 88])
        self.w_down = din("w_down", [NL, FFN, D])
        self.ropeC = din("ropeC", [128, S])
        self.ropeS = din("ropeS", [128, S])
        self.perm_in = din("perm", [128, 128])
        self.ident_in = din("ident", [128, 128])
        self.out = nc.dram_tensor("out", [S, D], F32, kind="ExternalOutput").ap()
        self.Wi = scr("Wi", [NL, 28, 128, KC, 512])
        self.Pa = scr("Pa", [NL, 4, 128, 8, 512])
        self.Pb = scr("Pb", [NL, 4, 128, 8, 512])
        self.Pc = scr("Pc", [NL, 4, 128, 8, 512])
        self.Wo = scr("Wo", [NL, 4, 128, KC, 512])
        self.Wu = scr("Wu", [NL, 22, 128, KC, 512])
        self.Wd = scr("Wd", [NL, 16, 128, 44, 128])
        self.xT = scr("xT", [D, T], F32)
        self.NQ = scr("NQ", [1024, T])
        self.NK = scr("NK", [1024, T])
        self.NV = scr("NV", [NH, 128, T // 128, 128])
        self.GLU = scr("GLU", [1024, T])
        self.DQ = scr("DQ", [1024, T])
        self.DK = scr("DK", [1024, T])
        self.DV = scr("DV", [NH, 128, T // 128, 128])
        self.G = scr("G", [3 * D, T])
        self.OA = scr("OA", [1024, T])
        self.OB = scr("OB", [1024, T])
        self.OC = scr("OC", [1024, T])
        self.U = scr("U", [2 * FFN, T])

    def build(self, phases=None):
        nc = self.nc
        with ExitStack() as es:
            p = Prog(nc, es)
            self.p = p
            self.ones_bf = p.sb(es, [128, 128], BF16, "ones_bf")
            self.ones_f = p.sb(es, [128, 128], F32, "ones_f")
            self.ident = p.sb(es, [128, 128], F32, "ident")
            self.perm = p.sb(es, [128, 128], BF16, "perm")
            self.cs = p.sb(es, [128, KC, 2], F32, "cs")
            self.modc = p.sb(es, [128, 6, 2, KC], F32, "modc")
            self.lam = p.sb(es, [128, 2], F32, "lam")
            self.dlgs = p.sb(es, [128, 1], F32, "dlgs")
            self.psT = [p.ps(es, shape=(128, 1024)) for _ in range(4)]
            halves = []
            for T in self.psT:
                halves.append(Buf(T.name + "a", T.t[:, 0:512]))
                halves.append(Buf(T.name + "b", T.t[:, 512:1024]))
            self.psum = Pool(halves)
            self.cb_vals = [D * EPS, EPS, 128 * EPS]
            self.cbias = p.sb(es, [128, len(self.cb_vals)], F32, "cbias")
            for i, v in enumerate(self.cb_vals):
                p.op("dve", lambda h, i=i, v=v: h.memset(self.cbias[:, i:i + 1], v), (), [self.cbias], small=True)
            p.op("dve", lambda h: h.memset(self.ones_f[:], 1.0), (), [self.ones_f])
            p.op("dve", lambda h: h.memset(self.ones_bf[:], 1.0), (), [self.ones_bf])
            p.dma("sp", self.ident[:], self.ident_in, writes=[self.ident])
            p.dma("pool", self.perm[:], self.perm_in, writes=[self.perm])
            p.dma("sp", self.cs[:], self.cvec, writes=[self.cs])
            with ExitStack() as st:
                sg = p.sb(st, [128, KC, 2], F32, "sg")
                p.act(sg[:], self.cs[:], AF.Sigmoid, [self.cs], [sg], small=True)
                p.tt("dve", self.cs[:], self.cs[:], sg[:], ALU.mult, [self.cs, sg], [self.cs], small=True)
                p.barrier()
            S, T = self.S, self.T
            lat_tiles = [(t0, 512, 0) for t0 in range(0, S, 512)]
            ctx_tile = (S, L_CTX, 1)
            ph = phases
            if ph is None or "wconv" in ph:
                self.phase_wconv()
            if ph is None or "tin" in ph:
                self.phase_tin()
            for l in range(self.NL):
                last = l == self.NL - 1
                if ph is None or "mod" in ph:
                    self.phase_mod(l)
                if ph is None or "A" in ph:
                    self.phase_A(l, lat_tiles + [ctx_tile], last)
                tl = lat_tiles + ([] if last else [ctx_tile])
                if ph is None or "NA" in ph:
                    self.phase_NA(l, tl)
                if ph is None or "CF" in ph:
                    self.phase_conformer(l, tl)
                if ph is None or "DF" in ph:
                    self.phase_diff(l, tl)
                if ph is None or "MG" in ph:
                    self.phase_merge(l, tl)
                if ph is None or "C1" in ph:
                    self.phase_ffn_up(l, tl)
                if ph is None or "C2" in ph:
                    self.phase_ffn_down(l, tl)
            if ph is None or "tout" in ph:
                self.phase_tout()
            p.barrier()
        return nc

    def phase_wconv(self):
        p = self.p
        for l in range(self.NL):
            def cv(dst, src, kcn, ng, cols=512):
                v = src.rearrange("(kc p) (g c) -> g p kc c", p=128, c=cols)
                for g in range(ng):
                    p.dma("pool", dst[g], v[g])
            cv(self.Wi[l], self.w_in[l], KC, 28)
            cv(self.Pa[l], self.p_a[l], 8, 4)
            cv(self.Pb[l], self.p_b[l], 8, 4)
            cv(self.Pc[l], self.p_c[l], 8, 4)
            cv(self.Wo[l], self.w_out[l], KC, 4)
            cv(self.Wu[l], self.w_up[l], KC, 22)
            cv(self.Wd[l], self.w_down[l], 44, 16, cols=128)
        p.barrier()

    def phase_tin(self):
        p = self.p
        S = self.S
        with ExitStack() as st:
            xin = p.sbpool(st, 8, [128, D], F32, "xin")
            xo = p.sbpool(st, 4, [128, 512], F32, "xo")
            tiles = [(self.x, t0, 512, t0) for t0 in range(0, S, 512)] + [(self.ctx, 0, L_CTX, S)]
            cnt = 0
            for (src, r0, n, c0) in tiles:
                nb = n // 128
                xs = []
                for i in range(nb):
                    b = xin.next()
                    p.dma("sp", b[:], src[r0 + i * 128: r0 + (i + 1) * 128, :], writes=[b])
                    xs.append(b)
                for fc in range(KC):
                    ps = self.psum.next()
                    for i in range(nb):
                        p.op("pe", lambda h, i=i, fc=fc, ps=ps: h.transpose(
                            ps[:, i * 128:(i + 1) * 128], xs[i][:, fc * 128:(fc + 1) * 128], self.ident[:]),
                            [xs[i], self.ident], [ps])
                    o = xo.next()
                    eng = "act" if cnt % 2 == 0 else "dve"
                    cnt += 1
                    p.copy(eng, o[:, :n], ps[:, :n], [ps], [o])
                    p.dma("pool", self.xT[fc * 128:(fc + 1) * 128, c0:c0 + n], o[:, :n], reads=[o])
            p.barrier()

    def phase_tout(self):
        p = self.p
        S = self.S
        xTv = self.xT.rearrange("(kc p) t -> p kc t", p=128)
        with ExitStack() as st:
            xi = p.sbpool(st, 2, [128, KC, 512], F32, "xi")
            xo = p.sbpool(st, 3, [128, D], F32, "xo2")
            cnt = 0
            for t0 in range(0, S, 512):
                b = xi.next()
                p.dma("sp", b[:], xTv[:, :, t0:t0 + 512], writes=[b])
                for i in range(4):
                    o = xo.next()
                    for f4 in range(4):
                        ps = self.psum.next()
                        for k in range(4):
                            fc = f4 * 4 + k
                            p.op("pe", lambda h, ps=ps, k=k, fc=fc, i=i, b=b: h.transpose(
                                ps[:, k * 128:(k + 1) * 128], b[:, fc, i * 128:(i + 1) * 128], self.ident[:]),
                                [b, self.ident], [ps])
                        eng = "act" if cnt % 2 == 0 else "dve"
                        cnt += 1
                        p.copy(eng, o[:, f4 * 512:(f4 + 1) * 512], ps[:], [ps], [o])
                    p.dma("pool", self.out[t0 + i * 128: t0 + (i + 1) * 128, :], o[:], reads=[o])
            p.barrier()

    def phase_mod(self, l):
        p = self.p
        with ExitStack() as st:
            wm = p.sbpool(st, 2, [128, KC, 512], F32, "wm")
            modv = p.sb(st, [128, 96, 2], F32, "modv")
            bm = p.sb(st, [128, 96], F32, "bm")
            gv = p.sb(st, [128, 4, KC], F32, "gv")
            lv = p.sb(st, [128, 4, 64], F32, "lv")
            lt = p.sb(st, [128, 2, 64], F32, "lt")
            ls = p.sb(st, [128, 2], F32, "ls")
            p.dma("sp", bm[:], self.b_mod[l], writes=[bm])
            p.dma("sp", gv[:], self.gvec[l], writes=[gv])
            p.dma("sp", self.dlgs[:], self.dlg[l], writes=[self.dlgs])
            p.dma("sp", lv[:], self.lamv[l], writes=[lv])
            wv = self.w_mod[l].rearrange("(kc p) (g c) -> g p kc c", p=128, c=512)
            ps = self.psum.next()
            for g in range(24):
                w = wm.next()
                p.dma("sp", w[:], wv[g], writes=[w])
                for j in range(4):
                    oc = g * 4 + j
                    for kc in range(KC):
                        p.mm(ps[:, oc * 2:oc * 2 + 2], w[:, kc, j * 128:(j + 1) * 128], self.cs[:, kc, :],
                             kc == 0, kc == KC - 1, [w, self.cs], [ps])
            psv = ps[:, 0:192].rearrange("p (o n) -> p o n", n=2)
            for n in range(2):
                p.tt("dve", modv[:, :, n], psv[:, :, n], bm[:], ALU.add, [ps, bm], [modv], small=True)
            sD = math.sqrt(D)
            mc = self.modc
            for n in range(2):
                m = lambda j: modv[:, j * 16:(j + 1) * 16, n]
                p.stt("dve", mc[:, 0, n, :], m(1), 1.0, gv[:, 0, :], ALU.add, ALU.mult, [modv, gv], [mc], small=True)
                p.ts("dve", mc[:, 0, n, :], mc[:, 0, n, :], sD, None, ALU.mult, None, [mc], [mc], small=True)
                p.copy("dve", mc[:, 1, n, :], m(0), [modv], [mc], small=True)
                p.stt("dve", mc[:, 2, n, :], m(2), sD, gv[:, 1, :], ALU.mult, ALU.mult, [modv, gv], [mc], small=True)
                p.stt("dve", mc[:, 3, n, :], m(4), 1.0, gv[:, 2, :], ALU.add, ALU.mult, [modv, gv], [mc], small=True)
                p.ts("dve", mc[:, 3, n, :], mc[:, 3, n, :], sD, None, ALU.mult, None, [mc], [mc], small=True)
                p.copy("dve", mc[:, 4, n, :], m(3), [modv], [mc], small=True)
                p.stt("dve", mc[:, 5, n, :], m(5), sD, gv[:, 3, :], ALU.mult, ALU.mult, [modv, gv], [mc], small=True)
            lam_init = 0.8 - 0.6 * math.exp(-0.3 * l)
            p.tt("dve", lt[:, 0, :], lv[:, 0, :], lv[:, 1, :], ALU.mult, [lv], [lt], small=True)
            p.tt("dve", lt[:, 1, :], lv[:, 2, :], lv[:, 3, :], ALU.mult, [lv], [lt], small=True)
            p.op("dve", lambda h: h.tensor_reduce(ls[:], lt[:], mybir.AxisListType.X, ALU.add), [lt], [ls], small=True)
            p.act(ls[:], ls[:], AF.Exp, [ls], [ls], small=True)
            p.stt("dve", self.lam[:, 0:1], ls[:, 1:2], -lam_init, ls[:, 0:1], ALU.add, ALU.subtract,
                  [ls], [self.lam], small=True)
            p.ts("dve", self.dlgs[:], self.dlgs[:], (1.0 - lam_init) * math.sqrt(128.0), None, ALU.mult, None,
                 [self.dlgs], [self.dlgs], small=True)
            p.barrier()

    def rsqrt(self, out, in_, c, reads, obuf, small=False):
        p = self.p
        p.act(out, in_, AF.Sqrt, reads + [self.cbias], [obuf], bias=self.cbias[:, self.cbias_idx(c):self.cbias_idx(c) + 1], small=small)
        p.op("dve", lambda h: h.reciprocal(out, out), [obuf], [obuf], small=small)

    def cbias_idx(self, c):
        return self.cb_vals.index(c)

    def norm_mod(self, xt, hT, n, ia, ib, m, sq, rs, tmp):
        p = self.p
        ps = self.psum.next()
        for kc in range(KC):
            q = sq.next()
            p.act(q[:, :n], xt[:, kc, :n], AF.Square, [xt], [q])
            p.mm(ps[:, :n], self.ones_bf[:], q[:, :n], kc == 0, kc == KC - 1, [self.ones_bf, q], [ps], signal=True)
        self.rsqrt(rs[:, :n], ps[:, :n], D * EPS, [ps], rs)
        mc = self.modc
        for kc in range(KC):
            t = tmp.next()
            p.stt("dve", t[:, :n], xt[:, kc, :n], mc[:, ia, m, kc:kc + 1], rs[:, :n], ALU.mult, ALU.mult,
                  [xt, mc, rs], [t])
            p.act(hT[:, kc, :n], t[:, :n], AF.Identity, [t, mc], [hT], bias=mc[:, ib, m, kc:kc + 1])

    def phase_A(self, l, tiles, last):
        p = self.p
        S = self.S
        xTv = self.xT.rearrange("(kc p) t -> p kc t", p=128)
        with ExitStack() as st:
            xpool = p.sbpool(st, 2, [128, KC, 512], F32, "xa")
            sq = p.sbpool(st, 2, [128, 512], BF16, "sq")
            rs = p.sb(st, [128, 512], F32, "rs")
            tmp = p.sbpool(st, 2, [128, 512], F32, "tmpa")
            hpool = p.sbpool(st, 2, [128, KC, 512], BF16, "hT")
            wpool = p.sbpool(st, 3, [128, KC, 512], BF16, "wa")
            abuf = p.sb(st, [128, 8, 512], F32, "abuf")
            stg = p.sbpool(st, 4, [128, 512], BF16, "stg")
            f32t = p.sbpool(st, 3, [128, 512], F32, "f32t")
            rc = p.sbpool(st, 2, [128, 512], F32, "rc")
            rsn = p.sbpool(st, 2, [128, 512], F32, "rsn")
            cnt = 0
            def prep(tile):
                t0_, n_, m_ = tile
                xt_ = xpool.next()
                p.dma("sp", xt_[:, :, :n_], xTv[:, :, t0_:t0_ + n_], writes=[xt_])
                hT_ = hpool.next()
                self.norm_mod(xt_, hT_, n_, 0, 1, m_, sq, rs, tmp)
                return hT_
            hT_next = prep(tiles[0])
            for ti, (t0, n, m) in enumerate(tiles):
                hT = hT_next
                if not m:
                    rcb, rsb = rc.next(), rsn.next()
                    p.dma("sp", rcb[:, :n], self.ropeC[:, t0:t0 + n], writes=[rcb])
                    p.dma("sp", rsb[:, :n], self.ropeS[:, t0:t0 + n], writes=[rsb])
                groups = range(28)
                import os
                if os.environ.get("KGROUPS"):
                    groups = [int(v) for v in os.environ["KGROUPS"].split(",")]
                if m and last:
                    groups = [2, 3, 4, 5, 12, 13, 14, 15]
                groups = list(groups)
                for gi, g in enumerate(groups):
                    if gi == len(groups) // 2 and ti + 1 < len(tiles):
                        hT_next = prep(tiles[ti + 1])
                    w = wpool.next()
                    p.dma("sp", w[:], self.Wi[l, g], writes=[w])
                    if g in (4, 5, 14, 15):
                        dst = self.NV if g < 6 else self.DV
                        hh = (g - 4) * 4 if g < 6 else (g - 14) * 4
                        for tb in range(n // 128):
                            ps = self.psum.next()
                            for kc in range(KC):
                                p.mm(ps[:, :], hT[:, kc, tb * 128:(tb + 1) * 128], w[:, kc, :], kc == 0, kc == KC - 1,
                                     [hT, w], [ps])
                            o = stg.next()
                            eng = "act" if cnt % 2 == 0 else "dve"
                            cnt += 1
                            p.copy(eng, o[:], ps[:], [ps], [o])
                            blk = (t0 + tb * 128) // 128
                            p.dma("pool", dst[hh:hh + 4, :, blk, :].rearrange("h p c -> p h c"),
                                  o[:].rearrange("p (h c) -> p h c", c=128), reads=[o])
                        continue
                    for j in range(4):
                        c = g * 4 + j
                        ps = self.psum.next()
                        for kc in range(KC):
                            p.mm(ps[:, :n], w[:, kc, j * 128:(j + 1) * 128], hT[:, kc, :n], kc == 0, kc == KC - 1,
                                 [w, hT], [ps])
                        if c < 16:
                            dst = self.NQ if c < 8 else self.NK
                            o = stg.next()
                            eng = "act" if cnt % 2 == 0 else "dve"
                            cnt += 1
                            p.copy(eng, o[:, :n], ps[:, :n], [ps], [o])
                            r = (c % 8) * 128
                            p.dma("pool", dst[r:r + 128, t0:t0 + n], o[:, :n], reads=[o])
                        elif c < 32:
                            p.copy("act", abuf[:, c - 24, :n], ps[:, :n], [ps], [abuf])
                        elif c < 40:
                            f = f32t.next()
                            p.act(f[:, :n], ps[:, :n], AF.Sigmoid, [ps], [f])
                            o = stg.next()
                            p.tt("dve", o[:, :n], abuf[:, c - 32, :n], f[:, :n], ALU.mult, [abuf, f], [o])
                            r = (c - 32) * 128
                            p.dma("pool", self.GLU[r:r + 128, t0:t0 + n], o[:, :n], reads=[o])
                        elif c < 56:
                            dst = self.DQ if c < 48 else self.DK
                            r = (c % 8) * 128
                            xb = stg.next()
                            p.copy("act", xb[:, :n], ps[:, :n], [ps], [xb])
                            if m:
                                p.dma("pool", dst[r:r + 128, t0:t0 + n], xb[:, :n], reads=[xb])
                            else:
                                ps2 = self.psum.next()
                                p.mm(ps2[:, :n], self.perm[:], xb[:, :n], True, True, [self.perm, xb], [ps2])
                                f1 = f32t.next()
                                p.tt("dve", f1[:, :n], ps2[:, :n], rsb[:, :n], ALU.mult, [ps2, rsb], [f1])
                                f2 = f32t.next()
                                p.tt("dve", f2[:, :n], xb[:, :n], rcb[:, :n], ALU.mult, [xb, rcb], [f2])
                                o = stg.next()
                                p.tt("dve", o[:, :n], f1[:, :n], f2[:, :n], ALU.add, [f1, f2], [o])
                                p.dma("pool", dst[r:r + 128, t0:t0 + n], o[:, :n], reads=[o])
                        else:
                            o = stg.next()
                            p.act(o[:, :n], ps[:, :n], AF.Sigmoid, [ps], [o])
                            r = (c - 64) * 128
                            p.dma("pool", self.G[r:r + 128, t0:t0 + n], o[:, :n], reads=[o])
            p.barrier()

    def phase_NA(self, l, tiles):
        p = self.p
        S, T = self.S, self.T
        nblk = S // 128
        scale = 128 ** -0.5
        NKv = self.NK.rearrange("(h p) t -> p h t", p=128)
        NQv = self.NQ.rearrange("(h p) t -> p h t", p=128)
        OAv = self.OA.rearrange("(h p) t -> p h t", p=128)
        with ExitStack() as st:
            kc_sb = p.sb(st, [128, NH, 256], BF16, "kcs")
            vc_sb = p.sb(st, [128, NH, 2, 128], BF16, "vcs")
            p.dma("sp", kc_sb[:], NKv[:, :, S:T], writes=[kc_sb])
            p.dma("sp", vc_sb[:], self.NV[:, :, nblk:nblk + 2, :].rearrange("h p b c -> p h b c"), writes=[vc_sb])
            kpool = p.sbpool(st, 2, [128, NH, 1024], BF16, "nak")
            vpool = p.sbpool(st, 2, [128, NH, 8, 128], BF16, "nav")
            qpool = p.sbpool(st, 2, [128, NH, 512], BF16, "naq")
            tabp = p.sbpool(st, 3, [128, 640], F32, "tab")
            sbp = p.sbpool(st, 2, [128, 896], F32, "nsb")
            ep = p.sbpool(st, 2, [128, 896], BF16, "nae")
            rzp = p.sbpool(st, 2, [128, 128], F32, "rz")
            oap = p.sbpool(st, 2, [128, NH, 512], BF16, "oat")
            tab0 = p.sb(st, [128, NH, 640], F32, "tab0")
            p.dma("sp", tab0[:], self.na_tab[l, :, 0].rearrange("h p c -> p h c"), writes=[tab0])
            for (t0, n, m) in tiles:
                qt = qpool.next()
                p.dma("sp", qt[:, :, :n], NQv[:, :, t0:t0 + n], writes=[qt])
                oa = oap.next()
                if not m:
                    j0 = t0 // 128
                    lo = kb0_of(j0, nblk)
                    hi = kb0_of(j0 + 3, nblk) + 5
                    nb = hi - lo
                    kt = kpool.next()
                    p.dma("sp", kt[:, :, :nb * 128], NKv[:, :, lo * 128:hi * 128], writes=[kt])
                    vt = vpool.next()
                    p.dma("sp", vt[:, :, :nb, :], self.NV[:, :, lo:hi, :].rearrange("h p b c -> p h b c"), writes=[vt])
                for h in range(NH):
                    for jj in range(n // 128):
                        q_ap = qt[:, h, jj * 128:(jj + 1) * 128]
                        e = ep.next()
                        if not m:
                            j = j0 + jj
                            kb = kb0_of(j, nblk) - lo
                            vr = variant_of(j, nblk)
                            if vr == 0:
                                tb = tab0
                                tbv = tab0[:, h, :]
                            else:
                                tb = tabp.next()
                                p.dma("sp", tb[:], self.na_tab[l, h, vr], writes=[tb])
                                tbv = tb[:]
                            psA = self.psum.next()
                            psB = self.psum.next()
                            for i in range(5):
                                dst = psA[:, i * 128:(i + 1) * 128] if i < 4 else psB[:, 0:128]
                                p.mm(dst, kt[:, h, (kb + i) * 128:(kb + i + 1) * 128], q_ap, True, True,
                                     [kt, qt], [psA if i < 4 else psB])
                            for i in range(2):
                                p.mm(psB[:, (1 + i) * 128:(2 + i) * 128], kc_sb[:, h, i * 128:(i + 1) * 128], q_ap,
                                     True, True, [kc_sb, qt], [psB])
                            sb = sbp.next()
                            p.stt("dve", sb[:, 0:512], psA[:, :], scale, tbv[:, 0:512], ALU.mult, ALU.add, [psA, tb], [sb])
                            p.stt("dve", sb[:, 512:640], psB[:, 0:128], scale, tbv[:, 512:640], ALU.mult, ALU.add,
                                  [psB, tb], [sb])
                            p.ts("dve", sb[:, 640:896], psB[:, 128:384], scale, None, ALU.mult, None, [psB], [sb])
                            p.act(e[:, 0:896], sb[:], AF.Exp, [sb], [e])
                            nkb = 7
                            lhs_v = lambda i: vt[:, h, kb + i, :] if i < 5 else vc_sb[:, h, i - 5, :]
                            vreads = [vt, vc_sb]
                        else:
                            psB = self.psum.next()
                            for i in range(2):
                                p.mm(psB[:, i * 128:(i + 1) * 128], kc_sb[:, h, i * 128:(i + 1) * 128], q_ap,
                                     True, True, [kc_sb, qt], [psB])
                            p.act(e[:, 0:256], psB[:, 0:256], AF.Exp, [psB], [e], scale=scale)
                            nkb = 2
                            lhs_v = lambda i: vc_sb[:, h, i, :]
                            vreads = [vc_sb]
                        psO = self.psum.next()
                        psZ = self.psum.next()
                        for i in range(nkb):
                            p.mm(psO[:, 0:128], lhs_v(i), e[:, i * 128:(i + 1) * 128], i == 0, i == nkb - 1,
                                 vreads + [e], [psO])
                            p.mm(psZ[:, 0:128], self.ones_bf[:], e[:, i * 128:(i + 1) * 128], i == 0, i == nkb - 1,
                                 [self.ones_bf, e], [psZ])
                        rz = rzp.next()
                        p.op("dve", lambda hh: hh.reciprocal(rz[:], psZ[:, 0:128]), [psZ], [rz])
                        p.tt("dve", oa[:, h, jj * 128:(jj + 1) * 128], psO[:, 0:128], rz[:], ALU.mult, [psO, rz], [oa])
                p.dma("pool", OAv[:, :, t0:t0 + n], oa[:, :, :n], reads=[oa])
            p.barrier()

    def phase_conformer(self, l, tiles):
        p = self.p
        S, T = self.S, self.T
        OBv = self.OB.rearrange("(c p) t -> p c t", p=128)
        with ExitStack() as st:
            cw = p.sb(st, [128, 8, 31], F32, "cw")
            cvv = p.sb(st, [128, 3, 8], F32, "cvv")
            p.dma("sp", cw[:], self.conv_w[l], writes=[cw])
            p.dma("sp", cvv[:], self.conv_v[l], writes=[cvv])
            glp = p.sbpool(st, 5, [128, 544], BF16, "gl")
            cps = Pool(self.psum.bufs[0:6])
            identb = p.sb(st, [128, 128], BF16, "identb")
            p.copy("dve", identb[:], self.ident[:], [self.ident], [identb])
            dg = p.sb(st, [128, 8, 31, 128], BF16, "dg")
            for i in range(8):
                for j in range(31):
                    p.ts("dve", dg[:, i, j, :], identb[:], cw[:, i, j:j + 1], None, ALU.mult, None, [identb, cw], [dg])
            acc = [p.sb(st, [128, 512], F32, "cacc") for _ in range(8)]
            sqp = p.sbpool(st, 2, [128, 512], F32, "csq")
            mu = p.sb(st, [128, 512], F32, "mu")
            var = p.sb(st, [128, 512], F32, "var")
            rstd = p.sb(st, [128, 512], F32, "rstd")
            tp = p.sbpool(st, 3, [128, 512], F32, "ct")
            obp = p.sbpool(st, 2, [128, 8, 512], BF16, "obt")
            for (t0, n, m) in tiles:
                seq_lo, seq_hi = (S, T) if m else (0, S)
                lo = max(t0 - 15, seq_lo)
                hi = min(t0 + n + 15, seq_hi)
                off = lo - (t0 - 15)
                psM = self.psum.bufs[6]
                psQ = self.psum.bufs[7]
                for i in range(8):
                    gl = glp.next()
                    if off > 0:
                        p.op("dve", lambda hh: hh.memset(gl[:, 0:off], 0.0), (), [gl], small=True)
                    if off + hi - lo < n + 30:
                        p.op("dve", lambda hh: hh.memset(gl[:, off + hi - lo:n + 30], 0.0), (), [gl], small=True)
                    p.dma("sp", gl[:, off:off + hi - lo], self.GLU[i * 128:(i + 1) * 128, lo:hi], writes=[gl])
                    a = acc[i]
                    psc = cps.next()
                    for j in range(31):
                        p.mm(psc[:, :n], dg[:, i, j, :], gl[:, j:j + n], j == 0, j == 30, [dg, gl], [psc])
                    p.act(a[:, :n], psc[:, :n], AF.Identity, [psc, cvv], [a], bias=cvv[:, 0, i:i + 1])
                    sq = sqp.next()
                    p.act(sq[:, :n], a[:, :n], AF.Square, [a], [sq])
                    p.mm(psM[:, :n], self.ones_f[:], a[:, :n], i == 0, i == 7, [self.ones_f, a], [psM], signal=True)
                    p.mm(psQ[:, :n], self.ones_f[:], sq[:, :n], i == 0, i == 7, [self.ones_f, sq], [psQ], signal=True)
                p.ts("dve", mu[:, :n], psM[:, :n], 1.0 / 1024, None, ALU.mult, None, [psM], [mu])
                p.tt("dve", var[:, :n], mu[:, :n], mu[:, :n], ALU.mult, [mu], [var])
                p.stt("dve", var[:, :n], psQ[:, :n], 1.0 / 1024, var[:, :n], ALU.mult, ALU.subtract, [psQ, var], [var])
                self.rsqrt(rstd[:, :n], var[:, :n], EPS, [var], rstd)
                ob = obp.next()
                for i in range(8):
                    t = tp.next()
                    p.tt("dve", t[:, :n], acc[i][:, :n], mu[:, :n], ALU.subtract, [acc[i], mu], [t])
                    p.tt("dve", t[:, :n], t[:, :n], rstd[:, :n], ALU.mult, [t, rstd], [t])
                    p.act(ob[:, i, :n], t[:, :n], AF.Silu, [t, cvv], [ob], scale=cvv[:, 1, i:i + 1], bias=cvv[:, 2, i:i + 1])
                p.dma("pool", OBv[:, :, t0:t0 + n], ob[:, :, :n], reads=[ob])
            p.barrier()

    def phase_diff(self, l, tiles):
        p = self.p
        S, T = self.S, self.T
        scale = 64 ** -0.5
        DQv = self.DQ.rearrange("(h p) t -> p h t", p=128)
        OCv = self.OC.rearrange("(h p) t -> p h t", p=128)
        with ExitStack() as st:
            qpool = p.sbpool(st, 2, [128, NH, 512], BF16, "dq")
            kpool = p.sbpool(st, 2, [128, T], BF16, "dk")
            vpool = p.sbpool(st, 2, [128, T // 128, 128], BF16, "dv")
            ocp = p.sbpool(st, 2, [128, NH, 512], BF16, "oct")
            fp = p.sbpool(st, 6, [128, 512], F32, "df")
            acc = self.psum.bufs[0:2]
            pairs = [(self.psT[i], self.psum.bufs[2 * i], self.psum.bufs[2 * i + 1]) for i in (1, 2, 3)]
            pi = 0
            z = p.sb(st, [128, 2, 512], F32, "z")
            ep = p.sbpool(st, 8, [128, 2, 512], BF16, "e12")
            esp = p.sbpool(st, 2, [128, 2, 512], BF16, "esum")
            for (t0, n, m) in tiles:
                qt = qpool.next()
                p.dma("sp", qt[:, :, :n], DQv[:, :, t0:t0 + n], writes=[qt])
                kts = list(range(S // 128, T // 128)) if m else list(range(T // 128))
                oc = ocp.next()
                for h in range(NH):
                    kb = kpool.next()
                    p.dma("sp", kb[:], self.DK[h * 128:(h + 1) * 128, :], writes=[kb])
                    vb = vpool.next()
                    p.dma("sp", vb[:], self.DV[h], writes=[vb])
                    O1, O2 = acc

                    pend = []

                    def pv(e, kt, first, lastk):
                        p.mm(O1[:, :n], vb[:, kt, :], e[:, 0, :n], first, lastk, [vb, e], [O1], signal=False)
                        p.mm(O2[:, :n], vb[:, kt, :], e[:, 1, :n], first, lastk, [vb, e], [O2], signal=True)
                        pend.append((e, first))
                        if len(pend) == 2:
                            (ea, fa), (eb, _) = pend
                            del pend[:]
                            es = esp.next()
                            p.tt("dve", es[:, :, :n], ea[:, :, :n], eb[:, :, :n], ALU.add, [ea, eb], [es])
                            if fa:
                                p.copy("dve", z[:, :, :n], es[:, :, :n], [es], [z])
                            else:
                                p.tt("dve", z[:, :, :n], z[:, :, :n], es[:, :, :n], ALU.add, [z, es], [z], strict=False)
                    prev = []
                    for idx, kt in enumerate(kts):
                        TT, s1, s2 = pairs[pi % 3]
                        pi += 1
                        p.mm(s1[:, :n], kb[0:64, kt * 128:(kt + 1) * 128], qt[0:64, h, :n], True, True, [kb, qt], [s1],
                             signal=False)
                        p.mm(s2[:, :n], kb[64:128, kt * 128:(kt + 1) * 128], qt[64:128, h, :n], True, True, [kb, qt], [s2])
                        e = ep.next()
                        p.act(e[:, :, :n], TT.t[:].rearrange("p (b c) -> p b c", b=2)[:, :, :n], AF.Exp, [s1, s2], [e],
                              scale=scale)
                        prev.append((e, kt, idx == 0, idx == len(kts) - 1))
                        if len(prev) > 2:
                            pv(*prev.pop(0))
                    while prev:
                        pv(*prev.pop(0))
                    z1 = z[:, 0, :]
                    z2 = z[:, 1, :]
                    z1b = z2b = z
                    _, Z1, Z2 = pairs[pi % 3]
                    pi += 1
                    p.mm(Z1[:, :n], self.ones_f[:], z1[:, :n], True, True, [self.ones_f, z], [Z1])
                    p.mm(Z2[:, :n], self.ones_f[:], z2[:, :n], True, True, [self.ones_f, z], [Z2])
                    r1 = fp.next()
                    p.op("dve", lambda hh: hh.reciprocal(r1[:, :n], Z1[:, :n]), [Z1], [r1])
                    t1 = fp.next()
                    p.tt("dve", t1[:, :n], O1[:, :n], r1[:, :n], ALU.mult, [O1, r1], [t1])
                    r2 = fp.next()
                    p.op("dve", lambda hh: hh.reciprocal(r2[:, :n], Z2[:, :n]), [Z2], [r2])
                    t2 = fp.next()
                    p.tt("dve", t2[:, :n], O2[:, :n], r2[:, :n], ALU.mult, [O2, r2], [t2])
                    p.stt("dve", t1[:, :n], t2[:, :n], self.lam[:, 0:1], t1[:, :n], ALU.mult, ALU.add,
                          [t2, self.lam, t1], [t1])
                    p.act(r1[:, :n], t1[:, :n], AF.Square, [t1], [r1])
                    _, psS, _unused = pairs[pi % 3]
                    pi += 1
                    p.mm(psS[:, :n], self.ones_f[:], r1[:, :n], True, True, [self.ones_f, r1], [psS])
                    self.rsqrt(r2[:, :n], psS[:, :n], 128 * EPS, [psS], r2)
                    p.stt("dve", oc[:, h, :n], t1[:, :n], self.dlgs[:, 0:1], r2[:, :n], ALU.mult, ALU.mult,
                          [t1, self.dlgs, r2], [oc])
                p.dma("pool", OCv[:, :, t0:t0 + n], oc[:, :, :n], reads=[oc])
            p.barrier()

    def post_residual(self, mix, pss, rs, n, t0, m, ig, xp, tp):
        p = self.p
        mc = self.modc
        self.rsqrt(rs[:, :n], pss[:, :n], D * EPS, [pss], rs)
        for kc in range(KC):
            xt = xp.next()
            p.dma("sp", xt[:, :n], self.xT[kc * 128:(kc + 1) * 128, t0:t0 + n], writes=[xt])
            t = tp.next()
            p.stt("dve", t[:, :n], mix[:, kc, :n], mc[:, ig, m, kc:kc + 1], rs[:, :n], ALU.mult, ALU.mult,
                  [mix, mc, rs], [t])
            p.tt("dve", xt[:, :n], xt[:, :n], t[:, :n], ALU.add, [xt, t], [xt])
            p.dma("pool", self.xT[kc * 128:(kc + 1) * 128, t0:t0 + n], xt[:, :n], reads=[xt])

    def phase_merge(self, l, tiles):
        p = self.p
        Gv = self.G.rearrange("(br c p) t -> p br c t", p=128, c=16)
        with ExitStack() as st:
            bp = [p.sbpool(st, 1, [128, 8, 512], BF16, "mb%d" % i) for i in range(3)]
            wp = [p.sbpool(st, 2, [128, 8, 512], BF16, "mw%d" % i) for i in range(3)]
            gtp = p.sbpool(st, 3, [128, 3, 512], BF16, "gt")
            tp = p.sbpool(st, 6, [128, 512], F32, "mt")
            y = p.sb(st, [128, KC, 512], BF16, "y")
            wop = p.sbpool(st, 2, [128, KC, 512], BF16, "wo")
            mix = p.sb(st, [128, KC, 512], F32, "mix")
            sq = p.sbpool(st, 2, [128, 512], BF16, "msq")
            rs = p.sb(st, [128, 512], F32, "mrs")
            xp = p.sbpool(st, 3, [128, 512], F32, "mx")
            srcs = [self.OA, self.OB, self.OC]
            Ws = [self.Pa, self.Pb, self.Pc]
            full_psum = self.psum
            pss = full_psum.bufs[7]
            self.psum = Pool(full_psum.bufs[0:7])
            for (t0, n, m) in tiles:
                br = []
                for i in range(3):
                    b = bp[i].next()
                    p.dma("sp", b[:, :, :n], srcs[i].rearrange("(c p) t -> p c t", p=128)[:, :, t0:t0 + n], writes=[b])
                    br.append(b)
                for og in range(4):
                    ws = []
                    for i in range(3):
                        w = wp[i].next()
                        p.dma("sp", w[:], Ws[i][l, og], writes=[w])
                        ws.append(w)
                    for j in range(4):
                        c = og * 4 + j
                        gt = gtp.next()
                        p.dma("sp", gt[:, :, :n], Gv[:, :, c, t0:t0 + n], writes=[gt])
                        ts_ = []
                        for i in range(3):
                            ps = self.psum.next()
                            for kc in range(8):
                                p.mm(ps[:, :n], ws[i][:, kc, j * 128:(j + 1) * 128], br[i][:, kc, :n], kc == 0, kc == 7,
                                     [ws[i], br[i]], [ps])
                            t = tp.next()
                            p.tt("dve", t[:, :n], ps[:, :n], gt[:, i, :n], ALU.mult, [ps, gt], [t])
                            ts_.append(t)
                        p.tt("dve", ts_[0][:, :n], ts_[0][:, :n], ts_[1][:, :n], ALU.add, [ts_[0], ts_[1]], [ts_[0]])
                        p.tt("dve", y[:, c, :n], ts_[0][:, :n], ts_[2][:, :n], ALU.add, [ts_[0], ts_[2]], [y])
                for og in range(4):
                    wo = wop.next()
                    p.dma("sp", wo[:], self.Wo[l, og], writes=[wo])
                    for j in range(4):
                        c = og * 4 + j
                        ps = self.psum.next()
                        for kc in range(KC):
                            p.mm(ps[:, :n], wo[:, kc, j * 128:(j + 1) * 128], y[:, kc, :n], kc == 0, kc == KC - 1,
                                 [wo, y], [ps])
                        p.copy("dve", mix[:, c, :n], ps[:, :n], [ps], [mix])
                        q = sq.next()
                        p.act(q[:, :n], mix[:, c, :n], AF.Square, [mix], [q])
                        p.mm(pss[:, :n], self.ones_bf[:], q[:, :n], c == 0, c == KC - 1, [self.ones_bf, q], [pss], signal=True)
                self.post_residual(mix, pss, rs, n, t0, m, 2, xp, tp)
            self.psum = full_psum
            p.barrier()

    def phase_ffn_up(self, l, tiles):
        p = self.p
        xTv = self.xT.rearrange("(kc p) t -> p kc t", p=128)
        with ExitStack() as st:
            xpool = p.sbpool(st, 2, [128, KC, 512], F32, "xu")
            sq = p.sbpool(st, 2, [128, 512], BF16, "usq")
            rs = p.sb(st, [128, 512], F32, "urs")
            tmp = p.sbpool(st, 2, [128, 512], F32, "utmp")
            hpool = p.sbpool(st, 2, [128, KC, 512], BF16, "uh")
            wpool = p.sbpool(st, 3, [128, KC, 512], BF16, "uw")
            stg = p.sbpool(st, 4, [128, 512], BF16, "ustg")
            cnt = 0
            def prep(tile):
                t0_, n_, m_ = tile
                xt_ = xpool.next()
                p.dma("sp", xt_[:, :, :n_], xTv[:, :, t0_:t0_ + n_], writes=[xt_])
                hT_ = hpool.next()
                self.norm_mod(xt_, hT_, n_, 3, 4, m_, sq, rs, tmp)
                return hT_
            hT_next = prep(tiles[0])
            for ti, (t0, n, m) in enumerate(tiles):
                hT = hT_next
                for g in range(22):
                    if g == 11 and ti + 1 < len(tiles):
                        hT_next = prep(tiles[ti + 1])
                    w = wpool.next()
                    p.dma("sp", w[:], self.Wu[l, g], writes=[w])
                    for j in range(4):
                        c = g * 4 + j
                        ps = self.psum.next()
                        for kc in range(KC):
                            p.mm(ps[:, :n], w[:, kc, j * 128:(j + 1) * 128], hT[:, kc, :n], kc == 0, kc == KC - 1,
                                 [w, hT], [ps])
                        o = stg.next()
                        eng = "act" if cnt % 2 == 0 else "dve"
                        cnt += 1
                        p.copy(eng, o[:, :n], ps[:, :n], [ps], [o])
                        p.dma("pool", self.U[c * 128:(c + 1) * 128, t0:t0 + n], o[:, :n], reads=[o])
            p.barrier()

    def phase_ffn_down(self, l, tiles):
        p = self.p
        S, T = self.S, self.T
        with ExitStack() as st:
            fw = p.sb(st, [128, 88, 3], F32, "fw")
            fb = p.sb(st, [128, 88], F32, "fb")
            p.dma("sp", fw[:], self.fcw[l], writes=[fw])
            p.dma("sp", fb[:], self.fcb[l], writes=[fb])
            up = p.sbpool(st, 12, [128, 516], BF16, "fu")
            ca = p.sbpool(st, 9, [128, 512], F32, "fca")
            gTp = p.sbpool(st, 2, [128, 44, 512], BF16, "gT")
            wdp = p.sbpool(st, 2, [128, 44, 128], BF16, "wd")
            mix = p.sb(st, [128, KC, 512], F32, "fmix")
            sq = p.sbpool(st, 2, [128, 512], BF16, "fsq")
            rs = p.sb(st, [128, 512], F32, "frs")
            xp = p.sbpool(st, 3, [128, 512], F32, "fx")
            tp = p.sbpool(st, 3, [128, 512], F32, "ft")
            full_psum = self.psum
            pss = full_psum.bufs[7]
            self.psum = Pool(full_psum.bufs[0:7])
            def conv_pair(tile, i, gT):
                t0, n, m = tile
                seq_lo, seq_hi = (S, T) if m else (0, S)
                lo = max(t0 - 1, seq_lo)
                hi = min(t0 + n + 1, seq_hi)
                off = lo - (t0 - 1)
                res = []
                for half in range(2):
                    ch = half * 44 + i
                    u = up.next()
                    if off > 0:
                        p.op("dve", lambda hh: hh.memset(u[:, 0:off], 0.0), (), [u], small=True)
                    if off + hi - lo < n + 2:
                        p.op("dve", lambda hh: hh.memset(u[:, off + hi - lo:n + 2], 0.0), (), [u], small=True)
                    p.dma("sp", u[:, off:off + hi - lo], self.U[ch * 128:(ch + 1) * 128, lo:hi], writes=[u])
                    a = ca.next()
                    p.act(a[:, :n], u[:, 1:n + 1], AF.Identity, [u, fw, fb], [a], scale=fw[:, ch, 1:2], bias=fb[:, ch:ch + 1])
                    p.stt("dve", a[:, :n], u[:, 0:n], fw[:, ch, 0:1], a[:, :n], ALU.mult, ALU.add, [u, fw, a], [a])
                    p.stt("dve", a[:, :n], u[:, 2:n + 2], fw[:, ch, 2:3], a[:, :n], ALU.mult, ALU.add, [u, fw, a], [a],
                          strict=False)
                    res.append(a)
                s_ = ca.next()
                p.act(s_[:, :n], res[0][:, :n], AF.Silu, [res[0]], [s_])
                p.tt("dve", gT[:, i, :n], s_[:, :n], res[1][:, :n], ALU.mult, [s_, res[1]], [gT])

            def down_chunk(tile, c, gT):
                t0, n, m = tile
                wd = wdp.next()
                p.dma("sp", wd[:], self.Wd[l, c], writes=[wd])
                ps = self.psum.next()
                for kc in range(44):
                    p.mm(ps[:, :n], wd[:, kc, :], gT[:, kc, :n], kc == 0, kc == 43, [wd, gT], [ps])
                p.copy("act", mix[:, c, :n], ps[:, :n], [ps], [mix])
                q = sq.next()
                p.act(q[:, :n], mix[:, c, :n], AF.Square, [mix], [q])
                p.mm(pss[:, :n], self.ones_bf[:], q[:, :n], c == 0, c == KC - 1, [self.ones_bf, q], [pss], signal=True)

            gT_cur = gTp.next()
            for i in range(44):
                conv_pair(tiles[0], i, gT_cur)
            for ti, tile in enumerate(tiles):
                nxt = tiles[ti + 1] if ti + 1 < len(tiles) else None
                gT_nxt = gTp.next() if nxt is not None else None
                for c in range(16):
                    if nxt is not None:
                        for i in range(c * 44 // 16, (c + 1) * 44 // 16):
                            conv_pair(nxt, i, gT_nxt)
                    down_chunk(tile, c, gT_cur)
                self.post_residual(mix, pss, rs, tile[1], tile[0], tile[2], 5, xp, tp)
                gT_cur = gT_nxt
            self.psum = full_psum
            p.barrier()


def _col(v, nch):
    v = np.asarray(v, np.float32)
    return np.ascontiguousarray(np.swapaxes(v.reshape(v.shape[:-1] + (nch, 128)), -1, -2))


def _na_tables(na_rpb, nrows):
    NL = na_rpb.shape[0]
    nblk = nrows // 2
    wr, wc = 8, 16
    col = np.arange(GRID_W)
    c0 = np.clip(col - wc // 2, 0, GRID_W - wc)
    rep = {0: min(2, nblk - 3), 1: 0, 2: 1, 3: nblk - 2, 4: nblk - 1}
    tab = np.full((NL, NH, 5, 128, 640), NEG, np.float32)
    for v, j in rep.items():
        kb0 = kb0_of(j, nblk)
        for qi in range(128):
            r = 2 * j + qi // 64
            c = qi % 64
            r0 = min(max(r - wr // 2, 0), nrows - wr)
            for i in range(5):
                for kr in range(2):
                    rr = (kb0 + i) * 2 + kr
                    if rr < r0 or rr >= r0 + wr:
                        continue
                    cc = np.arange(c0[c], c0[c] + wc)
                    tab[:, :, v, kr * 64 + cc, i * 128 + qi] = na_rpb[:, :, rr - r + 7, :][:, :, cc - c + 15]
    return tab


def _rope_tables(S):
    t = np.arange(S)
    rows, cols = t // GRID_W, t % GRID_W
    inv = (10000.0 ** (-np.arange(0, 32, 2, dtype=np.float32) / 32)).astype(np.float32)
    C = np.zeros((128, S), np.float32)
    Sn = np.zeros((128, S), np.float32)
    perm = np.zeros((128, 128), np.float32)
    for pp in range(128):
        sub = pp % 64
        part = sub // 32
        i = sub % 16
        pos = (rows if part == 0 else cols).astype(np.float32)
        ang = pos * inv[i]
        C[pp] = np.cos(ang)
        Sn[pp] = np.sin(ang)
        first = (sub % 32) < 16
        partner = pp + 16 if first else pp - 16
        perm[partner, pp] = -1.0 if first else 1.0
    return C, Sn, perm


def prep_shared(inp, S):
    NL = inp["w_mod"].shape[0]
    f = lambda k: np.ascontiguousarray(np.asarray(inp[k], np.float32))
    C, Sn, perm = _rope_tables(S)
    sh = {
        "w_mod": f("w_mod"), "w_in": f("w_in"), "p_a": f("p_a"), "p_b": f("p_b"), "p_c": f("p_c"),
        "w_out": f("w_out"), "w_up": f("w_up"), "w_down": f("w_down"),
        "b_mod_c": _col(inp["b_mod"], 96),
        "gvec": np.ascontiguousarray(np.stack([_col(inp[k], KC) for k in
                                               ("g_pre_mix", "g_post_mix", "g_pre_ffn", "g_post_ffn")], axis=2)),
        "na_tab": _na_tables(np.asarray(inp["na_rpb"], np.float32), S // GRID_W),
        "conv_w_c": np.ascontiguousarray(np.asarray(inp["conv_w"], np.float32).reshape(NL, 31, 8, 128).transpose(0, 3, 2, 1)),
        "conv_v_c": np.ascontiguousarray(np.stack([_col(inp[k], 8) for k in ("conv_b", "conv_ln_g", "conv_ln_b")], axis=2)),
        "lamv": np.ascontiguousarray(np.broadcast_to(
            np.stack([np.asarray(inp[k], np.float32) for k in ("lam_q1", "lam_k1", "lam_q2", "lam_k2")], axis=1)[:, None],
            (NL, 128, 4, 64))),
        "dlg_c": np.ascontiguousarray(np.asarray(inp["diff_ln_g"], np.float32).reshape(NL, 128, 1)),
        "fcw_c": np.ascontiguousarray(np.asarray(inp["ffn_conv_w"], np.float32).reshape(NL, 3, 88, 128).transpose(0, 3, 2, 1)),
        "fcb_c": _col(inp["ffn_conv_b"], 88),
        "ropeC": C, "ropeS": Sn, "perm": perm, "ident": np.eye(128, dtype=np.float32),
    }
    return sh


def prep_core(inp, b):
    cv = np.stack([_col(np.asarray(inp["c"], np.float32)[b], KC), _col(np.asarray(inp["c_ctx"], np.float32), KC)], axis=2)
    return {"x": np.ascontiguousarray(np.asarray(inp["x"], np.float32)[b]),
            "ctx": np.ascontiguousarray(np.asarray(inp["ctx"], np.float32)[b]),
            "cvec": np.ascontiguousarray(cv)}


def kernel(**inputs):
    B, S, _ = inputs["x"].shape
    kern = Kern(S)
    nc = kern.build()
    sh = prep_shared(inputs, S)
    in_maps = [dict(sh, **prep_core(inputs, b % B)) for b in range(8)]
    res = run_bass_kernel_spmd(nc, in_maps, core_ids=list(range(8)))
    return np.stack([np.asarray(res.results[b]["out"], np.float32) for b in range(B)], axis=0)
```

```python
import math
from contextlib import ExitStack
import numpy as np
import concourse.bass as bass
import concourse.mybir as mybir
from concourse.bass_utils import run_bass_kernel_spmd

F32 = mybir.dt.float32
BF16 = mybir.dt.bfloat16
AF = mybir.ActivationFunctionType
ALU = mybir.AluOpType

D = 2048
KC = 16
L_CTX = 256
GRID_W = 64
NH = 8
IN_COLS = 14336
FFN = 5632
EPS = 1e-6
NDMA_SEM = 16
NEG = -30000.0
STRICT = True


class Eng:
    def __init__(self, name, h, sem, dsems):
        self.name, self.h, self.sem, self.n = name, h, sem, 0
        self.seen = {}
        self.dsems = dsems
        self.ndma = 0


class Buf:
    __slots__ = ("name", "ap", "w", "r", "t", "prev")

    def __init__(self, name, t=None):
        self.name = name
        self.t = t
        self.ap = t
        self.w = {}
        self.r = {}
        self.prev = {}

    def __getitem__(self, idx):
        return self.t[idx]


class Pool:
    def __init__(self, bufs):
        self.bufs = bufs
        self.i = 0

    def next(self):
        b = self.bufs[self.i % len(self.bufs)]
        self.i += 1
        return b


class Prog:
    def __init__(self, nc, es):
        self.nc = nc
        self.es = es
        self.E = {}
        for name, h, nd in (("pe", nc.tensor, 0), ("act", nc.scalar, 0), ("dve", nc.vector, 0),
                            ("pool", nc.gpsimd, NDMA_SEM), ("sp", nc.sync, NDMA_SEM)):
            sem = es.enter_context(nc.semaphore("sem_" + name))
            ds = [es.enter_context(nc.semaphore("dsem_%s_%d" % (name, i))) for i in range(nd)]
            self.E[name] = Eng(name, h, sem, ds)
        self.uid = 0

    def sb(self, stack, shape, dt, name=None):
        self.uid += 1
        name = (name or "t") + "_%d" % self.uid
        t = stack.enter_context(self.nc.sbuf_tensor(name, list(shape), dt))
        return Buf(name, t)

    def sbpool(self, stack, n, shape, dt, name=None):
        return Pool([self.sb(stack, shape, dt, name) for _ in range(n)])

    def ps(self, stack, shape=(128, 512), dt=F32, name=None):
        self.uid += 1
        name = (name or "ps") + "_%d" % self.uid
        t = stack.enter_context(self.nc.psum_tensor(name, list(shape), dt))
        return Buf(name, t)

    def _deps(self, eng, reads, writes, waw, strict=True):
        deps = {}

        def add(d):
            for k, (sem, val, small) in d.items():
                if sem is eng.sem and (eng.name == "pe" or not (small or (STRICT and strict))):
                    continue
                if k not in deps or deps[k][1] < val:
                    deps[k] = (sem, val)
        for b in reads:
            add(b.w)
        for b in writes:
            if b.r:
                add(b.r)
            else:
                add(b.prev)
                if waw:
                    add(b.w)
        return deps

    def _wait(self, eng, deps):
        for k, (sem, val) in deps.items():
            if eng.seen.get(k, 0) >= val:
                continue
            eng.h.wait_ge(sem, val)
            eng.seen[k] = val

    def _record(self, tok, reads, writes):
        k = id(tok[0])
        for b in reads:
            b.r[k] = tok
        for b in writes:
            if b.r:
                b.w = {k: tok}
                b.prev = b.r
                b.r = {}
            else:
                b.w[k] = tok

    def op(self, engname, fn, reads=(), writes=(), small=False, waw=True, strict=True, signal=True):
        eng = self.E[engname]
        self._wait(eng, self._deps(eng, reads, writes, waw, strict))
        ins = fn(eng.h)
        if signal:
            ins.then_inc(eng.sem, 1)
            eng.n += 1
            self._record((eng.sem, eng.n, small), reads, writes)
        else:
            self._record((eng.sem, eng.n + 1, small), reads, writes)

    def dma(self, q, out, in_, reads=(), writes=(), waw=False):
        eng = self.E[q]
        deps = self._deps(eng, reads, writes, waw)
        i = eng.ndma % NDMA_SEM
        rnd = eng.ndma // NDMA_SEM
        sem = eng.dsems[i]
        if rnd > 0:
            k = id(sem)
            if k not in deps or deps[k][1] < 16 * rnd:
                deps[k] = (sem, 16 * rnd)
        self._wait(eng, deps)
        eng.h.dma_start(out=out, in_=in_).then_inc(sem, 16)
        eng.ndma += 1
        self._record((sem, 16 * (rnd + 1), False), reads, writes)

    def barrier(self):
        toks = {}
        for e in self.E.values():
            if e.n:
                toks[id(e.sem)] = (e.sem, e.n, e)
            for i, s in enumerate(e.dsems):
                cnt = (e.ndma - i + NDMA_SEM - 1) // NDMA_SEM
                if cnt > 0:
                    toks[id(s)] = (s, 16 * cnt, None)
        for e in self.E.values():
            d = {k: (s, v) for k, (s, v, own) in toks.items() if own is not e}
            self._wait(e, d)

    def mm(self, out, lhsT, rhs, start, stop, reads, writes, signal=None):
        if signal is None:
            signal = stop
        self.op("pe", lambda h: h.matmul(out, lhsT, rhs, start=start, stop=stop), reads, writes, signal=signal)

    def act(self, out, in_, func, reads, writes, bias=None, scale=None, small=False):
        kw = {}
        if bias is not None:
            kw["bias"] = bias
        if scale is not None:
            kw["scale"] = scale
        self.op("act", lambda h: h.activation(out, in_, func, **kw), reads, writes, small=small)

    def tt(self, eng, out, in0, in1, op, reads, writes, small=False, strict=True):
        self.op(eng, lambda h: h.tensor_tensor(out, in0, in1, op), reads, writes, small=small, strict=strict)

    def ts(self, eng, out, in0, s1, s2, op0, op1, reads, writes, small=False):
        if s2 is None:
            self.op(eng, lambda h: h.tensor_scalar(out, in0, s1, None, op0), reads, writes, small=small)
        else:
            self.op(eng, lambda h: h.tensor_scalar(out, in0, s1, s2, op0, op1), reads, writes, small=small)

    def stt(self, eng, out, in0, scalar, in1, op0, op1, reads, writes, small=False, strict=True):
        self.op(eng, lambda h: h.scalar_tensor_tensor(out, in0, scalar, in1, op0, op1), reads, writes, small=small,
                strict=strict)

    def copy(self, eng, out, in_, reads, writes, small=False, strict=True):
        if eng == "act":
            self.op("act", lambda h: h.copy(out, in_), reads, writes, small=small, strict=strict)
        else:
            self.op(eng, lambda h: h.tensor_copy(out, in_), reads, writes, small=small, strict=strict)


def kb0_of(j, nblk):
    return min(max(j - 2, 0), nblk - 5)


def variant_of(j, nblk):
    if j == 0:
        return 1
    if j == 1:
        return 2
    if j == nblk - 2:
        return 3
    if j == nblk - 1:
        return 4
    return 0


class Kern:
    def __init__(self, S, NL=2, debug=()):
        self.S, self.NL = S, NL
        self.T = S + L_CTX
        self.debug = set(debug)
        nc = bass.Bass("TRN2", target_bir_lowering=False)
        self.nc = nc
        T = self.T

        def din(name, shape, dt=F32):
            return nc.dram_tensor(name, list(shape), dt, kind="ExternalInput").ap()

        def scr(name, shape, dt=BF16):
            kind = "ExternalOutput" if name in self.debug else "Internal"
            return nc.dram_tensor(name, list(shape), dt, kind=kind).ap()

        self.x = din("x", [S, D])
        self.ctx = din("ctx", [L_CTX, D])
        self.cvec = din("cvec", [128, KC, 2])
        self.w_mod = din("w_mod", [NL, D, 6 * D])
        self.b_mod = din("b_mod_c", [NL, 128, 96])
        self.gvec = din("gvec", [NL, 128, 4, KC])
        self.w_in = din("w_in", [NL, D, IN_COLS])
        self.na_tab = din("na_tab", [NL, NH, 5, 128, 640])
        self.conv_w = din("conv_w_c", [NL, 128, 8, 31])
        self.conv_v = din("conv_v_c", [NL, 128, 3, 8])
        self.lamv = din("lamv", [NL, 128, 4, 64])
        self.dlg = din("dlg_c", [NL, 128, 1])
        self.p_a = din("p_a", [NL, 1024, D])
        self.p_b = din("p_b", [NL, 1024, D])
        self.p_c = din("p_c", [NL, 1024, D])
        self.w_out = din("w_out", [NL, D, D])
        self.w_up = din("w_up", [NL, D, 2 * FFN])
        self.fcw = din("fcw_c", [NL, 128, 88, 3])
        self.fcb = din("fcb_c", [NL, 128, 88])
        self.w_down = din("w_down", [NL, FFN, D])
        self.ropeC = din("ropeC", [128, S])
        self.ropeS = din("ropeS", [128, S])
        self.perm_in = din("perm", [128, 128])
        self.ident_in = din("ident", [128, 128])
        self.out = nc.dram_tensor("out", [S, D], F32, kind="ExternalOutput").ap()
        self.Wi = scr("Wi", [NL, 28, 128, KC, 512])
        self.Pa = scr("Pa", [NL, 4, 128, 8, 512])
        self.Pb = scr("Pb", [NL, 4, 128, 8, 512])
        self.Pc = scr("Pc", [NL, 4, 128, 8, 512])
        self.Wo = scr("Wo", [NL, 4, 128, KC, 512])
        self.Wu = scr("Wu", [NL, 22, 128, KC, 512])
        self.Wd = scr("Wd", [NL, 16, 128, 44, 128])
        self.xT = scr("xT", [D, T], F32)
        self.NQ = scr("NQ", [1024, T])
        self.NK = scr("NK", [1024, T])
        self.NV = scr("NV", [NH, 128, T // 128, 128])
        self.GLU = scr("GLU", [1024, T])
        self.DQ = scr("DQ", [1024, T])
        self.DK = scr("DK", [1024, T])
        self.DV = scr("DV", [NH, 128, T // 128, 128])
        self.G = scr("G", [3 * D, T])
        self.OA = scr("OA", [1024, T])
        self.OB = scr("OB", [1024, T])
        self.OC = scr("OC", [1024, T])
        self.U = scr("U", [2 * FFN, T])

    def build(self, phases=None):
        nc = self.nc
        with ExitStack() as es:
            p = Prog(nc, es)
            self.p = p
            self.ones_bf = p.sb(es, [128, 128], BF16, "ones_bf")
            self.ones_f = p.sb(es, [128, 128], F32, "ones_f")
            self.ident = p.sb(es, [128, 128], F32, "ident")
            self.perm = p.sb(es, [128, 128], BF16, "perm")
            self.cs = p.sb(es, [128, KC, 2], F32, "cs")
            self.modc = p.sb(es, [128, 6, 2, KC], F32, "modc")
            self.lam = p.sb(es, [128, 2], F32, "lam")
            self.dlgs = p.sb(es, [128, 1], F32, "dlgs")
            self.psT = [p.ps(es, shape=(128, 1024)) for _ in range(4)]
            halves = []
            for T in self.psT:
                halves.append(Buf(T.name + "a", T.t[:, 0:512]))
                halves.append(Buf(T.name + "b", T.t[:, 512:1024]))
            self.psum = Pool(halves)
            self.cb_vals = [D * EPS, EPS, 128 * EPS]
            self.cbias = p.sb(es, [128, len(self.cb_vals)], F32, "cbias")
            for i, v in enumerate(self.cb_vals):
                p.op("dve", lambda h, i=i, v=v: h.memset(self.cbias[:, i:i + 1], v), (), [self.cbias], small=True)
            p.op("dve", lambda h: h.memset(self.ones_f[:], 1.0), (), [self.ones_f])
            p.op("dve", lambda h: h.memset(self.ones_bf[:], 1.0), (), [self.ones_bf])
            p.dma("sp", self.ident[:], self.ident_in, writes=[self.ident])
            p.dma("pool", self.perm[:], self.perm_in, writes=[self.perm])
            p.dma("sp", self.cs[:], self.cvec, writes=[self.cs])
            with ExitStack() as st:
                sg = p.sb(st, [128, KC, 2], F32, "sg")
                p.act(sg[:], self.cs[:], AF.Sigmoid, [self.cs], [sg], small=True)
                p.tt("dve", self.cs[:], self.cs[:], sg[:], ALU.mult, [self.cs, sg], [self.cs], small=True)
                p.barrier()
            S, T = self.S, self.T
            lat_tiles = [(t0, 512, 0) for t0 in range(0, S, 512)]
            ctx_tile = (S, L_CTX, 1)
            ph = phases
            if ph is None or "wconv" in ph:
                self.phase_wconv()
            if ph is None or "tin" in ph:
                self.phase_tin()
            for l in range(self.NL):
                last = l == self.NL - 1
                if ph is None or "mod" in ph:
                    self.phase_mod(l)
                if ph is None or "A" in ph:
                    self.phase_A(l, lat_tiles + [ctx_tile], last)
                tl = lat_tiles + ([] if last else [ctx_tile])
                if ph is None or "NA" in ph:
                    self.phase_NA(l, tl)
                if ph is None or "CF" in ph:
                    self.phase_conformer(l, tl)
                if ph is None or "DF" in ph:
                    self.phase_diff(l, tl)
                if ph is None or "MG" in ph:
                    self.phase_merge(l, tl)
                if ph is None or "C1" in ph:
                    self.phase_ffn_up(l, tl)
                if ph is None or "C2" in ph:
                    self.phase_ffn_down(l, tl)
            if ph is None or "tout" in ph:
                self.phase_tout()
            p.barrier()
        return nc

    def phase_wconv(self):
        p = self.p
        for l in range(self.NL):
            def cv(dst, src, kcn, ng, cols=512):
                v = src.rearrange("(kc p) (g c) -> g p kc c", p=128, c=cols)
                for g in range(ng):
                    p.dma("pool", dst[g], v[g])
            cv(self.Wi[l], self.w_in[l], KC, 28)
            cv(self.Pa[l], self.p_a[l], 8, 4)
            cv(self.Pb[l], self.p_b[l], 8, 4)
            cv(self.Pc[l], self.p_c[l], 8, 4)
            cv(self.Wo[l], self.w_out[l], KC, 4)
            cv(self.Wu[l], self.w_up[l], KC, 22)
            cv(self.Wd[l], self.w_down[l], 44, 16, cols=128)
        p.barrier()

    def phase_tin(self):
        p = self.p
        S = self.S
        with ExitStack() as st:
            xin = p.sbpool(st, 8, [128, D], F32, "xin")
            xo = p.sbpool(st, 4, [128, 512], F32, "xo")
            tiles = [(self.x, t0, 512, t0) for t0 in range(0, S, 512)] + [(self.ctx, 0, L_CTX, S)]
            cnt = 0
            for (src, r0, n, c0) in tiles:
                nb = n // 128
                xs = []
                for i in range(nb):
                    b = xin.next()
                    p.dma("sp", b[:], src[r0 + i * 128: r0 + (i + 1) * 128, :], writes=[b])
                    xs.append(b)
                for fc in range(KC):
                    ps = self.psum.next()
                    for i in range(nb):
                        p.op("pe", lambda h, i=i, fc=fc, ps=ps: h.transpose(
                            ps[:, i * 128:(i + 1) * 128], xs[i][:, fc * 128:(fc + 1) * 128], self.ident[:]),
                            [xs[i], self.ident], [ps])
                    o = xo.next()
                    eng = "act" if cnt % 2 == 0 else "dve"
                    cnt += 1
                    p.copy(eng, o[:, :n], ps[:, :n], [ps], [o])
                    p.dma("pool", self.xT[fc * 128:(fc + 1) * 128, c0:c0 + n], o[:, :n], reads=[o])
            p.barrier()

    def phase_tout(self):
        p = self.p
        S = self.S
        xTv = self.xT.rearrange("(kc p) t -> p kc t", p=128)
        with ExitStack() as st:
            xi = p.sbpool(st, 2, [128, KC, 512], F32, "xi")
            xo = p.sbpool(st, 3, [128, D], F32, "xo2")
            cnt = 0
            for t0 in range(0, S, 512):
                b = xi.next()
                p.dma("sp", b[:], xTv[:, :, t0:t0 + 512], writes=[b])
                for i in range(4):
                    o = xo.next()
                    for f4 in range(4):
                        ps = self.psum.next()
                        for k in range(4):
                            fc = f4 * 4 + k
                            p.op("pe", lambda h, ps=ps, k=k, fc=fc, i=i, b=b: h.transpose(
                                ps[:, k * 128:(k + 1) * 128], b[:, fc, i * 128:(i + 1) * 128], self.ident[:]),
                                [b, self.ident], [ps])
                        eng = "act" if cnt % 2 == 0 else "dve"
                        cnt += 1
                        p.copy(eng, o[:, f4 * 512:(f4 + 1) * 512], ps[:], [ps], [o])
                    p.dma("pool", self.out[t0 + i * 128: t0 + (i + 1) * 128, :], o[:], reads=[o])
            p.barrier()

    def phase_mod(self, l):
        p = self.p
        with ExitStack() as st:
            wm = p.sbpool(st, 2, [128, KC, 512], F32, "wm")
            modv = p.sb(st, [128, 96, 2], F32, "modv")
            bm = p.sb(st, [128, 96], F32, "bm")
            gv = p.sb(st, [128, 4, KC], F32, "gv")
            lv = p.sb(st, [128, 4, 64], F32, "lv")
            lt = p.sb(st, [128, 2, 64], F32, "lt")
            ls = p.sb(st, [128, 2], F32, "ls")
            p.dma("sp", bm[:], self.b_mod[l], writes=[bm])
            p.dma("sp", gv[:], self.gvec[l], writes=[gv])
            p.dma("sp", self.dlgs[:], self.dlg[l], writes=[self.dlgs])
            p.dma("sp", lv[:], self.lamv[l], writes=[lv])
            wv = self.w_mod[l].rearrange("(kc p) (g c) -> g p kc c", p=128, c=512)
            ps = self.psum.next()
            for g in range(24):
                w = wm.next()
                p.dma("sp", w[:], wv[g], writes=[w])
                for j in range(4):
                    oc = g * 4 + j
                    for kc in range(KC):
                        p.mm(ps[:, oc * 2:oc * 2 + 2], w[:, kc, j * 128:(j + 1) * 128], self.cs[:, kc, :],
                             kc == 0, kc == KC - 1, [w, self.cs], [ps])
            psv = ps[:, 0:192].rearrange("p (o n) -> p o n", n=2)
            for n in range(2):
                p.tt("dve", modv[:, :, n], psv[:, :, n], bm[:], ALU.add, [ps, bm], [modv], small=True)
            sD = math.sqrt(D)
            mc = self.modc
            for n in range(2):
                m = lambda j: modv[:, j * 16:(j + 1) * 16, n]
                p.stt("dve", mc[:, 0, n, :], m(1), 1.0, gv[:, 0, :], ALU.add, ALU.mult, [modv, gv], [mc], small=True)
                p.ts("dve", mc[:, 0, n, :], mc[:, 0, n, :], sD, None, ALU.mult, None, [mc], [mc], small=True)
                p.copy("dve", mc[:, 1, n, :], m(0), [modv], [mc], small=True)
                p.stt("dve", mc[:, 2, n, :], m(2), sD, gv[:, 1, :], ALU.mult, ALU.mult, [modv, gv], [mc], small=True)
                p.stt("dve", mc[:, 3, n, :], m(4), 1.0, gv[:, 2, :], ALU.add, ALU.mult, [modv, gv], [mc], small=True)
                p.ts("dve", mc[:, 3, n, :], mc[:, 3, n, :], sD, None, ALU.mult, None, [mc], [mc], small=True)
                p.copy("dve", mc[:, 4, n, :], m(3), [modv], [mc], small=True)
                p.stt("dve", mc[:, 5, n, :], m(5), sD, gv[:, 3, :], ALU.mult, ALU.mult, [modv, gv], [mc], small=True)
            lam_init = 0.8 - 0.6 * math.exp(-0.3 * l)
            p.tt("dve", lt[:, 0, :], lv[:, 0, :], lv[:, 1, :], ALU.mult, [lv], [lt], small=True)
            p.tt("dve", lt[:, 1, :], lv[:, 2, :], lv[:, 3, :], ALU.mult, [lv], [lt], small=True)
            p.op("dve", lambda h: h.tensor_reduce(ls[:], lt[:], mybir.AxisListType.X, ALU.add), [lt], [ls], small=True)
            p.act(ls[:], ls[:], AF.Exp, [ls], [ls], small=True)
            p.stt("dve", self.lam[:, 0:1], ls[:, 1:2], -lam_init, ls[:, 0:1], ALU.add, ALU.subtract,
                  [ls], [self.lam], small=True)
            p.ts("dve", self.dlgs[:], self.dlgs[:], (1.0 - lam_init) * math.sqrt(128.0), None, ALU.mult, None,
                 [self.dlgs], [self.dlgs], small=True)
            p.barrier()

    def rsqrt(self, out, in_, c, reads, obuf, small=False):
        p = self.p
        p.act(out, in_, AF.Sqrt, reads + [self.cbias], [obuf], bias=self.cbias[:, self.cbias_idx(c):self.cbias_idx(c) + 1], small=small)
        p.op("dve", lambda h: h.reciprocal(out, out), [obuf], [obuf], small=small)

    def cbias_idx(self, c):
        return self.cb_vals.index(c)

    def norm_mod(self, xt, hT, n, ia, ib, m, sq, rs, tmp):
        p = self.p
        ps = self.psum.next()
        for kc in range(KC):
            q = sq.next()
            p.act(q[:, :n], xt[:, kc, :n], AF.Square, [xt], [q])
            p.mm(ps[:, :n], self.ones_bf[:], q[:, :n], kc == 0, kc == KC - 1, [self.ones_bf, q], [ps], signal=True)
        self.rsqrt(rs[:, :n], ps[:, :n], D * EPS, [ps], rs)
        mc = self.modc
        for kc in range(KC):
            t = tmp.next()
            p.stt("dve", t[:, :n], xt[:, kc, :n], mc[:, ia, m, kc:kc + 1], rs[:, :n], ALU.mult, ALU.mult,
                  [xt, mc, rs], [t])
            p.act(hT[:, kc, :n], t[:, :n], AF.Identity, [t, mc], [hT], bias=mc[:, ib, m, kc:kc + 1])

    def phase_A(self, l, tiles, last):
        p = self.p
        S = self.S
        xTv = self.xT.rearrange("(kc p) t -> p kc t", p=128)
        with ExitStack() as st:
            xpool = p.sbpool(st, 2, [128, KC, 512], F32, "xa")
            sq = p.sbpool(st, 2, [128, 512], BF16, "sq")
            rs = p.sb(st, [128, 512], F32, "rs")
            tmp = p.sbpool(st, 2, [128, 512], F32, "tmpa")
            hpool = p.sbpool(st, 2, [128, KC, 512], BF16, "hT")
            wpool = p.sbpool(st, 3, [128, KC, 512], BF16, "wa")
            abuf = p.sb(st, [128, 8, 512], F32, "abuf")
            stg = p.sbpool(st, 4, [128, 512], BF16, "stg")
            f32t = p.sbpool(st, 3, [128, 512], F32, "f32t")
            rc = p.sbpool(st, 2, [128, 512], F32, "rc")
            rsn = p.sbpool(st, 2, [128, 512], F32, "rsn")
            cnt = 0
            def prep(tile):
                t0_, n_, m_ = tile
                xt_ = xpool.next()
                p.dma("sp", xt_[:, :, :n_], xTv[:, :, t0_:t0_ + n_], writes=[xt_])
                hT_ = hpool.next()
                self.norm_mod(xt_, hT_, n_, 0, 1, m_, sq, rs, tmp)
                return hT_
            hT_next = prep(tiles[0])
            for ti, (t0, n, m) in enumerate(tiles):
                hT = hT_next
                if not m:
                    rcb, rsb = rc.next(), rsn.next()
                    p.dma("sp", rcb[:, :n], self.ropeC[:, t0:t0 + n], writes=[rcb])
                    p.dma("sp", rsb[:, :n], self.ropeS[:, t0:t0 + n], writes=[rsb])
                groups = range(28)
                import os
                if os.environ.get("KGROUPS"):
                    groups = [int(v) for v in os.environ["KGROUPS"].split(",")]
                if m and last:
                    groups = [2, 3, 4, 5, 12, 13, 14, 15]
                groups = list(groups)
                for gi, g in enumerate(groups):
                    if gi == len(groups) // 2 and ti + 1 < len(tiles):
                        hT_next = prep(tiles[ti + 1])
                    w = wpool.next()
                    p.dma("sp", w[:], self.Wi[l, g], writes=[w])
                    if g in (4, 5, 14, 15):
                        dst = self.NV if g < 6 else self.DV
                        hh = (g - 4) * 4 if g < 6 else (g - 14) * 4
                        for tb in range(n // 128):
                            ps = self.psum.next()
                            for kc in range(KC):
                                p.mm(ps[:, :], hT[:, kc, tb * 128:(tb + 1) * 128], w[:, kc, :], kc == 0, kc == KC - 1,
                                     [hT, w], [ps])
                            o = stg.next()
                            eng = "act" if cnt % 2 == 0 else "dve"
                            cnt += 1
                            p.copy(eng, o[:], ps[:], [ps], [o])
                            blk = (t0 + tb * 128) // 128
                            p.dma("pool", dst[hh:hh + 4, :, blk, :].rearrange("h p c -> p h c"),
                                  o[:].rearrange("p (h c) -> p h c", c=128), reads=[o])
                        continue
                    for j in range(4):
                        c = g * 4 + j
                        ps = self.psum.next()
                        for kc in range(KC):
                            p.mm(ps[:, :n], w[:, kc, j * 128:(j + 1) * 128], hT[:, kc, :n], kc == 0, kc == KC - 1,
                                 [w, hT], [ps])
                        if c < 16:
                            dst = self.NQ if c < 8 else self.NK
                            o = stg.next()
                            eng = "act" if cnt % 2 == 0 else "dve"
                            cnt += 1
                            p.copy(eng, o[:, :n], ps[:, :n], [ps], [o])
                            r = (c % 8) * 128
                            p.dma("pool", dst[r:r + 128, t0:t0 + n], o[:, :n], reads=[o])
                        elif c < 32:
                            p.copy("act", abuf[:, c - 24, :n], ps[:, :n], [ps], [abuf])
                        elif c < 40:
                            f = f32t.next()
                            p.act(f[:, :n], ps[:, :n], AF.Sigmoid, [ps], [f])
                            o = stg.next()
                            p.tt("dve", o[:, :n], abuf[:, c - 32, :n], f[:, :n], ALU.mult, [abuf, f], [o])
                            r = (c - 32) * 128
                            p.dma("pool", self.GLU[r:r + 128, t0:t0 + n], o[:, :n], reads=[o])
                        elif c < 56:
                            dst = self.DQ if c < 48 else self.DK
                            r = (c % 8) * 128
                            xb = stg.next()
                            p.copy("act", xb[:, :n], ps[:, :n], [ps], [xb])
                            if m:
                                p.dma("pool", dst[r:r + 128, t0:t0 + n], xb[:, :n], reads=[xb])
                            else:
                                ps2 = self.psum.next()
                                p.mm(ps2[:, :n], self.perm[:], xb[:, :n], True, True, [self.perm, xb], [ps2])
                                f1 = f32t.next()
                                p.tt("dve", f1[:, :n], ps2[:, :n], rsb[:, :n], ALU.mult, [ps2, rsb], [f1])
                                f2 = f32t.next()
                                p.tt("dve", f2[:, :n], xb[:, :n], rcb[:, :n], ALU.mult, [xb, rcb], [f2])
                                o = stg.next()
                                p.tt("dve", o[:, :n], f1[:, :n], f2[:, :n], ALU.add, [f1, f2], [o])
                                p.dma("pool", dst[r:r + 128, t0:t0 + n], o[:, :n], reads=[o])
                        else:
                            o = stg.next()
                            p.act(o[:, :n], ps[:, :n], AF.Sigmoid, [ps], [o])
                            r = (c - 64) * 128
                            p.dma("pool", self.G[r:r + 128, t0:t0 + n], o[:, :n], reads=[o])
            p.barrier()

    def phase_NA(self, l, tiles):
        p = self.p
        S, T = self.S, self.T
        nblk = S // 128
        scale = 128 ** -0.5
        NKv = self.NK.rearrange("(h p) t -> p h t", p=128)
        NQv = self.NQ.rearrange("(h p) t -> p h t", p=128)
        OAv = self.OA.rearrange("(h p) t -> p h t", p=128)
        with ExitStack() as st:
            kc_sb = p.sb(st, [128, NH, 256], BF16, "kcs")
            vc_sb = p.sb(st, [128, NH, 2, 128], BF16, "vcs")
            p.dma("sp", kc_sb[:], NKv[:, :, S:T], writes=[kc_sb])
            p.dma("sp", vc_sb[:], self.NV[:, :, nblk:nblk + 2, :].rearrange("h p b c -> p h b c"), writes=[vc_sb])
            kpool = p.sbpool(st, 2, [128, NH, 1024], BF16, "nak")
            vpool = p.sbpool(st, 2, [128, NH, 8, 128], BF16, "nav")
            qpool = p.sbpool(st, 2, [128, NH, 512], BF16, "naq")
            tabp = p.sbpool(st, 3, [128, 640], F32, "tab")
            sbp = p.sbpool(st, 2, [128, 896], F32, "nsb")
            ep = p.sbpool(st, 2, [128, 896], BF16, "nae")
            rzp = p.sbpool(st, 2, [128, 128], F32, "rz")
            oap = p.sbpool(st, 2, [128, NH, 512], BF16, "oat")
            tab0 = p.sb(st, [128, NH, 640], F32, "tab0")
            p.dma("sp", tab0[:], self.na_tab[l, :, 0].rearrange("h p c -> p h c"), writes=[tab0])
            for (t0, n, m) in tiles:
                qt = qpool.next()
                p.dma("sp", qt[:, :, :n], NQv[:, :, t0:t0 + n], writes=[qt])
                oa = oap.next()
                if not m:
                    j0 = t0 // 128
                    lo = kb0_of(j0, nblk)
                    hi = kb0_of(j0 + 3, nblk) + 5
                    nb = hi - lo
                    kt = kpool.next()
                    p.dma("sp", kt[:, :, :nb * 128], NKv[:, :, lo * 128:hi * 128], writes=[kt])
                    vt = vpool.next()
                    p.dma("sp", vt[:, :, :nb, :], self.NV[:, :, lo:hi, :].rearrange("h p b c -> p h b c"), writes=[vt])
                for h in range(NH):
                    for jj in range(n // 128):
                        q_ap = qt[:, h, jj * 128:(jj + 1) * 128]
                        e = ep.next()
                        if not m:
                            j = j0 + jj
                            kb = kb0_of(j, nblk) - lo
                            vr = variant_of(j, nblk)
                            if vr == 0:
                                tb = tab0
                                tbv = tab0[:, h, :]
                            else:
                                tb = tabp.next()
                                p.dma("sp", tb[:], self.na_tab[l, h, vr], writes=[tb])
                                tbv = tb[:]
                            psA = self.psum.next()
                            psB = self.psum.next()
                            for i in range(5):
                                dst = psA[:, i * 128:(i + 1) * 128] if i < 4 else psB[:, 0:128]
                                p.mm(dst, kt[:, h, (kb + i) * 128:(kb + i + 1) * 128], q_ap, True, True,
                                     [kt, qt], [psA if i < 4 else psB])
                            for i in range(2):
                                p.mm(psB[:, (1 + i) * 128:(2 + i) * 128], kc_sb[:, h, i * 128:(i + 1) * 128], q_ap,
                                     True, True, [kc_sb, qt], [psB])
                            sb = sbp.next()
                            p.stt("dve", sb[:, 0:512], psA[:, :], scale, tbv[:, 0:512], ALU.mult, ALU.add, [psA, tb], [sb])
                            p.stt("dve", sb[:, 512:640], psB[:, 0:128], scale, tbv[:, 512:640], ALU.mult, ALU.add,
                                  [psB, tb], [sb])
                            p.ts("dve", sb[:, 640:896], psB[:, 128:384], scale, None, ALU.mult, None, [psB], [sb])
                            p.act(e[:, 0:896], sb[:], AF.Exp, [sb], [e])
                            nkb = 7
                            lhs_v = lambda i: vt[:, h, kb + i, :] if i < 5 else vc_sb[:, h, i - 5, :]
                            vreads = [vt, vc_sb]
                        else:
                            psB = self.psum.next()
                            for i in range(2):
                                p.mm(psB[:, i * 128:(i + 1) * 128], kc_sb[:, h, i * 128:(i + 1) * 128], q_ap,
                                     True, True, [kc_sb, qt], [psB])
                            p.act(e[:, 0:256], psB[:, 0:256], AF.Exp, [psB], [e], scale=scale)
                            nkb = 2
                            lhs_v = lambda i: vc_sb[:, h, i, :]
                            vreads = [vc_sb]
                        psO = self.psum.next()
                        psZ = self.psum.next()
                        for i in range(nkb):
                            p.mm(psO[:, 0:128], lhs_v(i), e[:, i * 128:(i + 1) * 128], i == 0, i == nkb - 1,
                                 vreads + [e], [psO])
                            p.mm(psZ[:, 0:128], self.ones_bf[:], e[:, i * 128:(i + 1) * 128], i == 0, i == nkb - 1,
                                 [self.ones_bf, e], [psZ])
                        rz = rzp.next()
                        p.op("dve", lambda hh: hh.reciprocal(rz[:], psZ[:, 0:128]), [psZ], [rz])
                        p.tt("dve", oa[:, h, jj * 128:(jj + 1) * 128], psO[:, 0:128], rz[:], ALU.mult, [psO, rz], [oa])
                p.dma("pool", OAv[:, :, t0:t0 + n], oa[:, :, :n], reads=[oa])
            p.barrier()

    def phase_conformer(self, l, tiles):
        p = self.p
        S, T = self.S, self.T
        OBv = self.OB.rearrange("(c p) t -> p c t", p=128)
        with ExitStack() as st:
            cw = p.sb(st, [128, 8, 31], F32, "cw")
            cvv = p.sb(st, [128, 3, 8], F32, "cvv")
            p.dma("sp", cw[:], self.conv_w[l], writes=[cw])
            p.dma("sp", cvv[:], self.conv_v[l], writes=[cvv])
            glp = p.sbpool(st, 5, [128, 544], BF16, "gl")
            cps = Pool(self.psum.bufs[0:6])
            identb = p.sb(st, [128, 128], BF16, "identb")
            p.copy("dve", identb[:], self.ident[:], [self.ident], [identb])
            dg = p.sb(st, [128, 8, 31, 128], BF16, "dg")
            for i in range(8):
                for j in range(31):
                    p.ts("dve", dg[:, i, j, :], identb[:], cw[:, i, j:j + 1], None, ALU.mult, None, [identb, cw], [dg])
            acc = [p.sb(st, [128, 512], F32, "cacc") for _ in range(8)]
            sqp = p.sbpool(st, 2, [128, 512], F32, "csq")
            mu = p.sb(st, [128, 512], F32, "mu")
            var = p.sb(st, [128, 512], F32, "var")
            rstd = p.sb(st, [128, 512], F32, "rstd")
            tp = p.sbpool(st, 3, [128, 512], F32, "ct")
            obp = p.sbpool(st, 2, [128, 8, 512], BF16, "obt")
            for (t0, n, m) in tiles:
                seq_lo, seq_hi = (S, T) if m else (0, S)
                lo = max(t0 - 15, seq_lo)
                hi = min(t0 + n + 15, seq_hi)
                off = lo - (t0 - 15)
                psM = self.psum.bufs[6]
                psQ = self.psum.bufs[7]
                for i in range(8):
                    gl = glp.next()
                    if off > 0:
                        p.op("dve", lambda hh: hh.memset(gl[:, 0:off], 0.0), (), [gl], small=True)
                    if off + hi - lo < n + 30:
                        p.op("dve", lambda hh: hh.memset(gl[:, off + hi - lo:n + 30], 0.0), (), [gl], small=True)
                    p.dma("sp", gl[:, off:off + hi - lo], self.GLU[i * 128:(i + 1) * 128, lo:hi], writes=[gl])
                    a = acc[i]
                    psc = cps.next()
                    for j in range(31):
                        p.mm(psc[:, :n], dg[:, i, j, :], gl[:, j:j + n], j == 0, j == 30, [dg, gl], [psc])
                    p.act(a[:, :n], psc[:, :n], AF.Identity, [psc, cvv], [a], bias=cvv[:, 0, i:i + 1])
                    sq = sqp.next()
                    p.act(sq[:, :n], a[:, :n], AF.Square, [a], [sq])
                    p.mm(psM[:, :n], self.ones_f[:], a[:, :n], i == 0, i == 7, [self.ones_f, a], [psM], signal=True)
                    p.mm(psQ[:, :n], self.ones_f[:], sq[:, :n], i == 0, i == 7, [self.ones_f, sq], [psQ], signal=True)
                p.ts("dve", mu[:, :n], psM[:, :n], 1.0 / 1024, None, ALU.mult, None, [psM], [mu])
                p.tt("dve", var[:, :n], mu[:, :n], mu[:, :n], ALU.mult, [mu], [var])
                p.stt("dve", var[:, :n], psQ[:, :n], 1.0 / 1024, var[:, :n], ALU.mult, ALU.subtract, [psQ, var], [var])
                self.rsqrt(rstd[:, :n], var[:, :n], EPS, [var], rstd)
                ob = obp.next()
                for i in range(8):
                    t = tp.next()
                    p.tt("dve", t[:, :n], acc[i][:, :n], mu[:, :n], ALU.subtract, [acc[i], mu], [t])
                    p.tt("dve", t[:, :n], t[:, :n], rstd[:, :n], ALU.mult, [t, rstd], [t])
                    p.act(ob[:, i, :n], t[:, :n], AF.Silu, [t, cvv], [ob], scale=cvv[:, 1, i:i + 1], bias=cvv[:, 2, i:i + 1])
                p.dma("pool", OBv[:, :, t0:t0 + n], ob[:, :, :n], reads=[ob])
            p.barrier()

    def phase_diff(self, l, tiles):
        p = self.p
        S, T = self.S, self.T
        scale = 64 ** -0.5
        DQv = self.DQ.rearrange("(h p) t -> p h t", p=128)
        OCv = self.OC.rearrange("(h p) t -> p h t", p=128)
        with ExitStack() as st:
            qpool = p.sbpool(st, 2, [128, NH, 512], BF16, "dq")
            kpool = p.sbpool(st, 2, [128, T], BF16, "dk")
            vpool = p.sbpool(st, 2, [128, T // 128, 128], BF16, "dv")
            ocp = p.sbpool(st, 2, [128, NH, 512], BF16, "oct")
            fp = p.sbpool(st, 6, [128, 512], F32, "df")
            acc = self.psum.bufs[0:2]
            pairs = [(self.psT[i], self.psum.bufs[2 * i], self.psum.bufs[2 * i + 1]) for i in (1, 2, 3)]
            pi = 0
            z = p.sb(st, [128, 2, 512], F32, "z")
            ep = p.sbpool(st, 8, [128, 2, 512], BF16, "e12")
            esp = p.sbpool(st, 2, [128, 2, 512], BF16, "esum")
            for (t0, n, m) in tiles:
                qt = qpool.next()
                p.dma("sp", qt[:, :, :n], DQv[:, :, t0:t0 + n], writes=[qt])
                kts = list(range(S // 128, T // 128)) if m else list(range(T // 128))
                oc = ocp.next()
                for h in range(NH):
                    kb = kpool.next()
                    p.dma("sp", kb[:], self.DK[h * 128:(h + 1) * 128, :], writes=[kb])
                    vb = vpool.next()
                    p.dma("sp", vb[:], self.DV[h], writes=[vb])
                    O1, O2 = acc

                    pend = []

                    def pv(e, kt, first, lastk):
                        p.mm(O1[:, :n], vb[:, kt, :], e[:, 0, :n], first, lastk, [vb, e], [O1], signal=False)
                        p.mm(O2[:, :n], vb[:, kt, :], e[:, 1, :n], first, lastk, [vb, e], [O2], signal=True)
                        pend.append((e, first))
                        if len(pend) == 2:
                            (ea, fa), (eb, _) = pend
                            del pend[:]
                            es = esp.next()
                            p.tt("dve", es[:, :, :n], ea[:, :, :n], eb[:, :, :n], ALU.add, [ea, eb], [es])
                            if fa:
                                p.copy("dve", z[:, :, :n], es[:, :, :n], [es], [z])
                            else:
                                p.tt("dve", z[:, :, :n], z[:, :, :n], es[:, :, :n], ALU.add, [z, es], [z], strict=False)
                    prev = []
                    for idx, kt in enumerate(kts):
                        TT, s1, s2 = pairs[pi % 3]
                        pi += 1
                        p.mm(s1[:, :n], kb[0:64, kt * 128:(kt + 1) * 128], qt[0:64, h, :n], True, True, [kb, qt], [s1],
                             signal=False)
                        p.mm(s2[:, :n], kb[64:128, kt * 128:(kt + 1) * 128], qt[64:128, h, :n], True, True, [kb, qt], [s2])
                        e = ep.next()
                        p.act(e[:, :, :n], TT.t[:].rearrange("p (b c) -> p b c", b=2)[:, :, :n], AF.Exp, [s1, s2], [e],
                              scale=scale)
                        prev.append((e, kt, idx == 0, idx == len(kts) - 1))
                        if len(prev) > 2:
                            pv(*prev.pop(0))
                    while prev:
                        pv(*prev.pop(0))
                    z1 = z[:, 0, :]
                    z2 = z[:, 1, :]
                    z1b = z2b = z
                    _, Z1, Z2 = pairs[pi % 3]
                    pi += 1
                    p.mm(Z1[:, :n], self.ones_f[:], z1[:, :n], True, True, [self.ones_f, z], [Z1])
                    p.mm(Z2[:, :n], self.ones_f[:], z2[:, :n], True, True, [self.ones_f, z], [Z2])
                    r1 = fp.next()
                    p.op("dve", lambda hh: hh.reciprocal(r1[:, :n], Z1[:, :n]), [Z1], [r1])
                    t1 = fp.next()
                    p.tt("dve", t1[:, :n], O1[:, :n], r1[:, :n], ALU.mult, [O1, r1], [t1])
                    r2 = fp.next()
                    p.op("dve", lambda hh: hh.reciprocal(r2[:, :n], Z2[:, :n]), [Z2], [r2])
                    t2 = fp.next()
                    p.tt("dve", t2[:, :n], O2[:, :n], r2[:, :n], ALU.mult, [O2, r2], [t2])
                    p.stt("dve", t1[:, :n], t2[:, :n], self.lam[:, 0:1], t1[:, :n], ALU.mult, ALU.add,
                          [t2, self.lam, t1], [t1])
                    p.act(r1[:, :n], t1[:, :n], AF.Square, [t1], [r1])
                    _, psS, _unused = pairs[pi % 3]
                    pi += 1
                    p.mm(psS[:, :n], self.ones_f[:], r1[:, :n], True, True, [self.ones_f, r1], [psS])
                    self.rsqrt(r2[:, :n], psS[:, :n], 128 * EPS, [psS], r2)
                    p.stt("dve", oc[:, h, :n], t1[:, :n], self.dlgs[:, 0:1], r2[:, :n], ALU.mult, ALU.mult,
                          [t1, self.dlgs, r2], [oc])
                p.dma("pool", OCv[:, :, t0:t0 + n], oc[:, :, :n], reads=[oc])
            p.barrier()

    def post_residual(self, mix, pss, rs, n, t0, m, ig, xp, tp):
        p = self.p
        mc = self.modc
        self.rsqrt(rs[:, :n], pss[:, :n], D * EPS, [pss], rs)
        for kc in range(KC):
            xt = xp.next()
            p.dma("sp", xt[:, :n], self.xT[kc * 128:(kc + 1) * 128, t0:t0 + n], writes=[xt])
            t = tp.next()
            p.stt("dve", t[:, :n], mix[:, kc, :n], mc[:, ig, m, kc:kc + 1], rs[:, :n], ALU.mult, ALU.mult,
                  [mix, mc, rs], [t])
            p.tt("dve", xt[:, :n], xt[:, :n], t[:, :n], ALU.add, [xt, t], [xt])
            p.dma("pool", self.xT[kc * 128:(kc + 1) * 128, t0:t0 + n], xt[:, :n], reads=[xt])

    def phase_merge(self, l, tiles):
        p = self.p
        Gv = self.G.rearrange("(br c p) t -> p br c t", p=128, c=16)
        with ExitStack() as st:
            bp = [p.sbpool(st, 1, [128, 8, 512], BF16, "mb%d" % i) for i in range(3)]
            wp = [p.sbpool(st, 2, [128, 8, 512], BF16, "mw%d" % i) for i in range(3)]
            gtp = p.sbpool(st, 3, [128, 3, 512], BF16, "gt")
            tp = p.sbpool(st, 6, [128, 512], F32, "mt")
            y = p.sb(st, [128, KC, 512], BF16, "y")
            wop = p.sbpool(st, 2, [128, KC, 512], BF16, "wo")
            mix = p.sb(st, [128, KC, 512], F32, "mix")
            sq = p.sbpool(st, 2, [128, 512], BF16, "msq")
            rs = p.sb(st, [128, 512], F32, "mrs")
            xp = p.sbpool(st, 3, [128, 512], F32, "mx")
            srcs = [self.OA, self.OB, self.OC]
            Ws = [self.Pa, self.Pb, self.Pc]
            full_psum = self.psum
            pss = full_psum.bufs[7]
            self.psum = Pool(full_psum.bufs[0:7])
            for (t0, n, m) in tiles:
                br = []
                for i in range(3):
                    b = bp[i].next()
                    p.dma("sp", b[:, :, :n], srcs[i].rearrange("(c p) t -> p c t", p=128)[:, :, t0:t0 + n], writes=[b])
                    br.append(b)
                for og in range(4):
                    ws = []
                    for i in range(3):
                        w = wp[i].next()
                        p.dma("sp", w[:], Ws[i][l, og], writes=[w])
                        ws.append(w)
                    for j in range(4):
                        c = og * 4 + j
                        gt = gtp.next()
                        p.dma("sp", gt[:, :, :n], Gv[:, :, c, t0:t0 + n], writes=[gt])
                        ts_ = []
                        for i in range(3):
                            ps = self.psum.next()
                            for kc in range(8):
                                p.mm(ps[:, :n], ws[i][:, kc, j * 128:(j + 1) * 128], br[i][:, kc, :n], kc == 0, kc == 7,
                                     [ws[i], br[i]], [ps])
                            t = tp.next()
                            p.tt("dve", t[:, :n], ps[:, :n], gt[:, i, :n], ALU.mult, [ps, gt], [t])
                            ts_.append(t)
                        p.tt("dve", ts_[0][:, :n], ts_[0][:, :n], ts_[1][:, :n], ALU.add, [ts_[0], ts_[1]], [ts_[0]])
                        p.tt("dve", y[:, c, :n], ts_[0][:, :n], ts_[2][:, :n], ALU.add, [ts_[0], ts_[2]], [y])
                for og in range(4):
                    wo = wop.next()
                    p.dma("sp", wo[:], self.Wo[l, og], writes=[wo])
                    for j in range(4):
                        c = og * 4 + j
                        ps = self.psum.next()
                        for kc in range(KC):
                            p.mm(ps[:, :n], wo[:, kc, j * 128:(j + 1) * 128], y[:, kc, :n], kc == 0, kc == KC - 1,
                                 [wo, y], [ps])
                        p.copy("dve", mix[:, c, :n], ps[:, :n], [ps], [mix])
                        q = sq.next()
                        p.act(q[:, :n], mix[:, c, :n], AF.Square, [mix], [q])
                        p.mm(pss[:, :n], self.ones_bf[:], q[:, :n], c == 0, c == KC - 1, [self.ones_bf, q], [pss], signal=True)
                self.post_residual(mix, pss, rs, n, t0, m, 2, xp, tp)
            self.psum = full_psum
            p.barrier()

    def phase_ffn_up(self, l, tiles):
        p = self.p
        xTv = self.xT.rearrange("(kc p) t -> p kc t", p=128)
        with ExitStack() as st:
            xpool = p.sbpool(st, 2, [128, KC, 512], F32, "xu")
            sq = p.sbpool(st, 2, [128, 512], BF16, "usq")
            rs = p.sb(st, [128, 512], F32, "urs")
            tmp = p.sbpool(st, 2, [128, 512], F32, "utmp")
            hpool = p.sbpool(st, 2, [128, KC, 512], BF16, "uh")
            wpool = p.sbpool(st, 3, [128, KC, 512], BF16, "uw")
            stg = p.sbpool(st, 4, [128, 512], BF16, "ustg")
            cnt = 0
            def prep(tile):
                t0_, n_, m_ = tile
                xt_ = xpool.next()
                p.dma("sp", xt_[:, :, :n_], xTv[:, :, t0_:t0_ + n_], writes=[xt_])
                hT_ = hpool.next()
                self.norm_mod(xt_, hT_, n_, 3, 4, m_, sq, rs, tmp)
                return hT_
            hT_next = prep(tiles[0])
            for ti, (t0, n, m) in enumerate(tiles):
                hT = hT_next
                for g in range(22):
                    if g == 11 and ti + 1 < len(tiles):
                        hT_next = prep(tiles[ti + 1])
                    w = wpool.next()
                    p.dma("sp", w[:], self.Wu[l, g], writes=[w])
                    for j in range(4):
                        c = g * 4 + j
                        ps = self.psum.next()
                        for kc in range(KC):
                            p.mm(ps[:, :n], w[:, kc, j * 128:(j + 1) * 128], hT[:, kc, :n], kc == 0, kc == KC - 1,
                                 [w, hT], [ps])
                        o = stg.next()
                        eng = "act" if cnt % 2 == 0 else "dve"
                        cnt += 1
                        p.copy(eng, o[:, :n], ps[:, :n], [ps], [o])
                        p.dma("pool", self.U[c * 128:(c + 1) * 128, t0:t0 + n], o[:, :n], reads=[o])
            p.barrier()

    def phase_ffn_down(self, l, tiles):
        p = self.p
        S, T = self.S, self.T
        with ExitStack() as st:
            fw = p.sb(st, [128, 88, 3], F32, "fw")
            fb = p.sb(st, [128, 88], F32, "fb")
            p.dma("sp", fw[:], self.fcw[l], writes=[fw])
            p.dma("sp", fb[:], self.fcb[l], writes=[fb])
            up = p.sbpool(st, 12, [128, 516], BF16, "fu")
            ca = p.sbpool(st, 9, [128, 512], F32, "fca")
            gTp = p.sbpool(st, 2, [128, 44, 512], BF16, "gT")
            wdp = p.sbpool(st, 2, [128, 44, 128], BF16, "wd")
            mix = p.sb(st, [128, KC, 512], F32, "fmix")
            sq = p.sbpool(st, 2, [128, 512], BF16, "fsq")
            rs = p.sb(st, [128, 512], F32, "frs")
            xp = p.sbpool(st, 3, [128, 512], F32, "fx")
            tp = p.sbpool(st, 3, [128, 512], F32, "ft")
            full_psum = self.psum
            pss = full_psum.bufs[7]
            self.psum = Pool(full_psum.bufs[0:7])
            def conv_pair(tile, i, gT):
                t0, n, m = tile
                seq_lo, seq_hi = (S, T) if m else (0, S)
                lo = max(t0 - 1, seq_lo)
                hi = min(t0 + n + 1, seq_hi)
                off = lo - (t0 - 1)
                res = []
                for half in range(2):
                    ch = half * 44 + i
                    u = up.next()
                    if off > 0:
                        p.op("dve", lambda hh: hh.memset(u[:, 0:off], 0.0), (), [u], small=True)
                    if off + hi - lo < n + 2:
                        p.op("dve", lambda hh: hh.memset(u[:, off + hi - lo:n + 2], 0.0), (), [u], small=True)
                    p.dma("sp", u[:, off:off + hi - lo], self.U[ch * 128:(ch + 1) * 128, lo:hi], writes=[u])
                    a = ca.next()
                    p.act(a[:, :n], u[:, 1:n + 1], AF.Identity, [u, fw, fb], [a], scale=fw[:, ch, 1:2], bias=fb[:, ch:ch + 1])
                    p.stt("dve", a[:, :n], u[:, 0:n], fw[:, ch, 0:1], a[:, :n], ALU.mult, ALU.add, [u, fw, a], [a])
                    p.stt("dve", a[:, :n], u[:, 2:n + 2], fw[:, ch, 2:3], a[:, :n], ALU.mult, ALU.add, [u, fw, a], [a],
                          strict=False)
                    res.append(a)
                s_ = ca.next()
                p.act(s_[:, :n], res[0][:, :n], AF.Silu, [res[0]], [s_])
                p.tt("dve", gT[:, i, :n], s_[:, :n], res[1][:, :n], ALU.mult, [s_, res[1]], [gT])

            def down_chunk(tile, c, gT):
                t0, n, m = tile
                wd = wdp.next()
                p.dma("sp", wd[:], self.Wd[l, c], writes=[wd])
                ps = self.psum.next()
                for kc in range(44):
                    p.mm(ps[:, :n], wd[:, kc, :], gT[:, kc, :n], kc == 0, kc == 43, [wd, gT], [ps])
                p.copy("act", mix[:, c, :n], ps[:, :n], [ps], [mix])
                q = sq.next()
                p.act(q[:, :n], mix[:, c, :n], AF.Square, [mix], [q])
                p.mm(pss[:, :n], self.ones_bf[:], q[:, :n], c == 0, c == KC - 1, [self.ones_bf, q], [pss], signal=True)

            gT_cur = gTp.next()
            for i in range(44):
                conv_pair(tiles[0], i, gT_cur)
            for ti, tile in enumerate(tiles):
                nxt = tiles[ti + 1] if ti + 1 < len(tiles) else None
                gT_nxt = gTp.next() if nxt is not None else None
                for c in range(16):
                    if nxt is not None:
                        for i in range(c * 44 // 16, (c + 1) * 44 // 16):
                            conv_pair(nxt, i, gT_nxt)
                    down_chunk(tile, c, gT_cur)
                self.post_residual(mix, pss, rs, tile[1], tile[0], tile[2], 5, xp, tp)
                gT_cur = gT_nxt
            self.psum = full_psum
            p.barrier()


def _col(v, nch):
    v = np.asarray(v, np.float32)
    return np.ascontiguousarray(np.swapaxes(v.reshape(v.shape[:-1] + (nch, 128)), -1, -2))


def _na_tables(na_rpb, nrows):
    NL = na_rpb.shape[0]
    nblk = nrows // 2
    wr, wc = 8, 16
    col = np.arange(GRID_W)
    c0 = np.clip(col - wc // 2, 0, GRID_W - wc)
    rep = {0: min(2, nblk - 3), 1: 0, 2: 1, 3: nblk - 2, 4: nblk - 1}
    tab = np.full((NL, NH, 5, 128, 640), NEG, np.float32)
    for v, j in rep.items():
        kb0 = kb0_of(j, nblk)
        for qi in range(128):
            r = 2 * j + qi // 64
            c = qi % 64
            r0 = min(max(r - wr // 2, 0), nrows - wr)
            for i in range(5):
                for kr in range(2):
                    rr = (kb0 + i) * 2 + kr
                    if rr < r0 or rr >= r0 + wr:
                        continue
                    cc = np.arange(c0[c], c0[c] + wc)
                    tab[:, :, v, kr * 64 + cc, i * 128 + qi] = na_rpb[:, :, rr - r + 7, :][:, :, cc - c + 15]
    return tab


def _rope_tables(S):
    t = np.arange(S)
    rows, cols = t // GRID_W, t % GRID_W
    inv = (10000.0 ** (-np.arange(0, 32, 2, dtype=np.float32) / 32)).astype(np.float32)
    C = np.zeros((128, S), np.float32)
    Sn = np.zeros((128, S), np.float32)
    perm = np.zeros((128, 128), np.float32)
    for pp in range(128):
        sub = pp % 64
        part = sub // 32
        i = sub % 16
        pos = (rows if part == 0 else cols).astype(np.float32)
        ang = pos * inv[i]
        C[pp] = np.cos(ang)
        Sn[pp] = np.sin(ang)
        first = (sub % 32) < 16
        partner = pp + 16 if first else pp - 16
        perm[partner, pp] = -1.0 if first else 1.0
    return C, Sn, perm


def prep_shared(inp, S):
    NL = inp["w_mod"].shape[0]
    f = lambda k: np.ascontiguousarray(np.asarray(inp[k], np.float32))
    C, Sn, perm = _rope_tables(S)
    sh = {
        "w_mod": f("w_mod"), "w_in": f("w_in"), "p_a": f("p_a"), "p_b": f("p_b"), "p_c": f("p_c"),
        "w_out": f("w_out"), "w_up": f("w_up"), "w_down": f("w_down"),
        "b_mod_c": _col(inp["b_mod"], 96),
        "gvec": np.ascontiguousarray(np.stack([_col(inp[k], KC) for k in
                                               ("g_pre_mix", "g_post_mix", "g_pre_ffn", "g_post_ffn")], axis=2)),
        "na_tab": _na_tables(np.asarray(inp["na_rpb"], np.float32), S // GRID_W),
        "conv_w_c": np.ascontiguousarray(np.asarray(inp["conv_w"], np.float32).reshape(NL, 31, 8, 128).transpose(0, 3, 2, 1)),
        "conv_v_c": np.ascontiguousarray(np.stack([_col(inp[k], 8) for k in ("conv_b", "conv_ln_g", "conv_ln_b")], axis=2)),
        "lamv": np.ascontiguousarray(np.broadcast_to(
            np.stack([np.asarray(inp[k], np.float32) for k in ("lam_q1", "lam_k1", "lam_q2", "lam_k2")], axis=1)[:, None],
            (NL, 128, 4, 64))),
        "dlg_c": np.ascontiguousarray(np.asarray(inp["diff_ln_g"], np.float32).reshape(NL, 128, 1)),
        "fcw_c": np.ascontiguousarray(np.asarray(inp["ffn_conv_w"], np.float32).reshape(NL, 3, 88, 128).transpose(0, 3, 2, 1)),
        "fcb_c": _col(inp["ffn_conv_b"], 88),
        "ropeC": C, "ropeS": Sn, "perm": perm, "ident": np.eye(128, dtype=np.float32),
    }
    return sh


def prep_core(inp, b):
    cv = np.stack([_col(np.asarray(inp["c"], np.float32)[b], KC), _col(np.asarray(inp["c_ctx"], np.float32), KC)], axis=2)
    return {"x": np.ascontiguousarray(np.asarray(inp["x"], np.float32)[b]),
            "ctx": np.ascontiguousarray(np.asarray(inp["ctx"], np.float32)[b]),
            "cvec": np.ascontiguousarray(cv)}


def kernel(**inputs):
    B, S, _ = inputs["x"].shape
    kern = Kern(S)
    nc = kern.build()
    sh = prep_shared(inputs, S)
    active = [0, 2, 4, 6][:B]
    real = {c: dict(sh, **prep_core(inputs, b)) for b, c in enumerate(active)}
    zero = {k: np.zeros_like(v) for k, v in real[active[0]].items()}
    in_maps = [real.get(c, zero) for c in range(8)]
    res = run_bass_kernel_spmd(nc, in_maps, core_ids=list(range(8)))
    return np.stack([np.asarray(res.results[c]["out"], np.float32) for c in active], axis=0)
```

```python
import math
from contextlib import ExitStack
import numpy as np
import concourse.bass as bass
import concourse.mybir as mybir
from concourse.bass_utils import run_bass_kernel_spmd

F32 = mybir.dt.float32
BF16 = mybir.dt.bfloat16
AF = mybir.ActivationFunctionType
ALU = mybir.AluOpType

D = 2048
KC = 16
L_CTX = 256
GRID_W = 64
NH = 8
IN_COLS = 14336
FFN = 5632
EPS = 1e-6
NDMA_SEM = 16
NEG = -30000.0
STRICT = True


class Eng:
    def __init__(self, name, h, sem, dsems):
        self.name, self.h, self.sem, self.n = name, h, sem, 0
        self.seen = {}
        self.dsems = dsems
        self.ndma = 0


class Buf:
    __slots__ = ("name", "ap", "w", "r", "t", "prev")

    def __init__(self, name, t=None):
        self.name = name
        self.t = t
        self.ap = t
        self.w = {}
        self.r = {}
        self.prev = {}

    def __getitem__(self, idx):
        return self.t[idx]


class Pool:
    def __init__(self, bufs):
        self.bufs = bufs
        self.i = 0

    def next(self):
        b = self.bufs[self.i % len(self.bufs)]
        self.i += 1
        return b


class Prog:
    def __init__(self, nc, es):
        self.nc = nc
        self.es = es
        self.E = {}
        for name, h, nd in (("pe", nc.tensor, 0), ("act", nc.scalar, 0), ("dve", nc.vector, 0),
                            ("pool", nc.gpsimd, NDMA_SEM), ("sp", nc.sync, NDMA_SEM)):
            sem = es.enter_context(nc.semaphore("sem_" + name))
            ds = [es.enter_context(nc.semaphore("dsem_%s_%d" % (name, i))) for i in range(nd)]
            self.E[name] = Eng(name, h, sem, ds)
        self.uid = 0

    def sb(self, stack, shape, dt, name=None):
        self.uid += 1
        name = (name or "t") + "_%d" % self.uid
        t = stack.enter_context(self.nc.sbuf_tensor(name, list(shape), dt))
        return Buf(name, t)

    def sbpool(self, stack, n, shape, dt, name=None):
        return Pool([self.sb(stack, shape, dt, name) for _ in range(n)])

    def ps(self, stack, shape=(128, 512), dt=F32, name=None):
        self.uid += 1
        name = (name or "ps") + "_%d" % self.uid
        t = stack.enter_context(self.nc.psum_tensor(name, list(shape), dt))
        return Buf(name, t)

    def _deps(self, eng, reads, writes, waw, strict=True):
        deps = {}

        def add(d):
            for k, (sem, val, small) in d.items():
                if sem is eng.sem and (eng.name == "pe" or not (small or (STRICT and strict))):
                    continue
                if k not in deps or deps[k][1] < val:
                    deps[k] = (sem, val)
        for b in reads:
            add(b.w)
        for b in writes:
            if b.r:
                add(b.r)
            else:
                add(b.prev)
                if waw:
                    add(b.w)
        return deps

    def _wait(self, eng, deps):
        for k, (sem, val) in deps.items():
            if eng.seen.get(k, 0) >= val:
                continue
            eng.h.wait_ge(sem, val)
            eng.seen[k] = val

    def _record(self, tok, reads, writes):
        k = id(tok[0])
        for b in reads:
            b.r[k] = tok
        for b in writes:
            if b.r:
                b.w = {k: tok}
                b.prev = b.r
                b.r = {}
            else:
                b.w[k] = tok

    def op(self, engname, fn, reads=(), writes=(), small=False, waw=True, strict=True, signal=True):
        eng = self.E[engname]
        self._wait(eng, self._deps(eng, reads, writes, waw, strict))
        ins = fn(eng.h)
        if signal:
            ins.then_inc(eng.sem, 1)
            eng.n += 1
            self._record((eng.sem, eng.n, small), reads, writes)
        else:
            self._record((eng.sem, eng.n + 1, small), reads, writes)

    def dma(self, q, out, in_, reads=(), writes=(), waw=False):
        eng = self.E[q]
        deps = self._deps(eng, reads, writes, waw)
        i = eng.ndma % NDMA_SEM
        rnd = eng.ndma // NDMA_SEM
        sem = eng.dsems[i]
        if rnd > 0:
            k = id(sem)
            if k not in deps or deps[k][1] < 16 * rnd:
                deps[k] = (sem, 16 * rnd)
        self._wait(eng, deps)
        eng.h.dma_start(out=out, in_=in_).then_inc(sem, 16)
        eng.ndma += 1
        self._record((sem, 16 * (rnd + 1), False), reads, writes)

    def barrier(self):
        toks = {}
        for e in self.E.values():
            if e.n:
                toks[id(e.sem)] = (e.sem, e.n, e)
            for i, s in enumerate(e.dsems):
                cnt = (e.ndma - i + NDMA_SEM - 1) // NDMA_SEM
                if cnt > 0:
                    toks[id(s)] = (s, 16 * cnt, None)
        for e in self.E.values():
            d = {k: (s, v) for k, (s, v, own) in toks.items() if own is not e}
            self._wait(e, d)

    def mm(self, out, lhsT, rhs, start, stop, reads, writes, signal=None):
        if signal is None:
            signal = stop
        self.op("pe", lambda h: h.matmul(out, lhsT, rhs, start=start, stop=stop), reads, writes, signal=signal)

    def act(self, out, in_, func, reads, writes, bias=None, scale=None, small=False):
        kw = {}
        if bias is not None:
            kw["bias"] = bias
        if scale is not None:
            kw["scale"] = scale
        self.op("act", lambda h: h.activation(out, in_, func, **kw), reads, writes, small=small)

    def tt(self, eng, out, in0, in1, op, reads, writes, small=False, strict=True):
        self.op(eng, lambda h: h.tensor_tensor(out, in0, in1, op), reads, writes, small=small, strict=strict)

    def ts(self, eng, out, in0, s1, s2, op0, op1, reads, writes, small=False):
        if s2 is None:
            self.op(eng, lambda h: h.tensor_scalar(out, in0, s1, None, op0), reads, writes, small=small)
        else:
            self.op(eng, lambda h: h.tensor_scalar(out, in0, s1, s2, op0, op1), reads, writes, small=small)

    def stt(self, eng, out, in0, scalar, in1, op0, op1, reads, writes, small=False, strict=True):
        self.op(eng, lambda h: h.scalar_tensor_tensor(out, in0, scalar, in1, op0, op1), reads, writes, small=small,
                strict=strict)

    def copy(self, eng, out, in_, reads, writes, small=False, strict=True):
        if eng == "act":
            self.op("act", lambda h: h.copy(out, in_), reads, writes, small=small, strict=strict)
        else:
            self.op(eng, lambda h: h.tensor_copy(out, in_), reads, writes, small=small, strict=strict)


def kb0_of(j, nblk):
    return min(max(j - 2, 0), nblk - 5)


def variant_of(j, nblk):
    if j == 0:
        return 1
    if j == 1:
        return 2
    if j == nblk - 2:
        return 3
    if j == nblk - 1:
        return 4
    return 0


class Kern:
    def __init__(self, S, NL=2, debug=()):
        self.S, self.NL = S, NL
        self.T = S + L_CTX
        self.debug = set(debug)
        nc = bass.Bass("TRN2", target_bir_lowering=False)
        self.nc = nc
        T = self.T

        def din(name, shape, dt=F32):
            return nc.dram_tensor(name, list(shape), dt, kind="ExternalInput").ap()

        def scr(name, shape, dt=BF16):
            kind = "ExternalOutput" if name in self.debug else "Internal"
            return nc.dram_tensor(name, list(shape), dt, kind=kind).ap()

        self.x = din("x", [S, D])
        self.ctx = din("ctx", [L_CTX, D])
        self.cvec = din("cvec", [128, KC, 2])
        self.w_mod = din("w_mod", [NL, D, 6 * D])
        self.b_mod = din("b_mod_c", [NL, 128, 96])
        self.gvec = din("gvec", [NL, 128, 4, KC])
        self.w_in = din("w_in", [NL, D, IN_COLS])
        self.na_tab = din("na_tab", [NL, NH, 5, 128, 640])
        self.conv_w = din("conv_w_c", [NL, 128, 8, 31])
        self.conv_v = din("conv_v_c", [NL, 128, 3, 8])
        self.lamv = din("lamv", [NL, 128, 4, 64])
        self.dlg = din("dlg_c", [NL, 128, 1])
        self.p_a = din("p_a", [NL, 1024, D])
        self.p_b = din("p_b", [NL, 1024, D])
        self.p_c = din("p_c", [NL, 1024, D])
        self.w_out = din("w_out", [NL, D, D])
        self.w_up = din("w_up", [NL, D, 2 * FFN])
        self.fcw = din("fcw_c", [NL, 128, 88, 3])
        self.fcb = din("fcb_c", [NL, 128, 88])
        self.w_down = din("w_down", [NL, FFN, D])
        self.ropeC = din("ropeC", [128, S])
        self.ropeS = din("ropeS", [128, S])
        self.perm_in = din("perm", [128, 128])
        self.ident_in = din("ident", [128, 128])
        self.out = nc.dram_tensor("out", [S, D], F32, kind="ExternalOutput").ap()
        self.Wi = scr("Wi", [NL, 28, 128, KC, 512])
        self.Pa = scr("Pa", [NL, 4, 128, 8, 512])
        self.Pb = scr("Pb", [NL, 4, 128, 8, 512])
        self.Pc = scr("Pc", [NL, 4, 128, 8, 512])
        self.Wo = scr("Wo", [NL, 4, 128, KC, 512])
        self.Wu = scr("Wu", [NL, 22, 128, KC, 512])
        self.Wd = scr("Wd", [NL, 16, 128, 44, 128])
        self.xT = scr("xT", [D, T], F32)
        self.NQ = scr("NQ", [1024, T])
        self.NK = scr("NK", [1024, T])
        self.NV = scr("NV", [NH, 128, T // 128, 128])
        self.GLU = scr("GLU", [1024, T])
        self.DQ = scr("DQ", [1024, T])
        self.DK = scr("DK", [1024, T])
        self.DV = scr("DV", [NH, 128, T // 128, 128])
        self.G = scr("G", [3 * D, T])
        self.OA = scr("OA", [1024, T])
        self.OB = scr("OB", [1024, T])
        self.OC = scr("OC", [1024, T])
        self.U = scr("U", [2 * FFN, T])

    def build(self, phases=None):
        nc = self.nc
        with ExitStack() as es:
            p = Prog(nc, es)
            self.p = p
            self.ones_bf = p.sb(es, [128, 128], BF16, "ones_bf")
            self.ones_f = p.sb(es, [128, 128], F32, "ones_f")
            self.ident = p.sb(es, [128, 128], F32, "ident")
            self.perm = p.sb(es, [128, 128], BF16, "perm")
            self.cs = p.sb(es, [128, KC, 2], F32, "cs")
            self.modc = p.sb(es, [128, 6, 2, KC], F32, "modc")
            self.lam = p.sb(es, [128, 2], F32, "lam")
            self.dlgs = p.sb(es, [128, 1], F32, "dlgs")
            self.psT = [p.ps(es, shape=(128, 1024)) for _ in range(4)]
            halves = []
            for T in self.psT:
                halves.append(Buf(T.name + "a", T.t[:, 0:512]))
                halves.append(Buf(T.name + "b", T.t[:, 512:1024]))
            self.psum = Pool(halves)
            self.cb_vals = [D * EPS, EPS, 128 * EPS]
            self.cbias = p.sb(es, [128, len(self.cb_vals)], F32, "cbias")
            for i, v in enumerate(self.cb_vals):
                p.op("dve", lambda h, i=i, v=v: h.memset(self.cbias[:, i:i + 1], v), (), [self.cbias], small=True)
            p.op("dve", lambda h: h.memset(self.ones_f[:], 1.0), (), [self.ones_f])
            p.op("dve", lambda h: h.memset(self.ones_bf[:], 1.0), (), [self.ones_bf])
            p.dma("sp", self.ident[:], self.ident_in, writes=[self.ident])
            p.dma("pool", self.perm[:], self.perm_in, writes=[self.perm])
            p.dma("sp", self.cs[:], self.cvec, writes=[self.cs])
            with ExitStack() as st:
                sg = p.sb(st, [128, KC, 2], F32, "sg")
                p.act(sg[:], self.cs[:], AF.Sigmoid, [self.cs], [sg], small=True)
                p.tt("dve", self.cs[:], self.cs[:], sg[:], ALU.mult, [self.cs, sg], [self.cs], small=True)
                p.barrier()
            S, T = self.S, self.T
            lat_tiles = [(t0, 512, 0) for t0 in range(0, S, 512)]
            ctx_tile = (S, L_CTX, 1)
            ph = phases
            if ph is None or "wconv" in ph:
                self.phase_wconv()
            if ph is None or "tin" in ph:
                self.phase_tin()
            for l in range(self.NL):
                last = l == self.NL - 1
                if ph is None or "mod" in ph:
                    self.phase_mod(l)
                if ph is None or "A" in ph:
                    self.phase_A(l, lat_tiles + [ctx_tile], last)
                tl = lat_tiles + ([] if last else [ctx_tile])
                if ph is None or "NA" in ph:
                    self.phase_NA(l, tl)
                if ph is None or "CF" in ph:
                    self.phase_conformer(l, tl)
                if ph is None or "DF" in ph:
                    self.phase_diff(l, tl)
                if ph is None or "MG" in ph:
                    self.phase_merge(l, tl)
                if ph is None or "C1" in ph:
                    self.phase_ffn_up(l, tl)
                if ph is None or "C2" in ph:
                    self.phase_ffn_down(l, tl)
            if ph is None or "tout" in ph:
                self.phase_tout()
            p.barrier()
        return nc

    def phase_wconv(self):
        p = self.p
        for l in range(self.NL):
            def cv(dst, src, kcn, ng, cols=512):
                v = src.rearrange("(kc p) (g c) -> g p kc c", p=128, c=cols)
                for g in range(ng):
                    p.dma("pool", dst[g], v[g])
            cv(self.Wi[l], self.w_in[l], KC, 28)
            cv(self.Pa[l], self.p_a[l], 8, 4)
            cv(self.Pb[l], self.p_b[l], 8, 4)
            cv(self.Pc[l], self.p_c[l], 8, 4)
            cv(self.Wo[l], self.w_out[l], KC, 4)
            cv(self.Wu[l], self.w_up[l], KC, 22)
            cv(self.Wd[l], self.w_down[l], 44, 16, cols=128)
        p.barrier()

    def phase_tin(self):
        p = self.p
        S = self.S
        with ExitStack() as st:
            xin = p.sbpool(st, 8, [128, D], F32, "xin")
            xo = p.sbpool(st, 4, [128, 512], F32, "xo")
            tiles = [(self.x, t0, 512, t0) for t0 in range(0, S, 512)] + [(self.ctx, 0, L_CTX, S)]
            cnt = 0
            for (src, r0, n, c0) in tiles:
                nb = n // 128
                xs = []
                for i in range(nb):
                    b = xin.next()
                    p.dma("sp", b[:], src[r0 + i * 128: r0 + (i + 1) * 128, :], writes=[b])
                    xs.append(b)
                for fc in range(KC):
                    ps = self.psum.next()
                    for i in range(nb):
                        p.op("pe", lambda h, i=i, fc=fc, ps=ps: h.transpose(
                            ps[:, i * 128:(i + 1) * 128], xs[i][:, fc * 128:(fc + 1) * 128], self.ident[:]),
                            [xs[i], self.ident], [ps])
                    o = xo.next()
                    eng = "act" if cnt % 2 == 0 else "dve"
                    cnt += 1
                    p.copy(eng, o[:, :n], ps[:, :n], [ps], [o])
                    p.dma("pool", self.xT[fc * 128:(fc + 1) * 128, c0:c0 + n], o[:, :n], reads=[o])
            p.barrier()

    def phase_tout(self):
        p = self.p
        S = self.S
        xTv = self.xT.rearrange("(kc p) t -> p kc t", p=128)
        with ExitStack() as st:
            xi = p.sbpool(st, 2, [128, KC, 512], F32, "xi")
            xo = p.sbpool(st, 3, [128, D], F32, "xo2")
            cnt = 0
            for t0 in range(0, S, 512):
                b = xi.next()
                p.dma("sp", b[:], xTv[:, :, t0:t0 + 512], writes=[b])
                for i in range(4):
                    o = xo.next()
                    for f4 in range(4):
                        ps = self.psum.next()
                        for k in range(4):
                            fc = f4 * 4 + k
                            p.op("pe", lambda h, ps=ps, k=k, fc=fc, i=i, b=b: h.transpose(
                                ps[:, k * 128:(k + 1) * 128], b[:, fc, i * 128:(i + 1) * 128], self.ident[:]),
                                [b, self.ident], [ps])
                        eng = "act" if cnt % 2 == 0 else "dve"
                        cnt += 1
                        p.copy(eng, o[:, f4 * 512:(f4 + 1) * 512], ps[:], [ps], [o])
                    p.dma("pool", self.out[t0 + i * 128: t0 + (i + 1) * 128, :], o[:], reads=[o])
            p.barrier()

    def phase_mod(self, l):
        p = self.p
        with ExitStack() as st:
            wm = p.sbpool(st, 2, [128, KC, 512], F32, "wm")
            modv = p.sb(st, [128, 96, 2], F32, "modv")
            bm = p.sb(st, [128, 96], F32, "bm")
            gv = p.sb(st, [128, 4, KC], F32, "gv")
            lv = p.sb(st, [128, 4, 64], F32, "lv")
            lt = p.sb(st, [128, 2, 64], F32, "lt")
            ls = p.sb(st, [128, 2], F32, "ls")
            p.dma("sp", bm[:], self.b_mod[l], writes=[bm])
            p.dma("sp", gv[:], self.gvec[l], writes=[gv])
            p.dma("sp", self.dlgs[:], self.dlg[l], writes=[self.dlgs])
            p.dma("sp", lv[:], self.lamv[l], writes=[lv])
            wv = self.w_mod[l].rearrange("(kc p) (g c) -> g p kc c", p=128, c=512)
            ps = self.psum.next()
            for g in range(24):
                w = wm.next()
                p.dma("sp", w[:], wv[g], writes=[w])
                for j in range(4):
                    oc = g * 4 + j
                    for kc in range(KC):
                        p.mm(ps[:, oc * 2:oc * 2 + 2], w[:, kc, j * 128:(j + 1) * 128], self.cs[:, kc, :],
                             kc == 0, kc == KC - 1, [w, self.cs], [ps])
            psv = ps[:, 0:192].rearrange("p (o n) -> p o n", n=2)
            for n in range(2):
                p.tt("dve", modv[:, :, n], psv[:, :, n], bm[:], ALU.add, [ps, bm], [modv], small=True)
            sD = math.sqrt(D)
            mc = self.modc
            for n in range(2):
                m = lambda j: modv[:, j * 16:(j + 1) * 16, n]
                p.stt("dve", mc[:, 0, n, :], m(1), 1.0, gv[:, 0, :], ALU.add, ALU.mult, [modv, gv], [mc], small=True)
                p.ts("dve", mc[:, 0, n, :], mc[:, 0, n, :], sD, None, ALU.mult, None, [mc], [mc], small=True)
                p.copy("dve", mc[:, 1, n, :], m(0), [modv], [mc], small=True)
                p.stt("dve", mc[:, 2, n, :], m(2), sD, gv[:, 1, :], ALU.mult, ALU.mult, [modv, gv], [mc], small=True)
                p.stt("dve", mc[:, 3, n, :], m(4), 1.0, gv[:, 2, :], ALU.add, ALU.mult, [modv, gv], [mc], small=True)
                p.ts("dve", mc[:, 3, n, :], mc[:, 3, n, :], sD, None, ALU.mult, None, [mc], [mc], small=True)
                p.copy("dve", mc[:, 4, n, :], m(3), [modv], [mc], small=True)
                p.stt("dve", mc[:, 5, n, :], m(5), sD, gv[:, 3, :], ALU.mult, ALU.mult, [modv, gv], [mc], small=True)
            lam_init = 0.8 - 0.6 * math.exp(-0.3 * l)
            p.tt("dve", lt[:, 0, :], lv[:, 0, :], lv[:, 1, :], ALU.mult, [lv], [lt], small=True)
            p.tt("dve", lt[:, 1, :], lv[:, 2, :], lv[:, 3, :], ALU.mult, [lv], [lt], small=True)
            p.op("dve", lambda h: h.tensor_reduce(ls[:], lt[:], mybir.AxisListType.X, ALU.add), [lt], [ls], small=True)
            p.act(ls[:], ls[:], AF.Exp, [ls], [ls], small=True)
            p.stt("dve", self.lam[:, 0:1], ls[:, 1:2], -lam_init, ls[:, 0:1], ALU.add, ALU.subtract,
                  [ls], [self.lam], small=True)
            p.ts("dve", self.dlgs[:], self.dlgs[:], (1.0 - lam_init) * math.sqrt(128.0), None, ALU.mult, None,
                 [self.dlgs], [self.dlgs], small=True)
            p.barrier()

    def rsqrt(self, out, in_, c, reads, obuf, small=False):
        p = self.p
        p.act(out, in_, AF.Sqrt, reads + [self.cbias], [obuf], bias=self.cbias[:, self.cbias_idx(c):self.cbias_idx(c) + 1], small=small)
        p.op("dve", lambda h: h.reciprocal(out, out), [obuf], [obuf], small=small)

    def cbias_idx(self, c):
        return self.cb_vals.index(c)

    def norm_mod(self, xt, hT, n, ia, ib, m, sq, rs, tmp):
        p = self.p
        ps = self.psum.next()
        for kc in range(KC):
            q = sq.next()
            p.act(q[:, :n], xt[:, kc, :n], AF.Square, [xt], [q])
            p.mm(ps[:, :n], self.ones_bf[:], q[:, :n], kc == 0, kc == KC - 1, [self.ones_bf, q], [ps], signal=True)
        self.rsqrt(rs[:, :n], ps[:, :n], D * EPS, [ps], rs)
        mc = self.modc
        for kc in range(KC):
            t = tmp.next()
            p.stt("dve", t[:, :n], xt[:, kc, :n], mc[:, ia, m, kc:kc + 1], rs[:, :n], ALU.mult, ALU.mult,
                  [xt, mc, rs], [t])
            p.act(hT[:, kc, :n], t[:, :n], AF.Identity, [t, mc], [hT], bias=mc[:, ib, m, kc:kc + 1])

    def phase_A(self, l, tiles, last):
        p = self.p
        S = self.S
        xTv = self.xT.rearrange("(kc p) t -> p kc t", p=128)
        with ExitStack() as st:
            xpool = p.sbpool(st, 2, [128, KC, 512], F32, "xa")
            sq = p.sbpool(st, 2, [128, 512], BF16, "sq")
            rs = p.sb(st, [128, 512], F32, "rs")
            tmp = p.sbpool(st, 2, [128, 512], F32, "tmpa")
            hpool = p.sbpool(st, 2, [128, KC, 512], BF16, "hT")
            wpool = p.sbpool(st, 3, [128, KC, 512], BF16, "wa")
            abuf = p.sb(st, [128, 8, 512], F32, "abuf")
            stg = p.sbpool(st, 4, [128, 512], BF16, "stg")
            f32t = p.sbpool(st, 3, [128, 512], F32, "f32t")
            rc = p.sbpool(st, 2, [128, 512], F32, "rc")
            rsn = p.sbpool(st, 2, [128, 512], F32, "rsn")
            cnt = 0
            def prep(tile):
                t0_, n_, m_ = tile
                xt_ = xpool.next()
                p.dma("sp", xt_[:, :, :n_], xTv[:, :, t0_:t0_ + n_], writes=[xt_])
                hT_ = hpool.next()
                self.norm_mod(xt_, hT_, n_, 0, 1, m_, sq, rs, tmp)
                return hT_
            hT_next = prep(tiles[0])
            for ti, (t0, n, m) in enumerate(tiles):
                hT = hT_next
                if not m:
                    rcb, rsb = rc.next(), rsn.next()
                    p.dma("sp", rcb[:, :n], self.ropeC[:, t0:t0 + n], writes=[rcb])
                    p.dma("sp", rsb[:, :n], self.ropeS[:, t0:t0 + n], writes=[rsb])
                groups = range(28)
                import os
                if os.environ.get("KGROUPS"):
                    groups = [int(v) for v in os.environ["KGROUPS"].split(",")]
                if m and last:
                    groups = [2, 3, 4, 5, 12, 13, 14, 15]
                groups = list(groups)
                for gi, g in enumerate(groups):
                    if gi == len(groups) // 2 and ti + 1 < len(tiles):
                        hT_next = prep(tiles[ti + 1])
                    w = wpool.next()
                    p.dma("sp", w[:], self.Wi[l, g], writes=[w])
                    if g in (4, 5, 14, 15):
                        dst = self.NV if g < 6 else self.DV
                        hh = (g - 4) * 4 if g < 6 else (g - 14) * 4
                        for tb in range(n // 128):
                            ps = self.psum.next()
                            for kc in range(KC):
                                p.mm(ps[:, :], hT[:, kc, tb * 128:(tb + 1) * 128], w[:, kc, :], kc == 0, kc == KC - 1,
                                     [hT, w], [ps])
                            o = stg.next()
                            eng = "act" if cnt % 2 == 0 else "dve"
                            cnt += 1
                            p.copy(eng, o[:], ps[:], [ps], [o])
                            blk = (t0 + tb * 128) // 128
                            p.dma("pool", dst[hh:hh + 4, :, blk, :].rearrange("h p c -> p h c"),
                                  o[:].rearrange("p (h c) -> p h c", c=128), reads=[o])
                        continue
                    for j in range(4):
                        c = g * 4 + j
                        ps = self.psum.next()
                        for kc in range(KC):
                            p.mm(ps[:, :n], w[:, kc, j * 128:(j + 1) * 128], hT[:, kc, :n], kc == 0, kc == KC - 1,
                                 [w, hT], [ps])
                        if c < 16:
                            dst = self.NQ if c < 8 else self.NK
                            o = stg.next()
                            eng = "act" if cnt % 2 == 0 else "dve"
                            cnt += 1
                            p.copy(eng, o[:, :n], ps[:, :n], [ps], [o])
                            r = (c % 8) * 128
                            p.dma("pool", dst[r:r + 128, t0:t0 + n], o[:, :n], reads=[o])
                        elif c < 32:
                            p.copy("act", abuf[:, c - 24, :n], ps[:, :n], [ps], [abuf])
                        elif c < 40:
                            f = f32t.next()
                            p.act(f[:, :n], ps[:, :n], AF.Sigmoid, [ps], [f])
                            o = stg.next()
                            p.tt("dve", o[:, :n], abuf[:, c - 32, :n], f[:, :n], ALU.mult, [abuf, f], [o])
                            r = (c - 32) * 128
                            p.dma("pool", self.GLU[r:r + 128, t0:t0 + n], o[:, :n], reads=[o])
                        elif c < 56:
                            dst = self.DQ if c < 48 else self.DK
                            r = (c % 8) * 128
                            xb = stg.next()
                            p.copy("act", xb[:, :n], ps[:, :n], [ps], [xb])
                            if m:
                                p.dma("pool", dst[r:r + 128, t0:t0 + n], xb[:, :n], reads=[xb])
                            else:
                                ps2 = self.psum.next()
                                p.mm(ps2[:, :n], self.perm[:], xb[:, :n], True, True, [self.perm, xb], [ps2])
                                f1 = f32t.next()
                                p.tt("dve", f1[:, :n], ps2[:, :n], rsb[:, :n], ALU.mult, [ps2, rsb], [f1])
                                f2 = f32t.next()
                                p.tt("dve", f2[:, :n], xb[:, :n], rcb[:, :n], ALU.mult, [xb, rcb], [f2])
                                o = stg.next()
                                p.tt("dve", o[:, :n], f1[:, :n], f2[:, :n], ALU.add, [f1, f2], [o])
                                p.dma("pool", dst[r:r + 128, t0:t0 + n], o[:, :n], reads=[o])
                        else:
                            o = stg.next()
                            p.act(o[:, :n], ps[:, :n], AF.Sigmoid, [ps], [o])
                            r = (c - 64) * 128
                            p.dma("pool", self.G[r:r + 128, t0:t0 + n], o[:, :n], reads=[o])
            p.barrier()

    def phase_NA(self, l, tiles):
        p = self.p
        S, T = self.S, self.T
        nblk = S // 128
        scale = 128 ** -0.5
        NKv = self.NK.rearrange("(h p) t -> p h t", p=128)
        NQv = self.NQ.rearrange("(h p) t -> p h t", p=128)
        OAv = self.OA.rearrange("(h p) t -> p h t", p=128)
        with ExitStack() as st:
            kc_sb = p.sb(st, [128, NH, 256], BF16, "kcs")
            vc_sb = p.sb(st, [128, NH, 2, 128], BF16, "vcs")
            p.dma("sp", kc_sb[:], NKv[:, :, S:T], writes=[kc_sb])
            p.dma("sp", vc_sb[:], self.NV[:, :, nblk:nblk + 2, :].rearrange("h p b c -> p h b c"), writes=[vc_sb])
            kpool = p.sbpool(st, 2, [128, NH, 1024], BF16, "nak")
            vpool = p.sbpool(st, 2, [128, NH, 8, 128], BF16, "nav")
            qpool = p.sbpool(st, 2, [128, NH, 512], BF16, "naq")
            tabp = p.sbpool(st, 3, [128, 640], F32, "tab")
            sbp = p.sbpool(st, 3, [128, 896], F32, "nsb")
            ep = p.sbpool(st, 3, [128, 896], BF16, "nae")
            rzp = p.sbpool(st, 3, [128, 128], F32, "rz")
            oap = p.sbpool(st, 2, [128, NH, 512], BF16, "oat")
            tab0 = p.sb(st, [128, NH, 640], F32, "tab0")
            p.dma("sp", tab0[:], self.na_tab[l, :, 0].rearrange("h p c -> p h c"), writes=[tab0])
            for (t0, n, m) in tiles:
                qt = qpool.next()
                p.dma("sp", qt[:, :, :n], NQv[:, :, t0:t0 + n], writes=[qt])
                oa = oap.next()
                if not m:
                    j0 = t0 // 128
                    lo = kb0_of(j0, nblk)
                    hi = kb0_of(j0 + 3, nblk) + 5
                    nb = hi - lo
                    kt = kpool.next()
                    p.dma("sp", kt[:, :, :nb * 128], NKv[:, :, lo * 128:hi * 128], writes=[kt])
                    vt = vpool.next()
                    p.dma("sp", vt[:, :, :nb, :], self.NV[:, :, lo:hi, :].rearrange("h p b c -> p h b c"), writes=[vt])
                def stage1(h, jj):
                    q_ap = qt[:, h, jj * 128:(jj + 1) * 128]
                    e = ep.next()
                    if not m:
                        j = j0 + jj
                        kb = kb0_of(j, nblk) - lo
                        vr = variant_of(j, nblk)
                        if vr == 0:
                            tb = tab0
                            tbv = tab0[:, h, :]
                        else:
                            tb = tabp.next()
                            p.dma("sp", tb[:], self.na_tab[l, h, vr], writes=[tb])
                            tbv = tb[:]
                        psA = self.psum.next()
                        psB = self.psum.next()
                        for i in range(5):
                            dst = psA[:, i * 128:(i + 1) * 128] if i < 4 else psB[:, 0:128]
                            p.mm(dst, kt[:, h, (kb + i) * 128:(kb + i + 1) * 128], q_ap, True, True,
                                 [kt, qt], [psA if i < 4 else psB])
                        for i in range(2):
                            p.mm(psB[:, (1 + i) * 128:(2 + i) * 128], kc_sb[:, h, i * 128:(i + 1) * 128], q_ap,
                                 True, True, [kc_sb, qt], [psB])
                        sb = sbp.next()
                        p.stt("dve", sb[:, 0:512], psA[:, :], scale, tbv[:, 0:512], ALU.mult, ALU.add, [psA, tb], [sb])
                        p.stt("dve", sb[:, 512:640], psB[:, 0:128], scale, tbv[:, 512:640], ALU.mult, ALU.add,
                              [psB, tb], [sb])
                        p.ts("dve", sb[:, 640:896], psB[:, 128:384], scale, None, ALU.mult, None, [psB], [sb])
                        p.act(e[:, 0:896], sb[:], AF.Exp, [sb], [e])
                        nkb = 7
                        lhs_v = lambda i: vt[:, h, kb + i, :] if i < 5 else vc_sb[:, h, i - 5, :]
                        vreads = [vt, vc_sb]
                    else:
                        psB = self.psum.next()
                        for i in range(2):
                            p.mm(psB[:, i * 128:(i + 1) * 128], kc_sb[:, h, i * 128:(i + 1) * 128], q_ap,
                                 True, True, [kc_sb, qt], [psB])
                        p.act(e[:, 0:256], psB[:, 0:256], AF.Exp, [psB], [e], scale=scale)
                        nkb = 2
                        lhs_v = lambda i: vc_sb[:, h, i, :]
                        vreads = [vc_sb]
                    return (h, jj, e, nkb, lhs_v, vreads)

                def stage2(h, jj, e, nkb, lhs_v, vreads):
                    psO = self.psum.next()
                    psZ = self.psum.next()
                    for i in range(nkb):
                        p.mm(psO[:, 0:128], lhs_v(i), e[:, i * 128:(i + 1) * 128], i == 0, i == nkb - 1,
                             vreads + [e], [psO])
                        p.mm(psZ[:, 0:128], self.ones_bf[:], e[:, i * 128:(i + 1) * 128], i == 0, i == nkb - 1,
                             [self.ones_bf, e], [psZ])
                    rz = rzp.next()
                    p.op("dve", lambda hh: hh.reciprocal(rz[:], psZ[:, 0:128]), [psZ], [rz])
                    p.tt("dve", oa[:, h, jj * 128:(jj + 1) * 128], psO[:, 0:128], rz[:], ALU.mult, [psO, rz], [oa])
                prev = None
                for h in range(NH):
                    for jj in range(n // 128):
                        cur = stage1(h, jj)
                        if prev is not None:
                            stage2(*prev)
                        prev = cur
                stage2(*prev)
                p.dma("pool", OAv[:, :, t0:t0 + n], oa[:, :, :n], reads=[oa])
            p.barrier()

    def phase_conformer(self, l, tiles):
        p = self.p
        S, T = self.S, self.T
        OBv = self.OB.rearrange("(c p) t -> p c t", p=128)
        with ExitStack() as st:
            cw = p.sb(st, [128, 8, 31], F32, "cw")
            cvv = p.sb(st, [128, 3, 8], F32, "cvv")
            p.dma("sp", cw[:], self.conv_w[l], writes=[cw])
            p.dma("sp", cvv[:], self.conv_v[l], writes=[cvv])
            glp = p.sbpool(st, 5, [128, 544], BF16, "gl")
            cps = Pool(self.psum.bufs[0:6])
            identb = p.sb(st, [128, 128], BF16, "identb")
            p.copy("dve", identb[:], self.ident[:], [self.ident], [identb])
            dg = p.sb(st, [128, 8, 31, 128], BF16, "dg")
            for i in range(8):
                for j in range(31):
                    p.ts("dve", dg[:, i, j, :], identb[:], cw[:, i, j:j + 1], None, ALU.mult, None, [identb, cw], [dg])
            acc = [p.sb(st, [128, 512], F32, "cacc") for _ in range(8)]
            sqp = p.sbpool(st, 2, [128, 512], F32, "csq")
            mu = p.sb(st, [128, 512], F32, "mu")
            var = p.sb(st, [128, 512], F32, "var")
            rstd = p.sb(st, [128, 512], F32, "rstd")
            tp = p.sbpool(st, 3, [128, 512], F32, "ct")
            obp = p.sbpool(st, 2, [128, 8, 512], BF16, "obt")
            for (t0, n, m) in tiles:
                seq_lo, seq_hi = (S, T) if m else (0, S)
                lo = max(t0 - 15, seq_lo)
                hi = min(t0 + n + 15, seq_hi)
                off = lo - (t0 - 15)
                psM = self.psum.bufs[6]
                psQ = self.psum.bufs[7]
                for i in range(8):
                    gl = glp.next()
                    if off > 0:
                        p.op("dve", lambda hh: hh.memset(gl[:, 0:off], 0.0), (), [gl], small=True)
                    if off + hi - lo < n + 30:
                        p.op("dve", lambda hh: hh.memset(gl[:, off + hi - lo:n + 30], 0.0), (), [gl], small=True)
                    p.dma("sp", gl[:, off:off + hi - lo], self.GLU[i * 128:(i + 1) * 128, lo:hi], writes=[gl])
                    a = acc[i]
                    psc = cps.next()
                    for j in range(31):
                        p.mm(psc[:, :n], dg[:, i, j, :], gl[:, j:j + n], j == 0, j == 30, [dg, gl], [psc])
                    p.act(a[:, :n], psc[:, :n], AF.Identity, [psc, cvv], [a], bias=cvv[:, 0, i:i + 1])
                    sq = sqp.next()
                    p.act(sq[:, :n], a[:, :n], AF.Square, [a], [sq])
                    p.mm(psM[:, :n], self.ones_f[:], a[:, :n], i == 0, i == 7, [self.ones_f, a], [psM], signal=True)
                    p.mm(psQ[:, :n], self.ones_f[:], sq[:, :n], i == 0, i == 7, [self.ones_f, sq], [psQ], signal=True)
                p.ts("dve", mu[:, :n], psM[:, :n], 1.0 / 1024, None, ALU.mult, None, [psM], [mu])
                p.tt("dve", var[:, :n], mu[:, :n], mu[:, :n], ALU.mult, [mu], [var])
                p.stt("dve", var[:, :n], psQ[:, :n], 1.0 / 1024, var[:, :n], ALU.mult, ALU.subtract, [psQ, var], [var])
                self.rsqrt(rstd[:, :n], var[:, :n], EPS, [var], rstd)
                ob = obp.next()
                for i in range(8):
                    t = tp.next()
                    p.tt("dve", t[:, :n], acc[i][:, :n], mu[:, :n], ALU.subtract, [acc[i], mu], [t])
                    p.tt("dve", t[:, :n], t[:, :n], rstd[:, :n], ALU.mult, [t, rstd], [t])
                    p.act(ob[:, i, :n], t[:, :n], AF.Silu, [t, cvv], [ob], scale=cvv[:, 1, i:i + 1], bias=cvv[:, 2, i:i + 1])
                p.dma("pool", OBv[:, :, t0:t0 + n], ob[:, :, :n], reads=[ob])
            p.barrier()

    def phase_diff(self, l, tiles):
        p = self.p
        S, T = self.S, self.T
        scale = 64 ** -0.5
        DQv = self.DQ.rearrange("(h p) t -> p h t", p=128)
        OCv = self.OC.rearrange("(h p) t -> p h t", p=128)
        with ExitStack() as st:
            qpool = p.sbpool(st, 2, [128, NH, 512], BF16, "dq")
            kpool = p.sbpool(st, 2, [128, T], BF16, "dk")
            vpool = p.sbpool(st, 2, [128, T // 128, 128], BF16, "dv")
            ocp = p.sbpool(st, 2, [128, NH, 512], BF16, "oct")
            fp = p.sbpool(st, 6, [128, 512], F32, "df")
            acc = self.psum.bufs[0:2]
            pairs = [(self.psT[i], self.psum.bufs[2 * i], self.psum.bufs[2 * i + 1]) for i in (1, 2, 3)]
            pi = 0
            z = p.sb(st, [128, 2, 512], F32, "z")
            ep = p.sbpool(st, 8, [128, 2, 512], BF16, "e12")
            esp = p.sbpool(st, 2, [128, 2, 512], BF16, "esum")
            for (t0, n, m) in tiles:
                qt = qpool.next()
                p.dma("sp", qt[:, :, :n], DQv[:, :, t0:t0 + n], writes=[qt])
                kts = list(range(S // 128, T // 128)) if m else list(range(T // 128))
                oc = ocp.next()
                for h in range(NH):
                    kb = kpool.next()
                    p.dma("sp", kb[:], self.DK[h * 128:(h + 1) * 128, :], writes=[kb])
                    vb = vpool.next()
                    p.dma("sp", vb[:], self.DV[h], writes=[vb])
                    O1, O2 = acc

                    pend = []

                    def pv(e, kt, first, lastk):
                        p.mm(O1[:, :n], vb[:, kt, :], e[:, 0, :n], first, lastk, [vb, e], [O1], signal=False)
                        p.mm(O2[:, :n], vb[:, kt, :], e[:, 1, :n], first, lastk, [vb, e], [O2], signal=True)
                        pend.append((e, first))
                        if len(pend) == 2:
                            (ea, fa), (eb, _) = pend
                            del pend[:]
                            es = esp.next()
                            p.tt("dve", es[:, :, :n], ea[:, :, :n], eb[:, :, :n], ALU.add, [ea, eb], [es])
                            if fa:
                                p.copy("dve", z[:, :, :n], es[:, :, :n], [es], [z])
                            else:
                                p.tt("dve", z[:, :, :n], z[:, :, :n], es[:, :, :n], ALU.add, [z, es], [z], strict=False)
                    prev = []
                    for idx, kt in enumerate(kts):
                        TT, s1, s2 = pairs[pi % 3]
                        pi += 1
                        p.mm(s1[:, :n], kb[0:64, kt * 128:(kt + 1) * 128], qt[0:64, h, :n], True, True, [kb, qt], [s1],
                             signal=False)
                        p.mm(s2[:, :n], kb[64:128, kt * 128:(kt + 1) * 128], qt[64:128, h, :n], True, True, [kb, qt], [s2])
                        e = ep.next()
                        p.act(e[:, :, :n], TT.t[:].rearrange("p (b c) -> p b c", b=2)[:, :, :n], AF.Exp, [s1, s2], [e],
                              scale=scale)
                        prev.append((e, kt, idx == 0, idx == len(kts) - 1))
                        if len(prev) > 2:
                            pv(*prev.pop(0))
                    while prev:
                        pv(*prev.pop(0))
                    z1 = z[:, 0, :]
                    z2 = z[:, 1, :]
                    z1b = z2b = z
                    _, Z1, Z2 = pairs[pi % 3]
                    pi += 1
                    p.mm(Z1[:, :n], self.ones_f[:], z1[:, :n], True, True, [self.ones_f, z], [Z1])
                    p.mm(Z2[:, :n], self.ones_f[:], z2[:, :n], True, True, [self.ones_f, z], [Z2])
                    r1 = fp.next()
                    p.op("dve", lambda hh: hh.reciprocal(r1[:, :n], Z1[:, :n]), [Z1], [r1])
                    t1 = fp.next()
                    p.tt("dve", t1[:, :n], O1[:, :n], r1[:, :n], ALU.mult, [O1, r1], [t1])
                    r2 = fp.next()
                    p.op("dve", lambda hh: hh.reciprocal(r2[:, :n], Z2[:, :n]), [Z2], [r2])
                    t2 = fp.next()
                    p.tt("dve", t2[:, :n], O2[:, :n], r2[:, :n], ALU.mult, [O2, r2], [t2])
                    p.stt("dve", t1[:, :n], t2[:, :n], self.lam[:, 0:1], t1[:, :n], ALU.mult, ALU.add,
                          [t2, self.lam, t1], [t1])
                    p.act(r1[:, :n], t1[:, :n], AF.Square, [t1], [r1])
                    _, psS, _unused = pairs[pi % 3]
                    pi += 1
                    p.mm(psS[:, :n], self.ones_f[:], r1[:, :n], True, True, [self.ones_f, r1], [psS])
                    self.rsqrt(r2[:, :n], psS[:, :n], 128 * EPS, [psS], r2)
                    p.stt("dve", oc[:, h, :n], t1[:, :n], self.dlgs[:, 0:1], r2[:, :n], ALU.mult, ALU.mult,
                          [t1, self.dlgs, r2], [oc])
                p.dma("pool", OCv[:, :, t0:t0 + n], oc[:, :, :n], reads=[oc])
            p.barrier()

    def post_residual(self, mix, pss, rs, n, t0, m, ig, xp, tp):
        p = self.p
        mc = self.modc
        self.rsqrt(rs[:, :n], pss[:, :n], D * EPS, [pss], rs)
        for kc in range(KC):
            xt = xp.next()
            p.dma("sp", xt[:, :n], self.xT[kc * 128:(kc + 1) * 128, t0:t0 + n], writes=[xt])
            t = tp.next()
            p.stt("dve", t[:, :n], mix[:, kc, :n], mc[:, ig, m, kc:kc + 1], rs[:, :n], ALU.mult, ALU.mult,
                  [mix, mc, rs], [t])
            p.tt("dve", xt[:, :n], xt[:, :n], t[:, :n], ALU.add, [xt, t], [xt])
            p.dma("pool", self.xT[kc * 128:(kc + 1) * 128, t0:t0 + n], xt[:, :n], reads=[xt])

    def phase_merge(self, l, tiles):
        p = self.p
        Gv = self.G.rearrange("(br c p) t -> p br c t", p=128, c=16)
        with ExitStack() as st:
            bp = [p.sbpool(st, 1, [128, 8, 512], BF16, "mb%d" % i) for i in range(3)]
            wp = [p.sbpool(st, 2, [128, 8, 512], BF16, "mw%d" % i) for i in range(3)]
            gtp = p.sbpool(st, 3, [128, 3, 512], BF16, "gt")
            tp = p.sbpool(st, 6, [128, 512], F32, "mt")
            y = p.sb(st, [128, KC, 512], BF16, "y")
            wop = p.sbpool(st, 2, [128, KC, 512], BF16, "wo")
            mix = p.sb(st, [128, KC, 512], F32, "mix")
            sq = p.sbpool(st, 2, [128, 512], BF16, "msq")
            rs = p.sb(st, [128, 512], F32, "mrs")
            xp = p.sbpool(st, 8, [128, 512], F32, "mx")
            srcs = [self.OA, self.OB, self.OC]
            Ws = [self.Pa, self.Pb, self.Pc]
            full_psum = self.psum
            pss = full_psum.bufs[7]
            self.psum = Pool(full_psum.bufs[0:7])
            for (t0, n, m) in tiles:
                br = []
                for i in range(3):
                    b = bp[i].next()
                    p.dma("sp", b[:, :, :n], srcs[i].rearrange("(c p) t -> p c t", p=128)[:, :, t0:t0 + n], writes=[b])
                    br.append(b)
                for og in range(4):
                    ws = []
                    for i in range(3):
                        w = wp[i].next()
                        p.dma("sp", w[:], Ws[i][l, og], writes=[w])
                        ws.append(w)
                    for j in range(4):
                        c = og * 4 + j
                        gt = gtp.next()
                        p.dma("sp", gt[:, :, :n], Gv[:, :, c, t0:t0 + n], writes=[gt])
                        ts_ = []
                        for i in range(3):
                            ps = self.psum.next()
                            for kc in range(8):
                                p.mm(ps[:, :n], ws[i][:, kc, j * 128:(j + 1) * 128], br[i][:, kc, :n], kc == 0, kc == 7,
                                     [ws[i], br[i]], [ps])
                            t = tp.next()
                            p.tt("dve", t[:, :n], ps[:, :n], gt[:, i, :n], ALU.mult, [ps, gt], [t])
                            ts_.append(t)
                        p.tt("dve", ts_[0][:, :n], ts_[0][:, :n], ts_[1][:, :n], ALU.add, [ts_[0], ts_[1]], [ts_[0]])
                        p.tt("dve", y[:, c, :n], ts_[0][:, :n], ts_[2][:, :n], ALU.add, [ts_[0], ts_[2]], [y])
                for og in range(4):
                    wo = wop.next()
                    p.dma("sp", wo[:], self.Wo[l, og], writes=[wo])
                    for j in range(4):
                        c = og * 4 + j
                        ps = self.psum.next()
                        for kc in range(KC):
                            p.mm(ps[:, :n], wo[:, kc, j * 128:(j + 1) * 128], y[:, kc, :n], kc == 0, kc == KC - 1,
                                 [wo, y], [ps])
                        p.copy("act", mix[:, c, :n], ps[:, :n], [ps], [mix])
                        q = sq.next()
                        p.act(q[:, :n], mix[:, c, :n], AF.Square, [mix], [q])
                        p.mm(pss[:, :n], self.ones_bf[:], q[:, :n], c == 0, c == KC - 1, [self.ones_bf, q], [pss], signal=True)
                self.post_residual(mix, pss, rs, n, t0, m, 2, xp, tp)
            self.psum = full_psum
            p.barrier()

    def phase_ffn_up(self, l, tiles):
        p = self.p
        xTv = self.xT.rearrange("(kc p) t -> p kc t", p=128)
        with ExitStack() as st:
            xpool = p.sbpool(st, 2, [128, KC, 512], F32, "xu")
            sq = p.sbpool(st, 2, [128, 512], BF16, "usq")
            rs = p.sb(st, [128, 512], F32, "urs")
            tmp = p.sbpool(st, 2, [128, 512], F32, "utmp")
            hpool = p.sbpool(st, 2, [128, KC, 512], BF16, "uh")
            wpool = p.sbpool(st, 3, [128, KC, 512], BF16, "uw")
            stg = p.sbpool(st, 4, [128, 512], BF16, "ustg")
            cnt = 0
            def prep(tile):
                t0_, n_, m_ = tile
                xt_ = xpool.next()
                p.dma("sp", xt_[:, :, :n_], xTv[:, :, t0_:t0_ + n_], writes=[xt_])
                hT_ = hpool.next()
                self.norm_mod(xt_, hT_, n_, 3, 4, m_, sq, rs, tmp)
                return hT_
            hT_next = prep(tiles[0])
            for ti, (t0, n, m) in enumerate(tiles):
                hT = hT_next
                for g in range(22):
                    if g == 11 and ti + 1 < len(tiles):
                        hT_next = prep(tiles[ti + 1])
                    w = wpool.next()
                    p.dma("sp", w[:], self.Wu[l, g], writes=[w])
                    for j in range(4):
                        c = g * 4 + j
                        ps = self.psum.next()
                        for kc in range(KC):
                            p.mm(ps[:, :n], w[:, kc, j * 128:(j + 1) * 128], hT[:, kc, :n], kc == 0, kc == KC - 1,
                                 [w, hT], [ps])
                        o = stg.next()
                        eng = "act" if cnt % 2 == 0 else "dve"
                        cnt += 1
                        p.copy(eng, o[:, :n], ps[:, :n], [ps], [o])
                        p.dma("pool", self.U[c * 128:(c + 1) * 128, t0:t0 + n], o[:, :n], reads=[o])
            p.barrier()

    def phase_ffn_down(self, l, tiles):
        p = self.p
        S, T = self.S, self.T
        with ExitStack() as st:
            fw = p.sb(st, [128, 88, 3], F32, "fw")
            fb = p.sb(st, [128, 88], F32, "fb")
            p.dma("sp", fw[:], self.fcw[l], writes=[fw])
            p.dma("sp", fb[:], self.fcb[l], writes=[fb])
            up = p.sbpool(st, 8, [128, 516], BF16, "fu")
            ca = p.sbpool(st, 8, [128, 512], F32, "fca")
            gTp = p.sbpool(st, 2, [128, 44, 512], BF16, "gT")
            wdp = p.sbpool(st, 2, [128, 44, 128], BF16, "wd")
            mix = p.sb(st, [128, KC, 512], F32, "fmix")
            sq = p.sbpool(st, 2, [128, 512], BF16, "fsq")
            rs = p.sb(st, [128, 512], F32, "frs")
            xp = p.sbpool(st, 8, [128, 512], F32, "fx")
            tp = p.sbpool(st, 3, [128, 512], F32, "ft")
            full_psum = self.psum
            pss = full_psum.bufs[7]
            self.psum = Pool(full_psum.bufs[0:7])
            def conv_pair(tile, i, gT):
                t0, n, m = tile
                seq_lo, seq_hi = (S, T) if m else (0, S)
                lo = max(t0 - 1, seq_lo)
                hi = min(t0 + n + 1, seq_hi)
                off = lo - (t0 - 1)
                res = []
                for half in range(2):
                    ch = half * 44 + i
                    u = up.next()
                    if off > 0:
                        p.op("dve", lambda hh: hh.memset(u[:, 0:off], 0.0), (), [u], small=True)
                    if off + hi - lo < n + 2:
                        p.op("dve", lambda hh: hh.memset(u[:, off + hi - lo:n + 2], 0.0), (), [u], small=True)
                    p.dma("sp", u[:, off:off + hi - lo], self.U[ch * 128:(ch + 1) * 128, lo:hi], writes=[u])
                    a = ca.next()
                    p.act(a[:, :n], u[:, 1:n + 1], AF.Identity, [u, fw, fb], [a], scale=fw[:, ch, 1:2], bias=fb[:, ch:ch + 1])
                    p.stt("dve", a[:, :n], u[:, 0:n], fw[:, ch, 0:1], a[:, :n], ALU.mult, ALU.add, [u, fw, a], [a])
                    p.stt("dve", a[:, :n], u[:, 2:n + 2], fw[:, ch, 2:3], a[:, :n], ALU.mult, ALU.add, [u, fw, a], [a],
                          strict=False)
                    res.append(a)
                s_ = ca.next()
                p.act(s_[:, :n], res[0][:, :n], AF.Silu, [res[0]], [s_])
                p.tt("dve", gT[:, i, :n], s_[:, :n], res[1][:, :n], ALU.mult, [s_, res[1]], [gT])

            def down_chunk(tile, c, gT):
                t0, n, m = tile
                wd = wdp.next()
                p.dma("sp", wd[:], self.Wd[l, c], writes=[wd])
                ps = self.psum.next()
                for kc in range(44):
                    p.mm(ps[:, :n], wd[:, kc, :], gT[:, kc, :n], kc == 0, kc == 43, [wd, gT], [ps])
                p.copy("act", mix[:, c, :n], ps[:, :n], [ps], [mix])
                q = sq.next()
                p.act(q[:, :n], mix[:, c, :n], AF.Square, [mix], [q])
                p.mm(pss[:, :n], self.ones_bf[:], q[:, :n], c == 0, c == KC - 1, [self.ones_bf, q], [pss], signal=True)

            gT_cur = gTp.next()
            for i in range(44):
                conv_pair(tiles[0], i, gT_cur)
            for ti, tile in enumerate(tiles):
                nxt = tiles[ti + 1] if ti + 1 < len(tiles) else None
                gT_nxt = gTp.next() if nxt is not None else None
                for c in range(16):
                    if nxt is not None:
                        for i in range(c * 44 // 16, (c + 1) * 44 // 16):
                            conv_pair(nxt, i, gT_nxt)
                    down_chunk(tile, c, gT_cur)
                self.post_residual(mix, pss, rs, tile[1], tile[0], tile[2], 5, xp, tp)
                gT_cur = gT_nxt
            self.psum = full_psum
            p.barrier()


def _col(v, nch):
    v = np.asarray(v, np.float32)
    return np.ascontiguousarray(np.swapaxes(v.reshape(v.shape[:-1] + (nch, 128)), -1, -2))


def _na_tables(na_rpb, nrows):
    NL = na_rpb.shape[0]
    nblk = nrows // 2
    wr, wc = 8, 16
    col = np.arange(GRID_W)
    c0 = np.clip(col - wc // 2, 0, GRID_W - wc)
    rep = {0: min(2, nblk - 3), 1: 0, 2: 1, 3: nblk - 2, 4: nblk - 1}
    tab = np.full((NL, NH, 5, 128, 640), NEG, np.float32)
    for v, j in rep.items():
        kb0 = kb0_of(j, nblk)
        for qi in range(128):
            r = 2 * j + qi // 64
            c = qi % 64
            r0 = min(max(r - wr // 2, 0), nrows - wr)
            for i in range(5):
                for kr in range(2):
                    rr = (kb0 + i) * 2 + kr
                    if rr < r0 or rr >= r0 + wr:
                        continue
                    cc = np.arange(c0[c], c0[c] + wc)
                    tab[:, :, v, kr * 64 + cc, i * 128 + qi] = na_rpb[:, :, rr - r + 7, :][:, :, cc - c + 15]
    return tab


def _rope_tables(S):
    t = np.arange(S)
    rows, cols = t // GRID_W, t % GRID_W
    inv = (10000.0 ** (-np.arange(0, 32, 2, dtype=np.float32) / 32)).astype(np.float32)
    C = np.zeros((128, S), np.float32)
    Sn = np.zeros((128, S), np.float32)
    perm = np.zeros((128, 128), np.float32)
    for pp in range(128):
        sub = pp % 64
        part = sub // 32
        i = sub % 16
        pos = (rows if part == 0 else cols).astype(np.float32)
        ang = pos * inv[i]
        C[pp] = np.cos(ang)
        Sn[pp] = np.sin(ang)
        first = (sub % 32) < 16
        partner = pp + 16 if first else pp - 16
        perm[partner, pp] = -1.0 if first else 1.0
    return C, Sn, perm


def prep_shared(inp, S):
    NL = inp["w_mod"].shape[0]
    f = lambda k: np.ascontiguousarray(np.asarray(inp[k], np.float32))
    C, Sn, perm = _rope_tables(S)
    sh = {
        "w_mod": f("w_mod"), "w_in": f("w_in"), "p_a": f("p_a"), "p_b": f("p_b"), "p_c": f("p_c"),
        "w_out": f("w_out"), "w_up": f("w_up"), "w_down": f("w_down"),
        "b_mod_c": _col(inp["b_mod"], 96),
        "gvec": np.ascontiguousarray(np.stack([_col(inp[k], KC) for k in
                                               ("g_pre_mix", "g_post_mix", "g_pre_ffn", "g_post_ffn")], axis=2)),
        "na_tab": _na_tables(np.asarray(inp["na_rpb"], np.float32), S // GRID_W),
        "conv_w_c": np.ascontiguousarray(np.asarray(inp["conv_w"], np.float32).reshape(NL, 31, 8, 128).transpose(0, 3, 2, 1)),
        "conv_v_c": np.ascontiguousarray(np.stack([_col(inp[k], 8) for k in ("conv_b", "conv_ln_g", "conv_ln_b")], axis=2)),
        "lamv": np.ascontiguousarray(np.broadcast_to(
            np.stack([np.asarray(inp[k], np.float32) for k in ("lam_q1", "lam_k1", "lam_q2", "lam_k2")], axis=1)[:, None],
            (NL, 128, 4, 64))),
        "dlg_c": np.ascontiguousarray(np.asarray(inp["diff_ln_g"], np.float32).reshape(NL, 128, 1)),
        "fcw_c": np.ascontiguousarray(np.asarray(inp["ffn_conv_w"], np.float32).reshape(NL, 3, 88, 128).transpose(0, 3, 2, 1)),
        "fcb_c": _col(inp["ffn_conv_b"], 88),
        "ropeC": C, "ropeS": Sn, "perm": perm, "ident": np.eye(128, dtype=np.float32),
    }
    return sh


def prep_core(inp, b):
    cv = np.stack([_col(np.asarray(inp["c"], np.float32)[b], KC), _col(np.asarray(inp["c_ctx"], np.float32), KC)], axis=2)
    return {"x": np.ascontiguousarray(np.asarray(inp["x"], np.float32)[b]),
            "ctx": np.ascontiguousarray(np.asarray(inp["ctx"], np.float32)[b]),
            "cvec": np.ascontiguousarray(cv)}


def kernel(**inputs):
    B, S, _ = inputs["x"].shape
    kern = Kern(S)
    nc = kern.build()
    sh = prep_shared(inputs, S)
    active = [0, 2, 4, 6][:B]
    real = {c: dict(sh, **prep_core(inputs, b)) for b, c in enumerate(active)}
    zero = {k: np.zeros_like(v) for k, v in real[active[0]].items()}
    in_maps = [real.get(c, zero) for c in range(8)]
    res = run_bass_kernel_spmd(nc, in_maps, core_ids=list(range(8)))
    return np.stack([np.asarray(res.results[c]["out"], np.float32) for c in active], axis=0)
```

```python
import math
from contextlib import ExitStack
import numpy as np
import concourse.bass as bass
import concourse.mybir as mybir
from concourse.bass_utils import run_bass_kernel_spmd

F32 = mybir.dt.float32
BF16 = mybir.dt.bfloat16
AF = mybir.ActivationFunctionType
ALU = mybir.AluOpType

D = 2048
KC = 16
L_CTX = 256
GRID_W = 64
NH = 8
IN_COLS = 14336
FFN = 5632
EPS = 1e-6
NDMA_SEM = 16
NEG = -30000.0
STRICT = True


class Eng:
    def __init__(self, name, h, sem, dsems):
        self.name, self.h, self.sem, self.n = name, h, sem, 0
        self.seen = {}
        self.dsems = dsems
        self.ndma = 0


class Buf:
    __slots__ = ("name", "ap", "w", "r", "t", "prev")

    def __init__(self, name, t=None):
        self.name = name
        self.t = t
        self.ap = t
        self.w = {}
        self.r = {}
        self.prev = {}

    def __getitem__(self, idx):
        return self.t[idx]


class Pool:
    def __init__(self, bufs):
        self.bufs = bufs
        self.i = 0

    def next(self):
        b = self.bufs[self.i % len(self.bufs)]
        self.i += 1
        return b


class Prog:
    def __init__(self, nc, es):
        self.nc = nc
        self.es = es
        self.E = {}
        for name, h, nd in (("pe", nc.tensor, 0), ("act", nc.scalar, 0), ("dve", nc.vector, 0),
                            ("pool", nc.gpsimd, NDMA_SEM), ("sp", nc.sync, NDMA_SEM)):
            sem = es.enter_context(nc.semaphore("sem_" + name))
            ds = [es.enter_context(nc.semaphore("dsem_%s_%d" % (name, i))) for i in range(nd)]
            self.E[name] = Eng(name, h, sem, ds)
        self.uid = 0

    def sb(self, stack, shape, dt, name=None):
        self.uid += 1
        name = (name or "t") + "_%d" % self.uid
        t = stack.enter_context(self.nc.sbuf_tensor(name, list(shape), dt))
        return Buf(name, t)

    def sbpool(self, stack, n, shape, dt, name=None):
        return Pool([self.sb(stack, shape, dt, name) for _ in range(n)])

    def ps(self, stack, shape=(128, 512), dt=F32, name=None):
        self.uid += 1
        name = (name or "ps") + "_%d" % self.uid
        t = stack.enter_context(self.nc.psum_tensor(name, list(shape), dt))
        return Buf(name, t)

    def _deps(self, eng, reads, writes, waw, strict=True):
        deps = {}

        def add(d):
            for k, (sem, val, small) in d.items():
                if sem is eng.sem and (eng.name == "pe" or not (small or (STRICT and strict))):
                    continue
                if k not in deps or deps[k][1] < val:
                    deps[k] = (sem, val)
        for b in reads:
            add(b.w)
        for b in writes:
            if b.r:
                add(b.r)
            else:
                add(b.prev)
                if waw:
                    add(b.w)
        return deps

    def _wait(self, eng, deps):
        for k, (sem, val) in deps.items():
            if eng.seen.get(k, 0) >= val:
                continue
            eng.h.wait_ge(sem, val)
            eng.seen[k] = val

    def _record(self, tok, reads, writes):
        k = id(tok[0])
        for b in reads:
            b.r[k] = tok
        for b in writes:
            if b.r:
                b.w = {k: tok}
                b.prev = b.r
                b.r = {}
            else:
                b.w[k] = tok

    def op(self, engname, fn, reads=(), writes=(), small=False, waw=True, strict=True, signal=True):
        eng = self.E[engname]
        self._wait(eng, self._deps(eng, reads, writes, waw, strict))
        ins = fn(eng.h)
        if signal:
            ins.then_inc(eng.sem, 1)
            eng.n += 1
            self._record((eng.sem, eng.n, small), reads, writes)
        else:
            self._record((eng.sem, eng.n + 1, small), reads, writes)

    def dma(self, q, out, in_, reads=(), writes=(), waw=False):
        eng = self.E[q]
        deps = self._deps(eng, reads, writes, waw)
        i = eng.ndma % NDMA_SEM
        rnd = eng.ndma // NDMA_SEM
        sem = eng.dsems[i]
        if rnd > 0:
            k = id(sem)
            if k not in deps or deps[k][1] < 16 * rnd:
                deps[k] = (sem, 16 * rnd)
        self._wait(eng, deps)
        eng.h.dma_start(out=out, in_=in_).then_inc(sem, 16)
        eng.ndma += 1
        self._record((sem, 16 * (rnd + 1), False), reads, writes)

    def barrier(self):
        toks = {}
        for e in self.E.values():
            if e.n:
                toks[id(e.sem)] = (e.sem, e.n, e)
            for i, s in enumerate(e.dsems):
                cnt = (e.ndma - i + NDMA_SEM - 1) // NDMA_SEM
                if cnt > 0:
                    toks[id(s)] = (s, 16 * cnt, None)
        for e in self.E.values():
            d = {k: (s, v) for k, (s, v, own) in toks.items() if own is not e}
            self._wait(e, d)

    def mm(self, out, lhsT, rhs, start, stop, reads, writes, signal=None):
        if signal is None:
            signal = stop
        self.op("pe", lambda h: h.matmul(out, lhsT, rhs, start=start, stop=stop), reads, writes, signal=signal)

    def act(self, out, in_, func, reads, writes, bias=None, scale=None, small=False):
        kw = {}
        if bias is not None:
            kw["bias"] = bias
        if scale is not None:
            kw["scale"] = scale
        self.op("act", lambda h: h.activation(out, in_, func, **kw), reads, writes, small=small)

    def tt(self, eng, out, in0, in1, op, reads, writes, small=False, strict=True):
        self.op(eng, lambda h: h.tensor_tensor(out, in0, in1, op), reads, writes, small=small, strict=strict)

    def ts(self, eng, out, in0, s1, s2, op0, op1, reads, writes, small=False):
        if s2 is None:
            self.op(eng, lambda h: h.tensor_scalar(out, in0, s1, None, op0), reads, writes, small=small)
        else:
            self.op(eng, lambda h: h.tensor_scalar(out, in0, s1, s2, op0, op1), reads, writes, small=small)

    def stt(self, eng, out, in0, scalar, in1, op0, op1, reads, writes, small=False, strict=True):
        self.op(eng, lambda h: h.scalar_tensor_tensor(out, in0, scalar, in1, op0, op1), reads, writes, small=small,
                strict=strict)

    def copy(self, eng, out, in_, reads, writes, small=False, strict=True):
        if eng == "act":
            self.op("act", lambda h: h.copy(out, in_), reads, writes, small=small, strict=strict)
        else:
            self.op(eng, lambda h: h.tensor_copy(out, in_), reads, writes, small=small, strict=strict)


def kb0_of(j, nblk):
    return min(max(j - 2, 0), nblk - 5)


def variant_of(j, nblk):
    if j == 0:
        return 1
    if j == 1:
        return 2
    if j == nblk - 2:
        return 3
    if j == nblk - 1:
        return 4
    return 0


class Kern:
    def __init__(self, S, NL=2, debug=()):
        self.S, self.NL = S, NL
        self.T = S + L_CTX
        self.debug = set(debug)
        nc = bass.Bass("TRN2", target_bir_lowering=False)
        self.nc = nc
        T = self.T

        def din(name, shape, dt=F32):
            return nc.dram_tensor(name, list(shape), dt, kind="ExternalInput").ap()

        def scr(name, shape, dt=BF16):
            kind = "ExternalOutput" if name in self.debug else "Internal"
            return nc.dram_tensor(name, list(shape), dt, kind=kind).ap()

        self.x = din("x", [S, D])
        self.ctx = din("ctx", [L_CTX, D])
        self.cvec = din("cvec", [128, KC, 2])
        self.w_mod = din("w_mod", [NL, D, 6 * D])
        self.b_mod = din("b_mod_c", [NL, 128, 96])
        self.gvec = din("gvec", [NL, 128, 4, KC])
        self.w_in = din("w_in", [NL, D, IN_COLS])
        self.na_tab = din("na_tab", [NL, NH, 5, 128, 640])
        self.conv_w = din("conv_w_c", [NL, 128, 8, 31])
        self.conv_v = din("conv_v_c", [NL, 128, 3, 8])
        self.lamv = din("lamv", [NL, 128, 4, 64])
        self.dlg = din("dlg_c", [NL, 128, 1])
        self.p_a = din("p_a", [NL, 1024, D])
        self.p_b = din("p_b", [NL, 1024, D])
        self.p_c = din("p_c", [NL, 1024, D])
        self.w_out = din("w_out", [NL, D, D])
        self.w_up = din("w_up", [NL, D, 2 * FFN])
        self.fcw = din("fcw_c", [NL, 128, 88, 3])
        self.fcb = din("fcb_c", [NL, 128, 88])
        self.w_down = din("w_down", [NL, FFN, D])
        self.ropeC = din("ropeC", [128, S])
        self.ropeS = din("ropeS", [128, S])
        self.perm_in = din("perm", [128, 128])
        self.ident_in = din("ident", [128, 128])
        self.out = nc.dram_tensor("out", [S, D], F32, kind="ExternalOutput").ap()
        self.Wi = scr("Wi", [NL, 28, 128, KC, 512])
        self.Pa = scr("Pa", [NL, 4, 128, 8, 512])
        self.Pb = scr("Pb", [NL, 4, 128, 8, 512])
        self.Pc = scr("Pc", [NL, 4, 128, 8, 512])
        self.Wo = scr("Wo", [NL, 4, 128, KC, 512])
        self.Wu = scr("Wu", [NL, 22, 128, KC, 512])
        self.Wd = scr("Wd", [NL, 16, 128, 44, 128])
        self.xT = scr("xT", [D, T], F32)
        self.NQ = scr("NQ", [1024, T])
        self.NK = scr("NK", [1024, T])
        self.NV = scr("NV", [NH, 128, T // 128, 128])
        self.GLU = scr("GLU", [1024, T])
        self.DQ = scr("DQ", [1024, T])
        self.DK = scr("DK", [1024, T])
        self.DV = scr("DV", [NH, 128, T // 128, 128])
        self.G = scr("G", [3 * D, T])
        self.OA = scr("OA", [1024, T])
        self.OB = scr("OB", [1024, T])
        self.OC = scr("OC", [1024, T])
        self.U = scr("U", [2 * FFN, T])

    def build(self, phases=None):
        nc = self.nc
        with ExitStack() as es:
            p = Prog(nc, es)
            self.p = p
            self.ones_bf = p.sb(es, [128, 128], BF16, "ones_bf")
            self.ones_f = p.sb(es, [128, 128], F32, "ones_f")
            self.ident = p.sb(es, [128, 128], F32, "ident")
            self.perm = p.sb(es, [128, 128], BF16, "perm")
            self.cs = p.sb(es, [128, KC, 2], F32, "cs")
            self.modc = p.sb(es, [128, 6, 2, KC], F32, "modc")
            self.lam = p.sb(es, [128, 2], F32, "lam")
            self.dlgs = p.sb(es, [128, 1], F32, "dlgs")
            self.psT = [p.ps(es, shape=(128, 1024)) for _ in range(4)]
            halves = []
            for T in self.psT:
                halves.append(Buf(T.name + "a", T.t[:, 0:512]))
                halves.append(Buf(T.name + "b", T.t[:, 512:1024]))
            self.psum = Pool(halves)
            self.cb_vals = [D * EPS, EPS, 128 * EPS]
            self.cbias = p.sb(es, [128, len(self.cb_vals)], F32, "cbias")
            for i, v in enumerate(self.cb_vals):
                p.op("dve", lambda h, i=i, v=v: h.memset(self.cbias[:, i:i + 1], v), (), [self.cbias], small=True)
            p.op("dve", lambda h: h.memset(self.ones_f[:], 1.0), (), [self.ones_f])
            p.op("dve", lambda h: h.memset(self.ones_bf[:], 1.0), (), [self.ones_bf])
            p.dma("sp", self.ident[:], self.ident_in, writes=[self.ident])
            p.dma("pool", self.perm[:], self.perm_in, writes=[self.perm])
            p.dma("sp", self.cs[:], self.cvec, writes=[self.cs])
            with ExitStack() as st:
                sg = p.sb(st, [128, KC, 2], F32, "sg")
                p.act(sg[:], self.cs[:], AF.Sigmoid, [self.cs], [sg], small=True)
                p.tt("dve", self.cs[:], self.cs[:], sg[:], ALU.mult, [self.cs, sg], [self.cs], small=True)
                p.barrier()
            S, T = self.S, self.T
            lat_tiles = [(t0, 512, 0) for t0 in range(0, S, 512)]
            ctx_tile = (S, L_CTX, 1)
            ph = phases
            if ph is None or "wconv" in ph:
                self.phase_wconv([0], False)
            if ph is None or "tin" in ph:
                self.phase_tin()
            else:
                p.barrier()
            if (ph is None or "wconv" in ph) and self.NL > 1:
                self.phase_wconv(list(range(1, self.NL)), False)
            for l in range(self.NL):
                last = l == self.NL - 1
                if ph is None or "mod" in ph:
                    self.phase_mod(l)
                if ph is None or "A" in ph:
                    self.phase_A(l, lat_tiles + [ctx_tile], last)
                tl = lat_tiles + ([] if last else [ctx_tile])
                if ph is None or "NA" in ph:
                    self.phase_NA(l, tl)
                if ph is None or "CF" in ph:
                    self.phase_conformer(l, tl)
                if ph is None or "DF" in ph:
                    self.phase_diff(l, tl)
                if ph is None or "MG" in ph:
                    self.phase_merge(l, tl)
                if ph is None or "C1" in ph:
                    self.phase_ffn_up(l, tl)
                if ph is None or "C2" in ph:
                    self.phase_ffn_down(l, tl)
            if ph is None or "tout" in ph:
                self.phase_tout()
            p.barrier()
        return nc

    def phase_wconv(self, layers, barrier=True):
        p = self.p
        for l in layers:
            def cv(dst, src, kcn, ng, cols=512):
                v = src.rearrange("(kc p) (g c) -> g p kc c", p=128, c=cols)
                for g in range(ng):
                    p.dma("pool", dst[g], v[g])
            cv(self.Wi[l], self.w_in[l], KC, 28)
            cv(self.Pa[l], self.p_a[l], 8, 4)
            cv(self.Pb[l], self.p_b[l], 8, 4)
            cv(self.Pc[l], self.p_c[l], 8, 4)
            cv(self.Wo[l], self.w_out[l], KC, 4)
            cv(self.Wu[l], self.w_up[l], KC, 22)
            cv(self.Wd[l], self.w_down[l], 44, 16, cols=128)
        if barrier:
            p.barrier()

    def phase_tin(self):
        p = self.p
        S = self.S
        with ExitStack() as st:
            xin = p.sbpool(st, 8, [128, D], F32, "xin")
            xo = p.sbpool(st, 4, [128, 512], F32, "xo")
            tiles = [(self.x, t0, 512, t0) for t0 in range(0, S, 512)] + [(self.ctx, 0, L_CTX, S)]
            cnt = 0
            for (src, r0, n, c0) in tiles:
                nb = n // 128
                xs = []
                for i in range(nb):
                    b = xin.next()
                    p.dma("sp", b[:], src[r0 + i * 128: r0 + (i + 1) * 128, :], writes=[b])
                    xs.append(b)
                for fc in range(KC):
                    ps = self.psum.next()
                    for i in range(nb):
                        p.op("pe", lambda h, i=i, fc=fc, ps=ps: h.transpose(
                            ps[:, i * 128:(i + 1) * 128], xs[i][:, fc * 128:(fc + 1) * 128], self.ident[:]),
                            [xs[i], self.ident], [ps])
                    o = xo.next()
                    eng = "act" if cnt % 2 == 0 else "dve"
                    cnt += 1
                    p.copy(eng, o[:, :n], ps[:, :n], [ps], [o])
                    p.dma("sp", self.xT[fc * 128:(fc + 1) * 128, c0:c0 + n], o[:, :n], reads=[o])
            p.barrier()

    def phase_tout(self):
        p = self.p
        S = self.S
        xTv = self.xT.rearrange("(kc p) t -> p kc t", p=128)
        with ExitStack() as st:
            xi = p.sbpool(st, 2, [128, KC, 512], F32, "xi")
            xo = p.sbpool(st, 3, [128, D], F32, "xo2")
            cnt = 0
            for t0 in range(0, S, 512):
                b = xi.next()
                p.dma("sp", b[:], xTv[:, :, t0:t0 + 512], writes=[b])
                for i in range(4):
                    o = xo.next()
                    for f4 in range(4):
                        ps = self.psum.next()
                        for k in range(4):
                            fc = f4 * 4 + k
                            p.op("pe", lambda h, ps=ps, k=k, fc=fc, i=i, b=b: h.transpose(
                                ps[:, k * 128:(k + 1) * 128], b[:, fc, i * 128:(i + 1) * 128], self.ident[:]),
                                [b, self.ident], [ps])
                        eng = "act" if cnt % 2 == 0 else "dve"
                        cnt += 1
                        p.copy(eng, o[:, f4 * 512:(f4 + 1) * 512], ps[:], [ps], [o])
                    p.dma("pool", self.out[t0 + i * 128: t0 + (i + 1) * 128, :], o[:], reads=[o])
            p.barrier()

    def phase_mod(self, l):
        p = self.p
        with ExitStack() as st:
            wm = p.sbpool(st, 2, [128, KC, 512], F32, "wm")
            modv = p.sb(st, [128, 96, 2], F32, "modv")
            bm = p.sb(st, [128, 96], F32, "bm")
            gv = p.sb(st, [128, 4, KC], F32, "gv")
            lv = p.sb(st, [128, 4, 64], F32, "lv")
            lt = p.sb(st, [128, 2, 64], F32, "lt")
            ls = p.sb(st, [128, 2], F32, "ls")
            p.dma("sp", bm[:], self.b_mod[l], writes=[bm])
            p.dma("sp", gv[:], self.gvec[l], writes=[gv])
            p.dma("sp", self.dlgs[:], self.dlg[l], writes=[self.dlgs])
            p.dma("sp", lv[:], self.lamv[l], writes=[lv])
            wv = self.w_mod[l].rearrange("(kc p) (g c) -> g p kc c", p=128, c=512)
            ps = self.psum.next()
            for g in range(24):
                w = wm.next()
                p.dma("sp", w[:], wv[g], writes=[w])
                for j in range(4):
                    oc = g * 4 + j
                    for kc in range(KC):
                        p.mm(ps[:, oc * 2:oc * 2 + 2], w[:, kc, j * 128:(j + 1) * 128], self.cs[:, kc, :],
                             kc == 0, kc == KC - 1, [w, self.cs], [ps])
            psv = ps[:, 0:192].rearrange("p (o n) -> p o n", n=2)
            for n in range(2):
                p.tt("dve", modv[:, :, n], psv[:, :, n], bm[:], ALU.add, [ps, bm], [modv], small=True)
            sD = math.sqrt(D)
            mc = self.modc
            for n in range(2):
                m = lambda j: modv[:, j * 16:(j + 1) * 16, n]
                p.stt("dve", mc[:, 0, n, :], m(1), 1.0, gv[:, 0, :], ALU.add, ALU.mult, [modv, gv], [mc], small=True)
                p.ts("dve", mc[:, 0, n, :], mc[:, 0, n, :], sD, None, ALU.mult, None, [mc], [mc], small=True)
                p.copy("dve", mc[:, 1, n, :], m(0), [modv], [mc], small=True)
                p.stt("dve", mc[:, 2, n, :], m(2), sD, gv[:, 1, :], ALU.mult, ALU.mult, [modv, gv], [mc], small=True)
                p.stt("dve", mc[:, 3, n, :], m(4), 1.0, gv[:, 2, :], ALU.add, ALU.mult, [modv, gv], [mc], small=True)
                p.ts("dve", mc[:, 3, n, :], mc[:, 3, n, :], sD, None, ALU.mult, None, [mc], [mc], small=True)
                p.copy("dve", mc[:, 4, n, :], m(3), [modv], [mc], small=True)
                p.stt("dve", mc[:, 5, n, :], m(5), sD, gv[:, 3, :], ALU.mult, ALU.mult, [modv, gv], [mc], small=True)
            lam_init = 0.8 - 0.6 * math.exp(-0.3 * l)
            p.tt("dve", lt[:, 0, :], lv[:, 0, :], lv[:, 1, :], ALU.mult, [lv], [lt], small=True)
            p.tt("dve", lt[:, 1, :], lv[:, 2, :], lv[:, 3, :], ALU.mult, [lv], [lt], small=True)
            p.op("dve", lambda h: h.tensor_reduce(ls[:], lt[:], mybir.AxisListType.X, ALU.add), [lt], [ls], small=True)
            p.act(ls[:], ls[:], AF.Exp, [ls], [ls], small=True)
            p.stt("dve", self.lam[:, 0:1], ls[:, 1:2], -lam_init, ls[:, 0:1], ALU.add, ALU.subtract,
                  [ls], [self.lam], small=True)
            p.ts("dve", self.dlgs[:], self.dlgs[:], (1.0 - lam_init) * math.sqrt(128.0), None, ALU.mult, None,
                 [self.dlgs], [self.dlgs], small=True)
            p.barrier()

    def rsqrt(self, out, in_, c, reads, obuf, small=False):
        p = self.p
        p.act(out, in_, AF.Sqrt, reads + [self.cbias], [obuf], bias=self.cbias[:, self.cbias_idx(c):self.cbias_idx(c) + 1], small=small)
        p.op("dve", lambda h: h.reciprocal(out, out), [obuf], [obuf], small=small)

    def cbias_idx(self, c):
        return self.cb_vals.index(c)

    def norm_mod(self, xt, hT, n, ia, ib, m, sq, rs, tmp):
        p = self.p
        ps = self.psum.next()
        for kc in range(KC):
            q = sq.next()
            p.act(q[:, :n], xt[:, kc, :n], AF.Square, [xt], [q])
            p.mm(ps[:, :n], self.ones_bf[:], q[:, :n], kc == 0, kc == KC - 1, [self.ones_bf, q], [ps], signal=True)
        self.rsqrt(rs[:, :n], ps[:, :n], D * EPS, [ps], rs)
        mc = self.modc
        for kc in range(KC):
            t = tmp.next()
            p.stt("dve", t[:, :n], xt[:, kc, :n], mc[:, ia, m, kc:kc + 1], rs[:, :n], ALU.mult, ALU.mult,
                  [xt, mc, rs], [t])
            p.act(hT[:, kc, :n], t[:, :n], AF.Identity, [t, mc], [hT], bias=mc[:, ib, m, kc:kc + 1])

    def phase_A(self, l, tiles, last):
        p = self.p
        S = self.S
        xTv = self.xT.rearrange("(kc p) t -> p kc t", p=128)
        with ExitStack() as st:
            xpool = p.sbpool(st, 2, [128, KC, 512], F32, "xa")
            sq = p.sbpool(st, 2, [128, 512], BF16, "sq")
            rs = p.sb(st, [128, 512], F32, "rs")
            tmp = p.sbpool(st, 2, [128, 512], F32, "tmpa")
            hpool = p.sbpool(st, 2, [128, KC, 512], BF16, "hT")
            wpool = p.sbpool(st, 3, [128, KC, 512], BF16, "wa")
            abuf = p.sb(st, [128, 8, 512], F32, "abuf")
            stg = p.sbpool(st, 4, [128, 512], BF16, "stg")
            f32t = p.sbpool(st, 3, [128, 512], F32, "f32t")
            rc = p.sbpool(st, 2, [128, 512], F32, "rc")
            rsn = p.sbpool(st, 2, [128, 512], F32, "rsn")
            cnt = 0
            def prep(tile):
                t0_, n_, m_ = tile
                xt_ = xpool.next()
                p.dma("sp", xt_[:, :, :n_], xTv[:, :, t0_:t0_ + n_], writes=[xt_])
                hT_ = hpool.next()
                self.norm_mod(xt_, hT_, n_, 0, 1, m_, sq, rs, tmp)
                return hT_
            hT_next = prep(tiles[0])
            for ti, (t0, n, m) in enumerate(tiles):
                hT = hT_next
                if not m:
                    rcb, rsb = rc.next(), rsn.next()
                    p.dma("sp", rcb[:, :n], self.ropeC[:, t0:t0 + n], writes=[rcb])
                    p.dma("sp", rsb[:, :n], self.ropeS[:, t0:t0 + n], writes=[rsb])
                groups = range(28)
                import os
                if os.environ.get("KGROUPS"):
                    groups = [int(v) for v in os.environ["KGROUPS"].split(",")]
                if m and last:
                    groups = [2, 3, 4, 5, 12, 13, 14, 15]
                groups = list(groups)
                for gi, g in enumerate(groups):
                    if gi == len(groups) // 2 and ti + 1 < len(tiles):
                        hT_next = prep(tiles[ti + 1])
                    w = wpool.next()
                    p.dma("sp", w[:], self.Wi[l, g], writes=[w])
                    if g in (4, 5, 14, 15):
                        dst = self.NV if g < 6 else self.DV
                        hh = (g - 4) * 4 if g < 6 else (g - 14) * 4
                        for tb in range(n // 128):
                            ps = self.psum.next()
                            for kc in range(KC):
                                p.mm(ps[:, :], hT[:, kc, tb * 128:(tb + 1) * 128], w[:, kc, :], kc == 0, kc == KC - 1,
                                     [hT, w], [ps])
                            o = stg.next()
                            eng = "act" if cnt % 2 == 0 else "dve"
                            cnt += 1
                            p.copy(eng, o[:], ps[:], [ps], [o])
                            blk = (t0 + tb * 128) // 128
                            p.dma("pool", dst[hh:hh + 4, :, blk, :].rearrange("h p c -> p h c"),
                                  o[:].rearrange("p (h c) -> p h c", c=128), reads=[o])
                        continue
                    for j in range(4):
                        c = g * 4 + j
                        ps = self.psum.next()
                        for kc in range(KC):
                            p.mm(ps[:, :n], w[:, kc, j * 128:(j + 1) * 128], hT[:, kc, :n], kc == 0, kc == KC - 1,
                                 [w, hT], [ps])
                        if c < 16:
                            dst = self.NQ if c < 8 else self.NK
                            o = stg.next()
                            eng = "act" if cnt % 2 == 0 else "dve"
                            cnt += 1
                            p.copy(eng, o[:, :n], ps[:, :n], [ps], [o])
                            r = (c % 8) * 128
                            p.dma("pool", dst[r:r + 128, t0:t0 + n], o[:, :n], reads=[o])
                        elif c < 32:
                            p.copy("act", abuf[:, c - 24, :n], ps[:, :n], [ps], [abuf])
                        elif c < 40:
                            f = f32t.next()
                            p.act(f[:, :n], ps[:, :n], AF.Sigmoid, [ps], [f])
                            o = stg.next()
                            p.tt("dve", o[:, :n], abuf[:, c - 32, :n], f[:, :n], ALU.mult, [abuf, f], [o])
                            r = (c - 32) * 128
                            p.dma("pool", self.GLU[r:r + 128, t0:t0 + n], o[:, :n], reads=[o])
                        elif c < 56:
                            dst = self.DQ if c < 48 else self.DK
                            r = (c % 8) * 128
                            xb = stg.next()
                            p.copy("act", xb[:, :n], ps[:, :n], [ps], [xb])
                            if m:
                                p.dma("pool", dst[r:r + 128, t0:t0 + n], xb[:, :n], reads=[xb])
                            else:
                                ps2 = self.psum.next()
                                p.mm(ps2[:, :n], self.perm[:], xb[:, :n], True, True, [self.perm, xb], [ps2])
                                f1 = f32t.next()
                                p.tt("dve", f1[:, :n], ps2[:, :n], rsb[:, :n], ALU.mult, [ps2, rsb], [f1])
                                f2 = f32t.next()
                                p.tt("dve", f2[:, :n], xb[:, :n], rcb[:, :n], ALU.mult, [xb, rcb], [f2])
                                o = stg.next()
                                p.tt("dve", o[:, :n], f1[:, :n], f2[:, :n], ALU.add, [f1, f2], [o])
                                p.dma("pool", dst[r:r + 128, t0:t0 + n], o[:, :n], reads=[o])
                        else:
                            o = stg.next()
                            p.act(o[:, :n], ps[:, :n], AF.Sigmoid, [ps], [o])
                            r = (c - 64) * 128
                            p.dma("pool", self.G[r:r + 128, t0:t0 + n], o[:, :n], reads=[o])
            p.barrier()

    def phase_NA(self, l, tiles):
        p = self.p
        S, T = self.S, self.T
        nblk = S // 128
        scale = 128 ** -0.5
        NKv = self.NK.rearrange("(h p) t -> p h t", p=128)
        NQv = self.NQ.rearrange("(h p) t -> p h t", p=128)
        OAv = self.OA.rearrange("(h p) t -> p h t", p=128)
        with ExitStack() as st:
            kc_sb = p.sb(st, [128, NH, 256], BF16, "kcs")
            vc_sb = p.sb(st, [128, NH, 2, 128], BF16, "vcs")
            p.dma("sp", kc_sb[:], NKv[:, :, S:T], writes=[kc_sb])
            p.dma("sp", vc_sb[:], self.NV[:, :, nblk:nblk + 2, :].rearrange("h p b c -> p h b c"), writes=[vc_sb])
            kpool = p.sbpool(st, 2, [128, NH, 1024], BF16, "nak")
            vpool = p.sbpool(st, 2, [128, NH, 8, 128], BF16, "nav")
            qpool = p.sbpool(st, 2, [128, NH, 512], BF16, "naq")
            tabp = p.sbpool(st, 3, [128, 640], F32, "tab")
            sbp = p.sbpool(st, 3, [128, 896], F32, "nsb")
            ep = p.sbpool(st, 3, [128, 896], BF16, "nae")
            rzp = p.sbpool(st, 3, [128, 128], F32, "rz")
            oap = p.sbpool(st, 2, [128, NH, 512], BF16, "oat")
            tab0 = p.sb(st, [128, NH, 640], F32, "tab0")
            p.dma("sp", tab0[:], self.na_tab[l, :, 0].rearrange("h p c -> p h c"), writes=[tab0])
            for (t0, n, m) in tiles:
                qt = qpool.next()
                p.dma("sp", qt[:, :, :n], NQv[:, :, t0:t0 + n], writes=[qt])
                oa = oap.next()
                if not m:
                    j0 = t0 // 128
                    lo = kb0_of(j0, nblk)
                    hi = kb0_of(j0 + 3, nblk) + 5
                    nb = hi - lo
                    kt = kpool.next()
                    p.dma("sp", kt[:, :, :nb * 128], NKv[:, :, lo * 128:hi * 128], writes=[kt])
                    vt = vpool.next()
                    p.dma("sp", vt[:, :, :nb, :], self.NV[:, :, lo:hi, :].rearrange("h p b c -> p h b c"), writes=[vt])
                def stage1(h, jj):
                    q_ap = qt[:, h, jj * 128:(jj + 1) * 128]
                    e = ep.next()
                    if not m:
                        j = j0 + jj
                        kb = kb0_of(j, nblk) - lo
                        vr = variant_of(j, nblk)
                        if vr == 0:
                            tb = tab0
                            tbv = tab0[:, h, :]
                        else:
                            tb = tabp.next()
                            p.dma("sp", tb[:], self.na_tab[l, h, vr], writes=[tb])
                            tbv = tb[:]
                        psA = self.psum.next()
                        psB = self.psum.next()
                        for i in range(5):
                            dst = psA[:, i * 128:(i + 1) * 128] if i < 4 else psB[:, 0:128]
                            p.mm(dst, kt[:, h, (kb + i) * 128:(kb + i + 1) * 128], q_ap, True, True,
                                 [kt, qt], [psA if i < 4 else psB])
                        for i in range(2):
                            p.mm(psB[:, (1 + i) * 128:(2 + i) * 128], kc_sb[:, h, i * 128:(i + 1) * 128], q_ap,
                                 True, True, [kc_sb, qt], [psB])
                        sb = sbp.next()
                        p.stt("dve", sb[:, 0:512], psA[:, :], scale, tbv[:, 0:512], ALU.mult, ALU.add, [psA, tb], [sb])
                        p.stt("dve", sb[:, 512:640], psB[:, 0:128], scale, tbv[:, 512:640], ALU.mult, ALU.add,
                              [psB, tb], [sb])
                        p.ts("dve", sb[:, 640:896], psB[:, 128:384], scale, None, ALU.mult, None, [psB], [sb])
                        p.act(e[:, 0:896], sb[:], AF.Exp, [sb], [e])
                        nkb = 7
                        lhs_v = lambda i: vt[:, h, kb + i, :] if i < 5 else vc_sb[:, h, i - 5, :]
                        vreads = [vt, vc_sb]
                    else:
                        psB = self.psum.next()
                        for i in range(2):
                            p.mm(psB[:, i * 128:(i + 1) * 128], kc_sb[:, h, i * 128:(i + 1) * 128], q_ap,
                                 True, True, [kc_sb, qt], [psB])
                        p.act(e[:, 0:256], psB[:, 0:256], AF.Exp, [psB], [e], scale=scale)
                        nkb = 2
                        lhs_v = lambda i: vc_sb[:, h, i, :]
                        vreads = [vc_sb]
                    return (h, jj, e, nkb, lhs_v, vreads)

                def stage2(h, jj, e, nkb, lhs_v, vreads):
                    psO = self.psum.next()
                    psZ = self.psum.next()
                    for i in range(nkb):
                        p.mm(psO[:, 0:128], lhs_v(i), e[:, i * 128:(i + 1) * 128], i == 0, i == nkb - 1,
                             vreads + [e], [psO])
                        p.mm(psZ[:, 0:128], self.ones_bf[:], e[:, i * 128:(i + 1) * 128], i == 0, i == nkb - 1,
                             [self.ones_bf, e], [psZ])
                    rz = rzp.next()
                    p.op("dve", lambda hh: hh.reciprocal(rz[:], psZ[:, 0:128]), [psZ], [rz])
                    p.tt("dve", oa[:, h, jj * 128:(jj + 1) * 128], psO[:, 0:128], rz[:], ALU.mult, [psO, rz], [oa])
                prev = None
                for h in range(NH):
                    for jj in range(n // 128):
                        cur = stage1(h, jj)
                        if prev is not None:
                            stage2(*prev)
                        prev = cur
                stage2(*prev)
                p.dma("pool", OAv[:, :, t0:t0 + n], oa[:, :, :n], reads=[oa])
            p.barrier()

    def phase_conformer(self, l, tiles):
        p = self.p
        S, T = self.S, self.T
        OBv = self.OB.rearrange("(c p) t -> p c t", p=128)
        with ExitStack() as st:
            cw = p.sb(st, [128, 8, 31], F32, "cw")
            cvv = p.sb(st, [128, 3, 8], F32, "cvv")
            p.dma("sp", cw[:], self.conv_w[l], writes=[cw])
            p.dma("sp", cvv[:], self.conv_v[l], writes=[cvv])
            glp = p.sbpool(st, 5, [128, 544], BF16, "gl")
            cps = Pool(self.psum.bufs[0:6])
            identb = p.sb(st, [128, 128], BF16, "identb")
            p.copy("dve", identb[:], self.ident[:], [self.ident], [identb])
            dg = p.sb(st, [128, 8, 31, 128], BF16, "dg")
            for i in range(8):
                for j in range(31):
                    p.ts("dve", dg[:, i, j, :], identb[:], cw[:, i, j:j + 1], None, ALU.mult, None, [identb, cw], [dg])
            acc = [p.sb(st, [128, 512], F32, "cacc") for _ in range(8)]
            sqp = p.sbpool(st, 2, [128, 512], F32, "csq")
            mu = p.sb(st, [128, 512], F32, "mu")
            var = p.sb(st, [128, 512], F32, "var")
            rstd = p.sb(st, [128, 512], F32, "rstd")
            tp = p.sbpool(st, 3, [128, 512], F32, "ct")
            obp = p.sbpool(st, 2, [128, 8, 512], BF16, "obt")
            for (t0, n, m) in tiles:
                seq_lo, seq_hi = (S, T) if m else (0, S)
                lo = max(t0 - 15, seq_lo)
                hi = min(t0 + n + 15, seq_hi)
                off = lo - (t0 - 15)
                psM = self.psum.bufs[6]
                psQ = self.psum.bufs[7]
                for i in range(8):
                    gl = glp.next()
                    if off > 0:
                        p.op("dve", lambda hh: hh.memset(gl[:, 0:off], 0.0), (), [gl], small=True)
                    if off + hi - lo < n + 30:
                        p.op("dve", lambda hh: hh.memset(gl[:, off + hi - lo:n + 30], 0.0), (), [gl], small=True)
                    p.dma("sp", gl[:, off:off + hi - lo], self.GLU[i * 128:(i + 1) * 128, lo:hi], writes=[gl])
                    a = acc[i]
                    psc = cps.next()
                    for j in range(31):
                        p.mm(psc[:, :n], dg[:, i, j, :], gl[:, j:j + n], j == 0, j == 30, [dg, gl], [psc])
                    p.act(a[:, :n], psc[:, :n], AF.Identity, [psc, cvv], [a], bias=cvv[:, 0, i:i + 1])
                    sq = sqp.next()
                    p.act(sq[:, :n], a[:, :n], AF.Square, [a], [sq])
                    p.mm(psM[:, :n], self.ones_f[:], a[:, :n], i == 0, i == 7, [self.ones_f, a], [psM], signal=True)
                    p.mm(psQ[:, :n], self.ones_f[:], sq[:, :n], i == 0, i == 7, [self.ones_f, sq], [psQ], signal=True)
                p.ts("dve", mu[:, :n], psM[:, :n], 1.0 / 1024, None, ALU.mult, None, [psM], [mu])
                p.tt("dve", var[:, :n], mu[:, :n], mu[:, :n], ALU.mult, [mu], [var])
                p.stt("dve", var[:, :n], psQ[:, :n], 1.0 / 1024, var[:, :n], ALU.mult, ALU.subtract, [psQ, var], [var])
                self.rsqrt(rstd[:, :n], var[:, :n], EPS, [var], rstd)
                ob = obp.next()
                for i in range(8):
                    t = tp.next()
                    p.tt("dve", t[:, :n], acc[i][:, :n], mu[:, :n], ALU.subtract, [acc[i], mu], [t])
                    p.tt("dve", t[:, :n], t[:, :n], rstd[:, :n], ALU.mult, [t, rstd], [t])
                    p.act(ob[:, i, :n], t[:, :n], AF.Silu, [t, cvv], [ob], scale=cvv[:, 1, i:i + 1], bias=cvv[:, 2, i:i + 1])
                p.dma("pool", OBv[:, :, t0:t0 + n], ob[:, :, :n], reads=[ob])
            p.barrier()

    def phase_diff(self, l, tiles):
        p = self.p
        S, T = self.S, self.T
        scale = 64 ** -0.5
        DQv = self.DQ.rearrange("(h p) t -> p h t", p=128)
        OCv = self.OC.rearrange("(h p) t -> p h t", p=128)
        with ExitStack() as st:
            qpool = p.sbpool(st, 2, [128, NH, 512], BF16, "dq")
            kpool = p.sbpool(st, 2, [128, T], BF16, "dk")
            vpool = p.sbpool(st, 2, [128, T // 128, 128], BF16, "dv")
            ocp = p.sbpool(st, 2, [128, NH, 512], BF16, "oct")
            fp = p.sbpool(st, 6, [128, 512], F32, "df")
            acc = self.psum.bufs[0:2]
            pairs = [(self.psT[i], self.psum.bufs[2 * i], self.psum.bufs[2 * i + 1]) for i in (1, 2, 3)]
            pi = 0
            z = p.sb(st, [128, 2, 512], F32, "z")
            ep = p.sbpool(st, 8, [128, 2, 512], BF16, "e12")
            esp = p.sbpool(st, 2, [128, 2, 512], BF16, "esum")
            for (t0, n, m) in tiles:
                qt = qpool.next()
                p.dma("sp", qt[:, :, :n], DQv[:, :, t0:t0 + n], writes=[qt])
                kts = list(range(S // 128, T // 128)) if m else list(range(T // 128))
                oc = ocp.next()
                for h in range(NH):
                    kb = kpool.next()
                    p.dma("sp", kb[:], self.DK[h * 128:(h + 1) * 128, :], writes=[kb])
                    vb = vpool.next()
                    p.dma("sp", vb[:], self.DV[h], writes=[vb])
                    O1, O2 = acc

                    pend = []

                    def pv(e, kt, first, lastk):
                        p.mm(O1[:, :n], vb[:, kt, :], e[:, 0, :n], first, lastk, [vb, e], [O1], signal=False)
                        p.mm(O2[:, :n], vb[:, kt, :], e[:, 1, :n], first, lastk, [vb, e], [O2], signal=True)
                        pend.append((e, first))
                        if len(pend) == 2:
                            (ea, fa), (eb, _) = pend
                            del pend[:]
                            es = esp.next()
                            p.tt("dve", es[:, :, :n], ea[:, :, :n], eb[:, :, :n], ALU.add, [ea, eb], [es])
                            if fa:
                                p.copy("dve", z[:, :, :n], es[:, :, :n], [es], [z])
                            else:
                                p.tt("dve", z[:, :, :n], z[:, :, :n], es[:, :, :n], ALU.add, [z, es], [z], strict=False)
                    prev = []
                    for idx, kt in enumerate(kts):
                        TT, s1, s2 = pairs[pi % 3]
                        pi += 1
                        p.mm(s1[:, :n], kb[0:64, kt * 128:(kt + 1) * 128], qt[0:64, h, :n], True, True, [kb, qt], [s1],
                             signal=False)
                        p.mm(s2[:, :n], kb[64:128, kt * 128:(kt + 1) * 128], qt[64:128, h, :n], True, True, [kb, qt], [s2])
                        e = ep.next()
                        p.act(e[:, :, :n], TT.t[:].rearrange("p (b c) -> p b c", b=2)[:, :, :n], AF.Exp, [s1, s2], [e],
                              scale=scale)
                        prev.append((e, kt, idx == 0, idx == len(kts) - 1))
                        if len(prev) > 2:
                            pv(*prev.pop(0))
                    while prev:
                        pv(*prev.pop(0))
                    z1 = z[:, 0, :]
                    z2 = z[:, 1, :]
                    z1b = z2b = z
                    _, Z1, Z2 = pairs[pi % 3]
                    pi += 1
                    p.mm(Z1[:, :n], self.ones_f[:], z1[:, :n], True, True, [self.ones_f, z], [Z1])
                    p.mm(Z2[:, :n], self.ones_f[:], z2[:, :n], True, True, [self.ones_f, z], [Z2])
                    r1 = fp.next()
                    p.op("dve", lambda hh: hh.reciprocal(r1[:, :n], Z1[:, :n]), [Z1], [r1])
                    t1 = fp.next()
                    p.tt("dve", t1[:, :n], O1[:, :n], r1[:, :n], ALU.mult, [O1, r1], [t1])
                    r2 = fp.next()
                    p.op("dve", lambda hh: hh.reciprocal(r2[:, :n], Z2[:, :n]), [Z2], [r2])
                    t2 = fp.next()
                    p.tt("dve", t2[:, :n], O2[:, :n], r2[:, :n], ALU.mult, [O2, r2], [t2])
                    p.stt("dve", t1[:, :n], t2[:, :n], self.lam[:, 0:1], t1[:, :n], ALU.mult, ALU.add,
                          [t2, self.lam, t1], [t1])
                    p.act(r1[:, :n], t1[:, :n], AF.Square, [t1], [r1])
                    _, psS, _unused = pairs[pi % 3]
                    pi += 1
                    p.mm(psS[:, :n], self.ones_f[:], r1[:, :n], True, True, [self.ones_f, r1], [psS])
                    self.rsqrt(r2[:, :n], psS[:, :n], 128 * EPS, [psS], r2)
                    p.stt("dve", oc[:, h, :n], t1[:, :n], self.dlgs[:, 0:1], r2[:, :n], ALU.mult, ALU.mult,
                          [t1, self.dlgs, r2], [oc])
                p.dma("pool", OCv[:, :, t0:t0 + n], oc[:, :, :n], reads=[oc])
            p.barrier()

    def post_residual(self, mix, pss, rs, n, t0, m, ig, xp, tp):
        p = self.p
        mc = self.modc
        self.rsqrt(rs[:, :n], pss[:, :n], D * EPS, [pss], rs)
        for kc in range(KC):
            xt = xp.next()
            p.dma("sp", xt[:, :n], self.xT[kc * 128:(kc + 1) * 128, t0:t0 + n], writes=[xt])
            t = tp.next()
            p.stt("dve", t[:, :n], mix[:, kc, :n], mc[:, ig, m, kc:kc + 1], rs[:, :n], ALU.mult, ALU.mult,
                  [mix, mc, rs], [t])
            p.tt("dve", xt[:, :n], xt[:, :n], t[:, :n], ALU.add, [xt, t], [xt])
            p.dma("pool", self.xT[kc * 128:(kc + 1) * 128, t0:t0 + n], xt[:, :n], reads=[xt])

    def phase_merge(self, l, tiles):
        p = self.p
        Gv = self.G.rearrange("(br c p) t -> p br c t", p=128, c=16)
        with ExitStack() as st:
            bp = [p.sbpool(st, 1, [128, 8, 512], BF16, "mb%d" % i) for i in range(3)]
            wp = [p.sbpool(st, 2, [128, 8, 512], BF16, "mw%d" % i) for i in range(3)]
            gtp = p.sbpool(st, 3, [128, 3, 512], BF16, "gt")
            tp = p.sbpool(st, 6, [128, 512], F32, "mt")
            y = p.sb(st, [128, KC, 512], BF16, "y")
            wop = p.sbpool(st, 2, [128, KC, 512], BF16, "wo")
            mix = p.sb(st, [128, KC, 512], F32, "mix")
            sq = p.sbpool(st, 2, [128, 512], BF16, "msq")
            rs = p.sb(st, [128, 512], F32, "mrs")
            xp = p.sbpool(st, 8, [128, 512], F32, "mx")
            srcs = [self.OA, self.OB, self.OC]
            Ws = [self.Pa, self.Pb, self.Pc]
            full_psum = self.psum
            pss = full_psum.bufs[7]
            self.psum = Pool(full_psum.bufs[0:7])
            for (t0, n, m) in tiles:
                br = []
                for i in range(3):
                    b = bp[i].next()
                    p.dma("sp", b[:, :, :n], srcs[i].rearrange("(c p) t -> p c t", p=128)[:, :, t0:t0 + n], writes=[b])
                    br.append(b)
                for og in range(4):
                    ws = []
                    for i in range(3):
                        w = wp[i].next()
                        p.dma("sp", w[:], Ws[i][l, og], writes=[w])
                        ws.append(w)
                    for j in range(4):
                        c = og * 4 + j
                        gt = gtp.next()
                        p.dma("sp", gt[:, :, :n], Gv[:, :, c, t0:t0 + n], writes=[gt])
                        ts_ = []
                        for i in range(3):
                            ps = self.psum.next()
                            for kc in range(8):
                                p.mm(ps[:, :n], ws[i][:, kc, j * 128:(j + 1) * 128], br[i][:, kc, :n], kc == 0, kc == 7,
                                     [ws[i], br[i]], [ps])
                            t = tp.next()
                            p.tt("dve", t[:, :n], ps[:, :n], gt[:, i, :n], ALU.mult, [ps, gt], [t])
                            ts_.append(t)
                        p.tt("dve", ts_[0][:, :n], ts_[0][:, :n], ts_[1][:, :n], ALU.add, [ts_[0], ts_[1]], [ts_[0]])
                        p.tt("dve", y[:, c, :n], ts_[0][:, :n], ts_[2][:, :n], ALU.add, [ts_[0], ts_[2]], [y])
                for og in range(4):
                    wo = wop.next()
                    p.dma("sp", wo[:], self.Wo[l, og], writes=[wo])
                    for j in range(4):
                        c = og * 4 + j
                        ps = self.psum.next()
                        for kc in range(KC):
                            p.mm(ps[:, :n], wo[:, kc, j * 128:(j + 1) * 128], y[:, kc, :n], kc == 0, kc == KC - 1,
                                 [wo, y], [ps])
                        p.copy("act", mix[:, c, :n], ps[:, :n], [ps], [mix])
                        q = sq.next()
                        p.act(q[:, :n], mix[:, c, :n], AF.Square, [mix], [q])
                        p.mm(pss[:, :n], self.ones_bf[:], q[:, :n], c == 0, c == KC - 1, [self.ones_bf, q], [pss], signal=True)
                self.post_residual(mix, pss, rs, n, t0, m, 2, xp, tp)
            self.psum = full_psum
            p.barrier()

    def phase_ffn_up(self, l, tiles):
        p = self.p
        xTv = self.xT.rearrange("(kc p) t -> p kc t", p=128)
        with ExitStack() as st:
            xpool = p.sbpool(st, 2, [128, KC, 512], F32, "xu")
            sq = p.sbpool(st, 2, [128, 512], BF16, "usq")
            rs = p.sb(st, [128, 512], F32, "urs")
            tmp = p.sbpool(st, 2, [128, 512], F32, "utmp")
            hpool = p.sbpool(st, 2, [128, KC, 512], BF16, "uh")
            wpool = p.sbpool(st, 3, [128, KC, 512], BF16, "uw")
            stg = p.sbpool(st, 4, [128, 512], BF16, "ustg")
            cnt = 0
            def prep(tile):
                t0_, n_, m_ = tile
                xt_ = xpool.next()
                p.dma("sp", xt_[:, :, :n_], xTv[:, :, t0_:t0_ + n_], writes=[xt_])
                hT_ = hpool.next()
                self.norm_mod(xt_, hT_, n_, 3, 4, m_, sq, rs, tmp)
                return hT_
            hT_next = prep(tiles[0])
            for ti, (t0, n, m) in enumerate(tiles):
                hT = hT_next
                for g in range(22):
                    if g == 11 and ti + 1 < len(tiles):
                        hT_next = prep(tiles[ti + 1])
                    w = wpool.next()
                    p.dma("sp", w[:], self.Wu[l, g], writes=[w])
                    for j in range(4):
                        c = g * 4 + j
                        ps = self.psum.next()
                        for kc in range(KC):
                            p.mm(ps[:, :n], w[:, kc, j * 128:(j + 1) * 128], hT[:, kc, :n], kc == 0, kc == KC - 1,
                                 [w, hT], [ps])
                        o = stg.next()
                        eng = "act" if cnt % 2 == 0 else "dve"
                        cnt += 1
                        p.copy(eng, o[:, :n], ps[:, :n], [ps], [o])
                        p.dma("pool", self.U[c * 128:(c + 1) * 128, t0:t0 + n], o[:, :n], reads=[o])
            p.barrier()

    def phase_ffn_down(self, l, tiles):
        p = self.p
        S, T = self.S, self.T
        with ExitStack() as st:
            fw = p.sb(st, [128, 88, 3], F32, "fw")
            fb = p.sb(st, [128, 88], F32, "fb")
            p.dma("sp", fw[:], self.fcw[l], writes=[fw])
            p.dma("sp", fb[:], self.fcb[l], writes=[fb])
            up = p.sbpool(st, 8, [128, 516], BF16, "fu")
            ca = p.sbpool(st, 8, [128, 512], F32, "fca")
            gTp = p.sbpool(st, 2, [128, 44, 512], BF16, "gT")
            wdp = p.sbpool(st, 2, [128, 44, 128], BF16, "wd")
            mix = p.sb(st, [128, KC, 512], F32, "fmix")
            sq = p.sbpool(st, 2, [128, 512], BF16, "fsq")
            rs = p.sb(st, [128, 512], F32, "frs")
            xp = p.sbpool(st, 8, [128, 512], F32, "fx")
            tp = p.sbpool(st, 3, [128, 512], F32, "ft")
            full_psum = self.psum
            pss = full_psum.bufs[7]
            self.psum = Pool(full_psum.bufs[0:7])
            def conv_pair(tile, i, gT):
                t0, n, m = tile
                seq_lo, seq_hi = (S, T) if m else (0, S)
                lo = max(t0 - 1, seq_lo)
                hi = min(t0 + n + 1, seq_hi)
                off = lo - (t0 - 1)
                res = []
                for half in range(2):
                    ch = half * 44 + i
                    u = up.next()
                    if off > 0:
                        p.op("dve", lambda hh: hh.memset(u[:, 0:off], 0.0), (), [u], small=True)
                    if off + hi - lo < n + 2:
                        p.op("dve", lambda hh: hh.memset(u[:, off + hi - lo:n + 2], 0.0), (), [u], small=True)
                    p.dma("sp", u[:, off:off + hi - lo], self.U[ch * 128:(ch + 1) * 128, lo:hi], writes=[u])
                    a = ca.next()
                    p.act(a[:, :n], u[:, 1:n + 1], AF.Identity, [u, fw, fb], [a], scale=fw[:, ch, 1:2], bias=fb[:, ch:ch + 1])
                    p.stt("dve", a[:, :n], u[:, 0:n], fw[:, ch, 0:1], a[:, :n], ALU.mult, ALU.add, [u, fw, a], [a])
                    p.stt("dve", a[:, :n], u[:, 2:n + 2], fw[:, ch, 2:3], a[:, :n], ALU.mult, ALU.add, [u, fw, a], [a],
                          strict=False)
                    res.append(a)
                s_ = ca.next()
                p.act(s_[:, :n], res[0][:, :n], AF.Silu, [res[0]], [s_])
                p.tt("dve", gT[:, i, :n], s_[:, :n], res[1][:, :n], ALU.mult, [s_, res[1]], [gT])

            def down_chunk(tile, c, gT):
                t0, n, m = tile
                wd = wdp.next()
                p.dma("sp", wd[:], self.Wd[l, c], writes=[wd])
                ps = self.psum.next()
                for kc in range(44):
                    p.mm(ps[:, :n], wd[:, kc, :], gT[:, kc, :n], kc == 0, kc == 43, [wd, gT], [ps])
                p.copy("act", mix[:, c, :n], ps[:, :n], [ps], [mix])
                q = sq.next()
                p.act(q[:, :n], mix[:, c, :n], AF.Square, [mix], [q])
                p.mm(pss[:, :n], self.ones_bf[:], q[:, :n], c == 0, c == KC - 1, [self.ones_bf, q], [pss], signal=True)

            gT_cur = gTp.next()
            for i in range(44):
                conv_pair(tiles[0], i, gT_cur)
            for ti, tile in enumerate(tiles):
                nxt = tiles[ti + 1] if ti + 1 < len(tiles) else None
                gT_nxt = gTp.next() if nxt is not None else None
                for c in range(16):
                    if nxt is not None:
                        for i in range(c * 44 // 16, (c + 1) * 44 // 16):
                            conv_pair(nxt, i, gT_nxt)
                    down_chunk(tile, c, gT_cur)
                self.post_residual(mix, pss, rs, tile[1], tile[0], tile[2], 5, xp, tp)
                gT_cur = gT_nxt
            self.psum = full_psum
            p.barrier()


def _col(v, nch):
    v = np.asarray(v, np.float32)
    return np.ascontiguousarray(np.swapaxes(v.reshape(v.shape[:-1] + (nch, 128)), -1, -2))


def _na_tables(na_rpb, nrows):
    NL = na_rpb.shape[0]
    nblk = nrows // 2
    wr, wc = 8, 16
    col = np.arange(GRID_W)
    c0 = np.clip(col - wc // 2, 0, GRID_W - wc)
    rep = {0: min(2, nblk - 3), 1: 0, 2: 1, 3: nblk - 2, 4: nblk - 1}
    tab = np.full((NL, NH, 5, 128, 640), NEG, np.float32)
    for v, j in rep.items():
        kb0 = kb0_of(j, nblk)
        for qi in range(128):
            r = 2 * j + qi // 64
            c = qi % 64
            r0 = min(max(r - wr // 2, 0), nrows - wr)
            for i in range(5):
                for kr in range(2):
                    rr = (kb0 + i) * 2 + kr
                    if rr < r0 or rr >= r0 + wr:
                        continue
                    cc = np.arange(c0[c], c0[c] + wc)
                    tab[:, :, v, kr * 64 + cc, i * 128 + qi] = na_rpb[:, :, rr - r + 7, :][:, :, cc - c + 15]
    return tab


def _rope_tables(S):
    t = np.arange(S)
    rows, cols = t // GRID_W, t % GRID_W
    inv = (10000.0 ** (-np.arange(0, 32, 2, dtype=np.float32) / 32)).astype(np.float32)
    C = np.zeros((128, S), np.float32)
    Sn = np.zeros((128, S), np.float32)
    perm = np.zeros((128, 128), np.float32)
    for pp in range(128):
        sub = pp % 64
        part = sub // 32
        i = sub % 16
        pos = (rows if part == 0 else cols).astype(np.float32)
        ang = pos * inv[i]
        C[pp] = np.cos(ang)
        Sn[pp] = np.sin(ang)
        first = (sub % 32) < 16
        partner = pp + 16 if first else pp - 16
        perm[partner, pp] = -1.0 if first else 1.0
    return C, Sn, perm


def prep_shared(inp, S):
    NL = inp["w_mod"].shape[0]
    f = lambda k: np.ascontiguousarray(np.asarray(inp[k], np.float32))
    C, Sn, perm = _rope_tables(S)
    sh = {
        "w_mod": f("w_mod"), "w_in": f("w_in"), "p_a": f("p_a"), "p_b": f("p_b"), "p_c": f("p_c"),
        "w_out": f("w_out"), "w_up": f("w_up"), "w_down": f("w_down"),
        "b_mod_c": _col(inp["b_mod"], 96),
        "gvec": np.ascontiguousarray(np.stack([_col(inp[k], KC) for k in
                                               ("g_pre_mix", "g_post_mix", "g_pre_ffn", "g_post_ffn")], axis=2)),
        "na_tab": _na_tables(np.asarray(inp["na_rpb"], np.float32), S // GRID_W),
        "conv_w_c": np.ascontiguousarray(np.asarray(inp["conv_w"], np.float32).reshape(NL, 31, 8, 128).transpose(0, 3, 2, 1)),
        "conv_v_c": np.ascontiguousarray(np.stack([_col(inp[k], 8) for k in ("conv_b", "conv_ln_g", "conv_ln_b")], axis=2)),
        "lamv": np.ascontiguousarray(np.broadcast_to(
            np.stack([np.asarray(inp[k], np.float32) for k in ("lam_q1", "lam_k1", "lam_q2", "lam_k2")], axis=1)[:, None],
            (NL, 128, 4, 64))),
        "dlg_c": np.ascontiguousarray(np.asarray(inp["diff_ln_g"], np.float32).reshape(NL, 128, 1)),
        "fcw_c": np.ascontiguousarray(np.asarray(inp["ffn_conv_w"], np.float32).reshape(NL, 3, 88, 128).transpose(0, 3, 2, 1)),
        "fcb_c": _col(inp["ffn_conv_b"], 88),
        "ropeC": C, "ropeS": Sn, "perm": perm, "ident": np.eye(128, dtype=np.float32),
    }
    return sh


def prep_core(inp, b):
    cv = np.stack([_col(np.asarray(inp["c"], np.float32)[b], KC), _col(np.asarray(inp["c_ctx"], np.float32), KC)], axis=2)
    return {"x": np.ascontiguousarray(np.asarray(inp["x"], np.float32)[b]),
            "ctx": np.ascontiguousarray(np.asarray(inp["ctx"], np.float32)[b]),
            "cvec": np.ascontiguousarray(cv)}


def kernel(**inputs):
    B, S, _ = inputs["x"].shape
    kern = Kern(S)
    nc = kern.build()
    sh = prep_shared(inputs, S)
    active = [0, 2, 4, 6][:B]
    real = {c: dict(sh, **prep_core(inputs, b)) for b, c in enumerate(active)}
    zero = {k: np.zeros_like(v) for k, v in real[active[0]].items()}
    in_maps = [real.get(c, zero) for c in range(8)]
    res = run_bass_kernel_spmd(nc, in_maps, core_ids=list(range(8)))
    return np.stack([np.asarray(res.results[c]["out"], np.float32) for c in active], axis=0)
```
